# Optimizing a Trainium2 kernel written in Bass

```python
import jax, jax.numpy as jnp
from jax import lax
import numpy as np

D_MODEL = 1024
BATCH = 8
SEQ = 2048
DEPTH = 1

N_MEM = 256
SB_HEADS = 8
SB_HEAD_DIM = 64
DSA_HEADS = 8
DSA_KV_HEADS = 2
DSA_HEAD_DIM = 64
IDX_HEADS = 8
IDX_HEAD_DIM = 64
TOPK_MAX = 256
MEM_HEADS = 4
MEM_HEAD_DIM = 128
D_FF = 4 * D_MODEL
Q_BLOCK = 128
ROPE_THETA = 10000.0
EPS = 1e-6

IN_WIDTHS = (
    SB_HEADS * SB_HEAD_DIM,
    SB_HEADS * SB_HEAD_DIM,
    SB_HEADS * SB_HEAD_DIM,
    DSA_HEADS * DSA_HEAD_DIM,
    DSA_KV_HEADS * DSA_HEAD_DIM,
    DSA_KV_HEADS * DSA_HEAD_DIM,
    IDX_HEADS * IDX_HEAD_DIM,
    IDX_HEAD_DIM,
    IDX_HEADS,
)
D_IN = int(sum(IN_WIDTHS))
IN_OFFSETS = [int(o) for o in np.cumsum(IN_WIDTHS)[:-1]]

kernel_name = "hybrid_stickbreak_dsa_gated_block"


def _rmsnorm(x, g):
    x32 = x.astype(jnp.float32)
    y = x32 * lax.rsqrt(jnp.mean(x32 * x32, axis=-1, keepdims=True) + EPS)
    return (y * g.astype(jnp.float32)).astype(x.dtype)


def _rope(x, pos):
    d = x.shape[-1]
    inv = ROPE_THETA ** (-jnp.arange(0, d, 2, dtype=jnp.float32) / d)
    ang = pos.astype(jnp.float32)[:, None] * inv[None, :]
    cos = jnp.cos(ang)[None, :, None, :]
    sin = jnp.sin(ang)[None, :, None, :]
    x32 = x.astype(jnp.float32)
    x1, x2 = x32[..., : d // 2], x32[..., d // 2:]
    out = jnp.concatenate([x1 * cos - x2 * sin, x2 * cos + x1 * sin], axis=-1)
    return out.astype(x.dtype)


def _to_blocks(a):
    b, s = a.shape[:2]
    return jnp.moveaxis(a.reshape(b, s // Q_BLOCK, Q_BLOCK, *a.shape[2:]), 1, 0)


def _from_blocks(a):
    nb, b, qb = a.shape[:3]
    return jnp.moveaxis(a, 0, 1).reshape(b, nb * qb, *a.shape[3:])


def _stick_breaking_attention(q, k, v):
    b, s, h, dh = q.shape
    scale = dh ** -0.5
    key_pos = jnp.arange(s)
    t_blocks = jnp.arange(s).reshape(s // Q_BLOCK, Q_BLOCK)

    def block(args):
        qb, t_idx = args
        z = jnp.einsum('bqhd,bkhd->bhqk', qb, k).astype(jnp.float32) * scale
        strict = (key_pos[None, :] < t_idx[:, None])[None, None]
        log_beta = jax.nn.log_sigmoid(z)
        log_1mb = jnp.where(strict, jax.nn.log_sigmoid(-z), 0.0)
        suffix = lax.cumsum(log_1mb, axis=3, reverse=True) - log_1mb
        a = jnp.where(strict, jnp.exp(log_beta + suffix), 0.0)
        return jnp.einsum('bhqk,bkhd->bqhd', a.astype(v.dtype), v)

    out = lax.map(block, (_to_blocks(q), t_blocks))
    return _from_blocks(out).reshape(b, s, h * dh)


def _dsa_attention(q, k, v, iq, ik, iw):
    b, s, h, dh = q.shape
    g = k.shape[2]
    r = h // g
    k_top = min(TOPK_MAX, s // 4)
    scale = dh ** -0.5
    idx_scale = iq.shape[-1] ** -0.5
    w_scale = iw.shape[-1] ** -0.5
    key_pos = jnp.arange(s)
    t_blocks = jnp.arange(s).reshape(s // Q_BLOCK, Q_BLOCK)

    def block(args):
        qb, iqb, iwb, t_idx = args
        dots = jnp.einsum('bqhd,bkd->bqhk', iqb, ik).astype(jnp.float32) * idx_scale
        score = jnp.einsum('bqhk,bqh->bqk', jax.nn.relu(dots),
                           iwb.astype(jnp.float32) * w_scale)
        causal = (key_pos[None, :] <= t_idx[:, None])[None]
        score = jnp.where(causal, score, -jnp.inf)
        vals, sel = lax.top_k(score, k_top)
        valid = jnp.isfinite(vals)
        kg = jax.vmap(lambda kk, ii: kk[ii])(k, sel)
        vg = jax.vmap(lambda vv, ii: vv[ii])(v, sel)
        qg = qb.reshape(b, Q_BLOCK, g, r, dh)
        logits = jnp.einsum('bqgrd,bqkgd->bqgrk', qg, kg).astype(jnp.float32) * scale
        logits = jnp.where(valid[:, :, None, None, :], logits, -jnp.inf)
        p = jax.nn.softmax(logits, axis=-1)
        o = jnp.einsum('bqgrk,bqkgd->bqgrd', p.astype(vg.dtype), vg)
        return o.reshape(b, Q_BLOCK, h * dh)

    out = lax.map(block, (_to_blocks(q), _to_blocks(iq), _to_blocks(iw), t_blocks))
    return _from_blocks(out)


def _token_mixer(u, w_in, w_branch_sb, w_branch_dsa, w_gate, b_gate, w_out):
    b, s, _ = u.shape
    pos = jnp.arange(s)
    proj = u @ w_in
    sb_q, sb_k, sb_v, dq, dk, dv, iq, ik, iw = jnp.split(proj, IN_OFFSETS, axis=-1)
    sb_q = sb_q.reshape(b, s, SB_HEADS, SB_HEAD_DIM)
    sb_k = sb_k.reshape(b, s, SB_HEADS, SB_HEAD_DIM)
    sb_v = sb_v.reshape(b, s, SB_HEADS, SB_HEAD_DIM)
    dq = _rope(dq.reshape(b, s, DSA_HEADS, DSA_HEAD_DIM), pos)
    dk = _rope(dk.reshape(b, s, DSA_KV_HEADS, DSA_HEAD_DIM), pos)
    dv = dv.reshape(b, s, DSA_KV_HEADS, DSA_HEAD_DIM)
    iq = _rope(iq.reshape(b, s, IDX_HEADS, IDX_HEAD_DIM), pos)
    ik = _rope(ik[:, :, None, :], pos)[:, :, 0, :]

    o_sb = _stick_breaking_attention(sb_q, sb_k, sb_v)
    o_dsa = _dsa_attention(dq, dk, dv, iq, ik, iw)

    gates = jax.nn.sigmoid((u @ w_gate + b_gate).astype(jnp.float32)).astype(u.dtype)
    g_sb, g_dsa = jnp.split(gates, 2, axis=-1)
    merged = g_sb * (o_sb @ w_branch_sb) + g_dsa * (o_dsa @ w_branch_dsa)
    return merged @ w_out


def _memory_cross_attention(u, mem_n, w_cq, w_ckv, w_co):
    b, s, _ = u.shape
    m = mem_n.shape[1]
    q = (u @ w_cq).reshape(b, s, MEM_HEADS, MEM_HEAD_DIM)
    k, v = jnp.split(mem_n @ w_ckv, 2, axis=-1)
    k = k.reshape(b, m, MEM_HEADS, MEM_HEAD_DIM)
    v = v.reshape(b, m, MEM_HEADS, MEM_HEAD_DIM)
    logits = jnp.einsum('bqhd,bmhd->bhqm', q, k).astype(jnp.float32) * MEM_HEAD_DIM ** -0.5
    p = jax.nn.softmax(logits, axis=-1)
    o = jnp.einsum('bhqm,bmhd->bqhd', p.astype(v.dtype), v)
    return o.reshape(b, s, MEM_HEADS * MEM_HEAD_DIM) @ w_co


def _squared_relu_mlp(u, w_up, w_down):
    hdn = jax.nn.relu(u @ w_up)
    return (hdn * hdn) @ w_down


def setup_inputs(seed: int = 0) -> dict:
    key = jax.random.key(seed)
    ks = jax.random.split(key, 20)

    def dense(k, fan_in, fan_out):
        return jax.random.normal(k, (DEPTH, fan_in, fan_out), jnp.float32) * fan_in ** -0.5

    def gain(k, n, depth=True):
        shape = (DEPTH, n) if depth else (n,)
        return 1.0 + 0.02 * jax.random.normal(k, shape, jnp.float32)

    return {
        "x": jax.random.normal(ks[0], (BATCH, SEQ, D_MODEL), jnp.float32),
        "mem": jax.random.normal(ks[1], (BATCH, N_MEM, D_MODEL), jnp.float32),
        "norm_mix": gain(ks[2], D_MODEL),
        "w_in": dense(ks[3], D_MODEL, D_IN),
        "w_branch_sb": dense(ks[4], SB_HEADS * SB_HEAD_DIM, D_MODEL),
        "w_branch_dsa": dense(ks[5], DSA_HEADS * DSA_HEAD_DIM, D_MODEL),
        "w_gate": dense(ks[6], D_MODEL, 2 * D_MODEL),
        "b_gate": 0.01 * jax.random.normal(ks[7], (DEPTH, 2 * D_MODEL), jnp.float32),
        "w_out": dense(ks[8], D_MODEL, D_MODEL),
        "norm_cross": gain(ks[9], D_MODEL),
        "norm_mem": gain(ks[10], D_MODEL),
        "w_cq": dense(ks[11], D_MODEL, MEM_HEADS * MEM_HEAD_DIM),
        "w_ckv": dense(ks[12], D_MODEL, 2 * MEM_HEADS * MEM_HEAD_DIM),
        "w_co": dense(ks[13], MEM_HEADS * MEM_HEAD_DIM, D_MODEL),
        "norm_mlp": gain(ks[14], D_MODEL),
        "w_up": dense(ks[15], D_MODEL, D_FF),
        "w_down": dense(ks[16], D_FF, D_MODEL),
        "norm_final": gain(ks[17], D_MODEL, depth=False),
    }


def reference(x, mem, norm_mix, w_in, w_branch_sb, w_branch_dsa, w_gate, b_gate, w_out,
              norm_cross, norm_mem, w_cq, w_ckv, w_co, norm_mlp, w_up, w_down, norm_final):
    h = x
    for l in range(DEPTH):
        u = _rmsnorm(h, norm_mix[l])
        h = h + _token_mixer(u, w_in[l], w_branch_sb[l], w_branch_dsa[l],
                             w_gate[l], b_gate[l], w_out[l])
        u = _rmsnorm(h, norm_cross[l])
        mem_n = _rmsnorm(mem, norm_mem[l])
        h = h + _memory_cross_attention(u, mem_n, w_cq[l], w_ckv[l], w_co[l])
        u = _rmsnorm(h, norm_mlp[l])
        h = h + _squared_relu_mlp(u, w_up[l], w_down[l])
    return _rmsnorm(h, norm_final)
```

```python
import numpy as np
import ml_dtypes
import concourse.bass as bass
import concourse.mybir as mybir
from concourse.bass_utils import run_bass_kernel_spmd

F32 = mybir.dt.float32
BF16 = mybir.dt.bfloat16
AF = mybir.ActivationFunctionType
ALU = mybir.AluOpType

S = 2048
D = 1024
NT = S // 128
KC = D // 128
NMEM = 256
DFF = 4096
D_IN = 2888
EPS = 1e-6
NEG = -1.0e30
NBIS = 14
AX = mybir.AxisListType
import os as _os
TK_ACT = 0


class H:
    __slots__ = ("writers", "readers", "name")

    def __init__(self, name=""):
        self.writers = []
        self.readers = []
        self.name = name


class Op:
    __slots__ = ("eng", "fn", "waits", "count", "needed", "is_dma", "dslot", "dval")


class Sched:
    ENG = ["pe", "act", "dve", "pool", "sp"]
    R = 8

    def __init__(self, nc):
        self.nc = nc
        self.ops = {e: [] for e in self.ENG}
        self.ndma = {e: 0 for e in self.ENG}
        self.dma_ops = {e: [] for e in self.ENG}
        self.cnt = {e: 0 for e in self.ENG}
        self.esem = {e: nc.alloc_semaphore("es_" + e) for e in self.ENG}
        self.dsem = {
            e: [nc.alloc_semaphore("ds_%s_%d" % (e, i)) for i in range(self.R)]
            for e in ("sp", "pool", "act")
        }
        self.waited = {e: {} for e in self.ENG}
        self.nblk = 0

    def op(self, eng, fn, reads=(), writes=(), wadd=(), dma=False):
        o = Op()
        o.eng = eng
        o.fn = fn
        o.is_dma = dma
        o.needed = False
        o.count = 0
        deps = []
        for h in reads:
            deps += h.writers
        for h in writes:
            deps += h.writers
            deps += h.readers
        for h in wadd:
            deps += h.readers
        seen = set()
        w = []
        for d in deps:
            if id(d) in seen:
                continue
            seen.add(id(d))
            if d.eng == eng and eng == "pe" and (not d.is_dma) and (not dma):
                continue
            w.append(d)
            d.needed = True
        if dma:
            q = self.ndma[eng]
            self.ndma[eng] += 1
            o.dslot = q % self.R
            o.dval = 16 * (q // self.R + 1)
            o.needed = True
            if q >= self.R:
                w.append(self.dma_ops[eng][q - self.R])
            self.dma_ops[eng].append(o)
        o.waits = w
        self.ops[eng].append(o)
        for h in reads:
            h.readers.append(o)
        for h in writes:
            h.writers = [o]
            h.readers = []
        for h in wadd:
            h.writers.append(o)
        return o

    def flush(self):
        nc = self.nc
        for e in self.ENG:
            c = self.cnt[e]
            for o in self.ops[e]:
                if o.needed and not o.is_dma:
                    c += 1
                    o.count = c
            self.cnt[e] = c
            assert c < 60000, (e, c)
        names = {"pe": "tensor", "act": "scalar", "dve": "vector", "pool": "gpsimd", "sp": "sync"}
        pending = {e: self.ops[e] for e in self.ENG}
        self.ops = {e: [] for e in self.ENG}
        with nc.Block() as blk:
            for e in self.ENG:
                if not pending[e]:
                    continue

                def body(eng, e=e):
                    waited = self.waited[e]
                    for o in pending[e]:
                        for d in o.waits:
                            if d.is_dma:
                                sem = self.dsem[d.eng][d.dslot]
                                key = (d.eng, d.dslot)
                                val = d.dval
                            else:
                                sem = self.esem[d.eng]
                                key = d.eng
                                val = d.count
                            if waited.get(key, 0) >= val:
                                continue
                            eng.wait_ge(sem, val)
                            waited[key] = val
                        ins = o.fn(eng)
                        if o.is_dma:
                            ins.then_inc(self.dsem[e][o.dslot], 16)
                        elif o.needed:
                            ins.then_inc(self.esem[e], 1)

                getattr(blk, names[e])(body)
        self.nblk += 1


class Prog:
    pass


def _consts():
    inv = 10000.0 ** (-np.arange(0, 64, 2, dtype=np.float64) / 64.0)
    pos = np.arange(S, dtype=np.float64)
    ang = inv[:, None] * pos[None, :]
    cos = np.cos(ang)
    sin = np.sin(ang)
    cosT = np.zeros((128, S), np.float32)
    sinT = np.zeros((128, S), np.float32)
    for p in range(128):
        d = p % 64
        f = d % 32
        cosT[p] = cos[f]
        sinT[p] = -sin[f] if d < 32 else sin[f]
    ident = np.eye(128, dtype=np.float32).astype(ml_dtypes.bfloat16)
    t = np.arange(128)[:, None]
    s = np.arange(128)[None, :]
    m_strict = (s < t).astype(np.float32)
    m_strict_inv = (s >= t).astype(np.float32)
    m_caus_add = np.where(s <= t, 0.0, NEG).astype(np.float32)
    m_neg = np.where(s >= t, -2048.0, 0.0).astype(np.float32).astype(ml_dtypes.bfloat16)
    nident = (np.eye(128, dtype=np.float32) * -30000.0).astype(ml_dtypes.bfloat16)
    pw = np.tile((2.0 ** -(np.arange(NBIS + 2, dtype=np.float64) + 1)).astype(np.float32)[None, :], (128, 1))
    iota8 = np.tile(np.arange(8, dtype=np.float32)[None, :], (128, 1))
    pm = np.zeros((128, 128), np.float32)
    for pp in range(128):
        pm[(pp % 64 + 32) % 64 + 64 * (pp // 64), pp] = 1.0
    pm = pm.astype(ml_dtypes.bfloat16)
    return dict(cosT=cosT, sinT=sinT, ident=ident, nident=nident, pw=pw, iota8=iota8, pm=pm,
                m_strict=m_strict.astype(ml_dtypes.bfloat16),
                m_strict_inv=m_strict_inv, m_caus_add=m_caus_add, m_neg=m_neg)


W_SPECS = [
    ("w_in", D, D_IN), ("w_branch_sb", 512, D), ("w_branch_dsa", 512, D),
    ("w_gate", D, 2 * D), ("w_out", D, D), ("w_cq", D, 512), ("w_ckv", D, D),
    ("w_co", 512, D), ("w_up", D, DFF), ("w_down", DFF, D),
]
V_SPECS = ["norm_mix", "norm_cross", "norm_mem", "norm_mlp", "norm_final"]


def build(stage="full"):
    nc = bass.Bass("TRN2", target_bir_lowering=False)
    P = Prog()
    P.nc = nc
    P.stage = stage
    sc = Sched(nc)
    P.sc = sc
    P.dbg = {}

    P.x = nc.dram_tensor("x", [S, D], F32, kind="ExternalInput").ap()
    P.mem = nc.dram_tensor("mem", [NMEM, D], F32, kind="ExternalInput").ap()
    P.w32 = {}
    P.wbf = {}
    P.wh = {}
    for name, r, c in W_SPECS:
        P.w32[name] = nc.dram_tensor(name, [r, c], F32, kind="ExternalInput").ap()
        P.wbf[name] = nc.dram_tensor(name + "_bf", [r, c], BF16, kind="Internal").ap()
    P.vec = {}
    for name in V_SPECS:
        P.vec[name] = nc.dram_tensor(name, [1, D], F32, kind="ExternalInput").ap()
    P.b_gate = nc.dram_tensor("b_gate", [128, 16], F32, kind="ExternalInput").ap()
    cs = _consts()
    P.cdram = {}
    for k, v in cs.items():
        dt = BF16 if v.dtype == ml_dtypes.bfloat16 else F32
        P.cdram[k] = nc.dram_tensor("c_" + k, list(v.shape), dt, kind="ExternalInput").ap()
    P.out = nc.dram_tensor("out", [S, D], F32, kind="ExternalOutput").ap()

    P.bank = [nc.alloc_psum_tensor("bank%d" % i, [128, 512], F32) for i in range(8)]
    P.bankh = [H("bank%d" % i) for i in range(8)]

    from contextlib import ExitStack
    with ExitStack() as g:
        def sb(name, shape, dt):
            return g.enter_context(nc.sbuf_tensor(name, shape, dt))
        P.sb_global = sb
        P.ident = sb("ident", [128, 128], BF16)
        P.h_const = H("const")
        sc.op("sp", lambda e: e.dma_start(out=P.ident[:], in_=P.cdram["ident"]),
              writes=[P.h_const], dma=True)
        P.nident = sb("nident", [128, 128], BF16)
        sc.op("sp", lambda e: e.dma_start(out=P.nident[:], in_=P.cdram["nident"]),
              wadd=[P.h_const], dma=True)
        phases(P, g)
        sc.flush()
    return nc, P


def cast_dma(P, fn, wh, first):
    sc = P.sc
    hist = P.__dict__.setdefault("cast_hist", [])
    hc = H()
    rd = [hist[-3]] if len(hist) >= 3 else []
    sc.op("pool", fn, reads=rd, writes=[hc] + ([wh] if first else []), wadd=() if first else [wh], dma=True)
    hist.append(hc)


def cast_weights(P, names, cols=None, hname=None):
    for name in names:
        w = P.w32[name]
        o = P.wbf[name]
        r, c = w.shape
        ca, cb = cols if cols is not None else (0, c)
        hn = hname or name
        P.wh[hn] = H("w_" + hn)
        ncs = (cb - ca + 2047) // 2048
        cw = (cb - ca + ncs - 1) // ncs
        first = True
        for r0 in range(0, r, 1024):
            r1 = min(r, r0 + 1024)
            for c0 in range(ca, cb, cw):
                c1 = min(cb, c0 + cw)
                cast_dma(P, lambda e, w=w, o=o, r0=r0, r1=r1, c0=c0, c1=c1: e.dma_start(
                    out=o[r0:r1, c0:c1], in_=w[r0:r1, c0:c1]), P.wh[hn], first)
                first = False


def load_w(P, eng, dst, name, c0, ncols, hdst, r0=0, nk=None, hname=None):
    w = P.wbf[name]
    rows = w.shape[0]
    if nk is None:
        nk = rows // 128
    src = w.rearrange("(kc p) n -> p kc n", p=128)[:, r0 // 128:r0 // 128 + nk, c0:c0 + ncols]
    return P.sc.op(eng, lambda e: e.dma_start(out=dst, in_=src),
                   reads=[P.wh[hname or name]], writes=[hdst], dma=True)


class NormTmp:
    def __init__(self, P, sb):
        self.gbc = sb("nt_gbc", [128, D], F32)
        self.xt = [sb("nt_xt%d" % i, [128, D], F32) for i in range(2)]
        self.junk = sb("nt_junk", [128, D], BF16)
        self.ub = [sb("nt_ub%d" % i, [128, D], BF16) for i in range(2)]
        self.st = sb("nt_st", [128, 4 * NT], F32)
        self.h_g = H()
        self.h_xt = [H(), H()]
        self.h_junk = H()
        self.h_ub = [H(), H()]
        self.h_st = [H() for _ in range(NT)]


def rmsnorm_T(P, g, src_kind, gname, uT, uTh, ntiles=NT, src=None, srch=None, tag="n"):
    nc, sc = P.nc, P.sc
    T = P.ntmp
    gbc, xt, junk, ub, st = T.gbc, T.xt, T.junk, T.ub, T.st
    h_g, h_xt, h_junk, h_ub, h_st = T.h_g, T.h_xt, T.h_junk, T.h_ub, T.h_st
    sc.op("sp", lambda e: e.dma_start(out=gbc[:], in_=P.vec[gname].to_broadcast([128, D])),
          writes=[h_g], dma=True)
    pbank = [6, 7]
    for i in range(ntiles):
        j = i % 2
        if src_kind == "dram":
            xin = xt[j][:]
            hx = h_xt[j]
            sc.op("sp", lambda e, i=i, j=j: e.dma_start(out=xt[j][:], in_=src[i * 128:(i + 1) * 128, :]),
                  writes=[hx], dma=True)
        else:
            xin = src[:, i, :]
            hx = srch[i]
        ss = st[:, 4 * i:4 * i + 1]
        ms = st[:, 4 * i + 1:4 * i + 2]
        sd = st[:, 4 * i + 2:4 * i + 3]
        rs = st[:, 4 * i + 3:4 * i + 4]
        sc.op("act", lambda e, xin=xin, ss=ss: e.activation(out=junk[:], in_=xin, func=AF.Square, accum_out=ss),
              reads=[hx], writes=[h_junk, h_st[i]])
        sc.op("dve", lambda e, ss=ss, ms=ms: e.tensor_scalar(out=ms, in0=ss, scalar1=1.0 / D, scalar2=EPS,
                                                               op0=ALU.mult, op1=ALU.add),
              reads=[h_st[i]], writes=[h_st[i]])
        sc.op("act", lambda e, sd=sd, ms=ms: e.activation(out=sd, in_=ms, func=AF.Sqrt),
              reads=[h_st[i]], writes=[h_st[i]])
        sc.op("dve", lambda e, sd=sd, rs=rs: e.reciprocal(out=rs, in_=sd),
              reads=[h_st[i]], writes=[h_st[i]])
        sc.op("dve", lambda e, xin=xin, rs=rs, j=j: e.scalar_tensor_tensor(
            out=ub[j][:], in0=xin, scalar=rs, in1=gbc[:], op0=ALU.mult, op1=ALU.mult),
            reads=[hx, h_st[i], h_g], writes=[h_ub[j]])
        pb = pbank[j]
        pv = P.bank[pb][:].bitcast(BF16)
        for c in range(KC):
            sc.op("pe", lambda e, c=c, j=j, pv=pv: e.transpose(
                out=pv[:, c * 128:(c + 1) * 128], in_=ub[j][:, c * 128:(c + 1) * 128], identity=P.ident[:]),
                reads=[h_ub[j], P.h_const], writes=[P.bankh[pb]] if c == 0 else (),
                wadd=() if c == 0 else [P.bankh[pb]])
        sc.op("act", lambda e, i=i, pv=pv: e.copy(
            out=uT[:, :, i * 128:(i + 1) * 128], in_=pv.rearrange("p (c t) -> p c t", c=KC)),
            reads=[P.bankh[pb]], writes=[uTh[i]])


class Banks:
    def __init__(self, P, ids):
        self.P = P
        self.ids = list(ids)
        self.i = 0

    def next(self):
        b = self.ids[self.i % len(self.ids)]
        self.i += 1
        return b


def evac_copy(P, k, out, in_, reads, writes, wadd=()):
    if k % 2 == 0:
        return P.sc.op("act", lambda e: e.copy(out=out, in_=in_), reads=reads, writes=writes, wadd=wadd)
    return P.sc.op("dve", lambda e: e.tensor_copy(out=out, in_=in_), reads=reads, writes=writes, wadd=wadd)


def proj_fm(P, wt, wth, ncols, uT, uTh, banks, evac, nk=KC, ntok=S):
    sc = P.sc
    tgw = min(512, ntok)
    for j in range(ncols // 128):
        for tg in range(ntok // tgw):
            b = banks.next()
            rd = [wth] + [uTh[i] for i in range(tg * tgw // 128, (tg + 1) * tgw // 128)]
            for kc in range(nk):
                sc.op("pe", lambda e, j=j, tg=tg, kc=kc, b=b: e.matmul(
                    P.bank[b][:, 0:tgw], lhsT=wt[:, kc, j * 128:(j + 1) * 128],
                    rhs=uT[:, kc, tg * tgw:(tg + 1) * tgw], start=(kc == 0), stop=(kc == nk - 1)),
                    reads=rd, writes=[P.bankh[b]] if kc == 0 else (), wadd=() if kc == 0 else [P.bankh[b]])
            evac(j, tg, P.bank[b][:, 0:tgw], P.bankh[b])


def proj_tm(P, wt, wth, ncols, uT, uTh, banks, evac, nk=KC, ntiles=NT):
    sc = P.sc
    for i in range(ntiles):
        for c0 in range(0, ncols, 512):
            cw = min(512, ncols - c0)
            b = banks.next()
            for kc in range(nk):
                sc.op("pe", lambda e, i=i, kc=kc, b=b, c0=c0, cw=cw: e.matmul(
                    P.bank[b][:, 0:cw], lhsT=uT[:, kc, i * 128:(i + 1) * 128],
                    rhs=wt[:, kc, c0:c0 + cw], start=(kc == 0), stop=(kc == nk - 1)),
                    reads=[wth, uTh[i]], writes=[P.bankh[b]] if kc == 0 else (),
                    wadd=() if kc == 0 else [P.bankh[b]])
            evac(i, c0, cw, P.bank[b][:, 0:cw], P.bankh[b])


def sb_attention(P, g, qT, kT, v, hq, hk, hv, osT, osTh):
    nc, sc = P.nc, P.sc
    sb = lambda n, s, d: g.enter_context(nc.sbuf_tensor("sb_" + n, s, d))
    scale = 0.125
    ones = sb("ones", [128, S], BF16)
    h_ones = H()
    sc.op("pool", lambda e: e.memset(ones[:], 1.0), writes=[h_ones])
    negm = sb("negm", [128, 128], BF16)
    h_m = H()
    sc.op("sp", lambda e: e.dma_start(out=negm[:], in_=P.cdram["m_neg"]), writes=[h_m], dma=True)
    NB = 2
    beta = [sb("beta%d" % i, [128, S], BF16) for i in range(NB)]
    omb = [sb("omb%d" % i, [128, S], F32) for i in range(NB)]
    Q = [sb("Q%d" % i, [128, S], BF16) for i in range(NB)]
    a = [sb("a%d" % i, [128, S], BF16) for i in range(NB)]
    aT = [sb("aT%d" % i, [128, NT, 128], BF16) for i in range(NB)]
    otm = [sb("otm%d" % i, [128, 512], BF16) for i in range(NB)]
    h_beta = [H() for _ in range(NB)]
    h_omb = [H() for _ in range(NB)]
    h_Q = [H() for _ in range(NB)]
    h_a = [H() for _ in range(NB)]
    h_aT = [H() for _ in range(NB)]
    h_otm = [H() for _ in range(NB)]
    for r in range(NB):
        sc.op("pool", lambda e, r=r: e.memset(Q[r][:], 1.0), writes=[h_Q[r]])
    its = [(tb, h) for tb in range(NT) for h in range(8)]

    def zbank(n, npc, p):
        return p + (2 * (n % 2) if npc <= 2 else 0)

    def stage1(n):
        tb, h = its[n]
        r = n % NB
        L = (tb + 1) * 128
        npc = (L + 511) // 512
        ch, p0 = h // 2, (h % 2) * 64
        for p in range(npc):
            n_ = min(512, L - p * 512)
            b = zbank(n, npc, p)
            last = (p == npc - 1)
            sc.op("pe", lambda e, p=p, n_=n_, b=b, last=last: e.matmul(
                P.bank[b][:, 0:n_], lhsT=qT[p0:p0 + 64, ch, tb * 128:(tb + 1) * 128],
                rhs=kT[p0:p0 + 64, ch, p * 512:p * 512 + n_], start=True, stop=not last),
                reads=[hq[tb]] + [hk[i] for i in range(p * 4, p * 4 + n_ // 128)], writes=[P.bankh[b]])
            if last:
                sc.op("pe", lambda e, n_=n_, b=b: e.matmul(
                    P.bank[b][:, n_ - 128:n_], lhsT=P.ident[:], rhs=negm[:], start=False, stop=True),
                    reads=[h_m, P.h_const], wadd=[P.bankh[b]])
        for p in range(npc):
            n_ = min(512, L - p * 512)
            b = zbank(n, npc, p)
            sc.op("act", lambda e, p=p, n_=n_, b=b: e.activation(
                out=omb[r][:, p * 512:p * 512 + n_], in_=P.bank[b][:, 0:n_], func=AF.Sigmoid, scale=scale),
                reads=[P.bankh[b]], writes=[h_omb[r]] if p == 0 else (), wadd=() if p == 0 else [h_omb[r]])
        sc.op("pool", lambda e: e.tensor_scalar(
            out=beta[r][:, 0:L], in0=omb[r][:, 0:L], scalar1=1.0, scalar2=0.0, op0=ALU.mult, op1=ALU.add),
            reads=[h_omb[r]], writes=[h_beta[r]])
        sc.op("pool", lambda e: e.tensor_scalar(
            out=omb[r][:, 0:L], in0=omb[r][:, 0:L], scalar1=-1.0, scalar2=1.0, op0=ALU.mult, op1=ALU.add),
            reads=[h_beta[r]], writes=[h_omb[r]])
    def stage1b(n):
        tb, h = its[n]
        r = n % NB
        L = (tb + 1) * 128
        sc.op("dve", lambda e: e.tensor_tensor_scan(
            out=Q[r][:, L - 2::-1], data0=omb[r][:, L - 1:0:-1], data1=ones[:, 0:L - 1],
            initial=1.0, op0=ALU.mult, op1=ALU.mult),
            reads=[h_omb[r], h_ones], writes=[h_Q[r]])
        sc.op("dve", lambda e: e.tensor_tensor(
            out=a[r][:, 0:L], in0=beta[r][:, 0:L], in1=Q[r][:, 0:L], op=ALU.mult),
            reads=[h_beta[r], h_Q[r]], writes=[h_a[r]])

    def stage2(n):
        tb, h = its[n]
        r = n % NB
        for kb0 in range(0, tb + 1, 8):
            nb = min(8, tb + 1 - kb0)
            pb = 4 + (kb0 // 8)
            pv = P.bank[pb][:].bitcast(BF16)
            for kb in range(kb0, kb0 + nb):
                sc.op("pe", lambda e, kb=kb, kb0=kb0, pv=pv: e.transpose(
                    out=pv[:, (kb - kb0) * 128:(kb - kb0 + 1) * 128], in_=a[r][:, kb * 128:(kb + 1) * 128],
                    identity=P.ident[:]),
                    reads=[h_a[r], P.h_const], writes=[P.bankh[pb]] if kb == kb0 else (),
                    wadd=() if kb == kb0 else [P.bankh[pb]])
            sc.op("act", lambda e, kb0=kb0, nb=nb, pv=pv: e.copy(
                out=aT[r][:, kb0:kb0 + nb, :], in_=pv[:, 0:nb * 128].rearrange("p (c t) -> p c t", c=nb)),
                reads=[P.bankh[pb]], writes=[h_aT[r]] if kb0 == 0 else (), wadd=() if kb0 == 0 else [h_aT[r]])
    def stage2b(n):
        tb, h = its[n]
        r = n % NB
        for kb in range(tb + 1):
            sc.op("pe", lambda e, kb=kb: e.matmul(
                P.bank[6][:, h * 64:(h + 1) * 64], lhsT=aT[r][:, kb, :], rhs=v[:, kb, h * 64:(h + 1) * 64],
                start=(kb == 0), stop=(kb == tb)),
                reads=[h_aT[r], hv[kb]], writes=[P.bankh[6]] if (kb == 0 and h == 0) else (),
                wadd=() if (kb == 0 and h == 0) else [P.bankh[6]])
        if h == 7:
            ro = tb % NB
            sc.op("dve", lambda e: e.tensor_copy(out=otm[ro][:], in_=P.bank[6][:, :]),
                  reads=[P.bankh[6]], writes=[h_otm[ro]])
            pv = P.bank[7][:].bitcast(BF16)
            for c in range(4):
                sc.op("pe", lambda e, c=c, pv=pv: e.transpose(
                    out=pv[:, c * 128:(c + 1) * 128], in_=otm[ro][:, c * 128:(c + 1) * 128], identity=P.ident[:]),
                    reads=[h_otm[ro], P.h_const], writes=[P.bankh[7]] if c == 0 else (),
                    wadd=() if c == 0 else [P.bankh[7]])
            sc.op("dve", lambda e, pv=pv: e.tensor_copy(
                out=osT[:, :, tb * 128:(tb + 1) * 128], in_=pv[:, 0:512].rearrange("p (c t) -> p c t", c=4)),
                reads=[P.bankh[7]], writes=[osTh[tb]])

    N = len(its)
    for n in range(N + 3):
        if n < N:
            stage1(n)
        if 1 <= n <= N:
            stage1b(n - 1)
        if 2 <= n <= N + 1:
            stage2(n - 2)
        if n >= 3:
            stage2b(n - 3)


DSA_COLS = 1408


def build_dsa_weights(P):
    nc, sc = P.nc, P.sc
    w = P.w32["w_in"]
    P.dsaA = nc.dram_tensor("dsaA_bf", [D, DSA_COLS], BF16, kind="Internal").ap()
    P.wh["dsaA"] = H()
    blocks = [(0, 1536, 8), (512, 2048, 1), (576, 2048, 1), (640, 2112, 1), (704, 2112, 1),
              (768, 2304, 8), (1280, 2816, 1), (1344, 2816, 1)]
    for d0, s0, nh in blocks:
        cast_dma(P, lambda e, d0=d0, s0=s0, nh=nh: e.dma_start(
            out=P.dsaA[:, d0:d0 + nh * 64], in_=w[:, s0:s0 + nh * 64]), P.wh["dsaA"], False)


def dsa_project(P, g, uT, uTh, dqT, kT2, iqT, ikT2, vp, iwS, hdq, hk2, hiq, hik2, hvp, hiw):
    nc, sc = P.nc, P.sc
    sb = lambda n, s, d: g.enter_context(nc.sbuf_tensor("dp_" + n, s, d))
    cosT = sb("cos", [128, S], F32)
    sinT = sb("sin", [128, S], F32)
    h_cs = H()
    sc.op("sp", lambda e: e.dma_start(out=cosT[:], in_=P.cdram["cosT"]), wadd=[h_cs], dma=True)
    sc.op("sp", lambda e: e.dma_start(out=sinT[:], in_=P.cdram["sinT"]), wadd=[h_cs], dma=True)
    wv = sb("wv", [128, KC, 136], BF16)
    h_wv = H()
    load_w(P, "sp", wv[:, :, 0:128], "w_in", 2176, 128, h_wv)
    wsrc = P.wbf["w_in"].rearrange("(kc p) n -> p kc n", p=128)[:, :, 2880:2888]
    sc.op("sp", lambda e: e.dma_start(out=wv[:, :, 128:136], in_=wsrc), reads=[P.wh["w_in"]], wadd=[h_wv], dma=True)
    h_one = H()
    sc.op("pool", lambda e: e.memset(vp[:], 1.0), writes=hvp)
    banks = Banks(P, [4, 5])
    wsc = 0.125 * (8.0 ** -0.5)

    def evv(i, c0, cw, bap, bh):
        sc.op("act", lambda e, i=i, bap=bap: e.copy(
            out=vp[:, i, :].rearrange("p (g c) -> p g c", c=128)[:, :, 0:64],
            in_=bap[:, 0:128].rearrange("p (g c) -> p g c", c=64)),
            reads=[bh], writes=[hvp[i]])
        sc.op("act", lambda e, i=i, bap=bap: e.mul(out=iwS[:, i, :], in_=bap[:, 128:136], mul=wsc),
              reads=[bh], writes=[hiw[i]])
    proj_tm(P, wv, h_wv, 136, uT, uTh, banks, evv)
    wA = sb("wA", [128, KC, 512], BF16)
    h_wA = H()
    pmt = sb("pm", [128, 128], BF16)
    h_pm = H()
    sc.op("sp", lambda e: e.dma_start(out=pmt[:], in_=P.cdram["pm"]), writes=[h_pm], dma=True)
    xb = [sb("xb%d" % i, [128, 512], BF16) for i in range(2)]
    h_xb = [H(), H()]
    t1 = [sb("t1_%d" % i, [128, 512], F32) for i in range(2)]
    t2 = [sb("t2_%d" % i, [128, 512], F32) for i in range(2)]
    h_t1 = [H(), H()]
    h_t2 = [H(), H()]
    dests = ([(dqT, c, hdq) for c in range(4)] + [(kT2, 0, hk2), (kT2, 1, hk2)] +
             [(iqT, c, hiq) for c in range(4)] + [(ikT2, 0, hik2)])
    it = [0]
    for grp, (c0, ncols) in enumerate([(0, 512), (512, 512), (1024, 384)]):
        sc.op("sp", lambda e, c0=c0, ncols=ncols: e.dma_start(
            out=wA[:, :, 0:ncols], in_=P.dsaA.rearrange("(kc p) n -> p kc n", p=128)[:, :, c0:c0 + ncols]),
            reads=[P.wh["dsaA"]], writes=[h_wA], dma=True)
        for jj in range(ncols // 128):
            dst, dc, hd = dests[c0 // 128 + jj]
            for tg in range(4):
                r = it[0] % 2
                it[0] += 1
                bA, bB = (0, 1) if r == 0 else (2, 3)
                rd = [uTh[i] for i in range(tg * 4, tg * 4 + 4)]
                for kc in range(KC):
                    sc.op("pe", lambda e, bA=bA, kc=kc, jj=jj, tg=tg: e.matmul(
                        P.bank[bA][:, :], lhsT=wA[:, kc, jj * 128:(jj + 1) * 128],
                        rhs=uT[:, kc, tg * 512:(tg + 1) * 512], start=(kc == 0), stop=(kc == KC - 1)),
                        reads=rd + [h_wA], writes=[P.bankh[bA]] if kc == 0 else (),
                        wadd=() if kc == 0 else [P.bankh[bA]])
                sc.op("act", lambda e, bA=bA, r=r: e.copy(out=xb[r][:], in_=P.bank[bA][:, :]),
                      reads=[P.bankh[bA]], writes=[h_xb[r]])
                sc.op("pe", lambda e, bB=bB, r=r: e.matmul(
                    P.bank[bB][:, :], lhsT=pmt[:], rhs=xb[r][:], start=True, stop=True),
                    reads=[h_xb[r], h_pm], writes=[P.bankh[bB]])
                sc.op("dve", lambda e, bA=bA, r=r, tg=tg: e.tensor_tensor(
                    out=t1[r][:], in0=P.bank[bA][:, :], in1=cosT[:, tg * 512:(tg + 1) * 512], op=ALU.mult),
                    reads=[P.bankh[bA], h_cs, h_xb[r]], writes=[h_t1[r]])
                sc.op("dve", lambda e, bB=bB, r=r, tg=tg: e.tensor_tensor(
                    out=t2[r][:], in0=P.bank[bB][:, :], in1=sinT[:, tg * 512:(tg + 1) * 512], op=ALU.mult),
                    reads=[P.bankh[bB], h_cs], writes=[h_t2[r]])
                sc.op("pool", lambda e, r=r, dst=dst, dc=dc, tg=tg: e.tensor_tensor(
                    out=dst[:, dc, tg * 512:(tg + 1) * 512], in0=t1[r][:], in1=t2[r][:], op=ALU.add),
                    reads=[h_t1[r], h_t2[r]], writes=[hd[i] for i in range(tg * 4, tg * 4 + 4)])


def dsa_attention(P, g, dqT, kT2, iqT, ikT2, vp, iwS, hdq, hk2, hiq, hik2, hvp, hiw, odT, odTh):
    nc, sc = P.nc, P.sc
    sb = lambda n, s, d: g.enter_context(nc.sbuf_tensor("da_" + n, s, d))
    scale = 0.125
    KTOP = 256
    U32 = mybir.dt.uint32
    mca = sb("mca", [128, 128], F32)
    h_m = H()
    sc.op("sp", lambda e: e.dma_start(out=mca[:], in_=P.cdram["m_caus_add"]), writes=[h_m], dma=True)
    thrc = sb("thrc", [128, 1], F32)
    h_thrc = H()
    sc.op("pool", lambda e: e.memset(thrc[:], -1.0e29), writes=[h_thrc])
    io8 = sb("io8", [128, 8], F32)
    h_io8 = H()
    sc.op("sp", lambda e: e.dma_start(out=io8[:], in_=P.cdram["iota8"]), writes=[h_io8], dma=True)
    pw = sb("pw", [128, NBIS + 2], F32)
    h_pw = H()
    sc.op("sp", lambda e: e.dma_start(out=pw[:], in_=P.cdram["pw"]), writes=[h_pw], dma=True)
    NA = 3
    acc = [sb("acc%d" % i, [128, S], F32) for i in range(NA)]
    h_acc = [H() for _ in range(NA)]
    rmin = [sb("rmin%d" % i, [128, 1], F32) for i in range(NA)]
    h_rmin = [H() for _ in range(NA)]
    junk = sb("junk", [128, S], F32)
    wsel = sb("wsel", [128, S], F32)
    h_junk, h_wsel = H(), H()
    NTH = 4
    th = [sb("th%d" % i, [128, 512], F32) for i in range(2)] + [P.ntmp.gbc[:, 0:512], P.ntmp.gbc[:, 512:1024]]
    h_th = [H() for _ in range(NTH)]
    mask = sb("mask", [128, S], BF16)
    h_mask = H()
    maskT = [sb("maskT%d" % i, [128, NT, 128], BF16) for i in range(2)]
    h_maskT = [H(), H()]
    NE = 3
    pbuf = [sb("p%d" % i, [128, 512], BF16) for i in range(NE)]
    h_p = [H() for _ in range(NE)]
    rec = [sb("rec%d" % i, [128, 512], F32) for i in range(2)]
    h_rec = [H(), H()]

    class Chain:
        pass
    chains = []
    for ci in range(2):
        c = Chain()
        c.j16 = P.ntmp.xt[ci][:].bitcast(BF16)
        c.dl = sb("dl%d" % ci, [128, NBIS + 2], F32)
        c.bs = sb("bs%d" % ci, [128, 8], F32)
        c.Mh = sb("Mh%d" % ci, [128, NBIS + 1], F32)
        c.Uh = sb("Uh%d" % ci, [128, NBIS], F32)
        c.cand = sb("cand%d" % ci, [128, NBIS], F32)
        c.candh = sb("candh%d" % ci, [128, NBIS], F32)
        c.Vh = sb("Vh%d" % ci, [128, NBIS], F32)
        c.fs = sb("fs%d" % ci, [128, 8], F32)
        c.m8 = sb("m8_%d" % ci, [128, 8], F32)
        c.m8b = sb("m8b%d" % ci, [128, 8], F32)
        c.t8 = sb("t8_%d" % ci, [128, 8], F32)
        for nm in ("j16", "dl", "rng", "Mh", "Uh", "cand", "candh", "Vh", "fs", "m8", "m8b", "t8", "cnt", "lo", "tmp"):
            setattr(c, "h_" + nm, H())
        chains.append(c)
    cnt = {"th": 0, "ep": 0, "lb": 0}

    def part_I(tb):
        L = (tb + 1) * 128
        npc = (L + 511) // 512
        ac, h_ac = acc[tb % NA], h_acc[tb % NA]
        thunks = []
        for h in range(8):
            def f(h=h):
                ch, p0 = h // 2, (h % 2) * 64
                for p in range(npc):
                    n = min(512, L - p * 512)
                    b = p % 2
                    r = cnt["th"] % NTH
                    cnt["th"] += 1
                    sc.op("pe", lambda e, p=p, n=n, b=b: e.matmul(
                        P.bank[b][:, 0:n], lhsT=iqT[p0:p0 + 64, ch, tb * 128:(tb + 1) * 128],
                        rhs=ikT2[p0:p0 + 64, 0, p * 512:p * 512 + n], start=True, stop=True),
                        reads=[hiq[tb]] + [hik2[i] for i in range(p * 4, p * 4 + n // 128)], writes=[P.bankh[b]])
                    if h == 0:
                        sc.op("act", lambda e, n=n, b=b, r=r: e.activation(
                            out=th[r][:, 0:n], in_=P.bank[b][:, 0:n], func=AF.Relu),
                            reads=[P.bankh[b]], writes=[h_th[r]])
                        sc.op("pool", lambda e, p=p, n=n, r=r: e.tensor_scalar(
                            out=ac[:, p * 512:p * 512 + n], in0=th[r][:, 0:n], scalar1=iwS[:, tb, h:h + 1],
                            scalar2=0.0, op0=ALU.mult, op1=ALU.add),
                            reads=[h_th[r], hiw[tb]], writes=[h_ac] if p == 0 else (), wadd=() if p == 0 else [h_ac])
                    else:
                        sc.op("act", lambda e, n=n, b=b, r=r: e.activation(
                            out=th[r][:, 0:n], in_=P.bank[b][:, 0:n], func=AF.Relu),
                            reads=[P.bankh[b]], writes=[h_th[r]])
                        sc.op("pool", lambda e, n=n, r=r: e.tensor_scalar(
                            out=th[r][:, 0:n], in0=th[r][:, 0:n], scalar1=iwS[:, tb, h:h + 1], scalar2=0.0,
                            op0=ALU.mult, op1=ALU.add),
                            reads=[hiw[tb]], writes=[h_th[r]])
                        sc.op("pool", lambda e, p=p, n=n, r=r: e.tensor_tensor(
                            out=ac[:, p * 512:p * 512 + n], in0=ac[:, p * 512:p * 512 + n], in1=th[r][:, 0:n],
                            op=ALU.add),
                            reads=[h_th[r]], writes=[h_ac])
                if h == 7:
                    if tb >= 2:
                        sc.op("dve", lambda e: e.tensor_reduce(
                            out=rmin[tb % NA][:], in_=ac[:, 0:L], axis=AX.X, op=ALU.min),
                            reads=[h_ac], writes=[h_rmin[tb % NA]])
                    sc.op("pool", lambda e: e.tensor_tensor(
                        out=ac[:, L - 128:L], in0=ac[:, L - 128:L], in1=mca[:], op=ALU.add),
                        reads=[h_m], writes=[h_ac])
            thunks.append(f)
        return thunks

    def part_T(tb):
        L = (tb + 1) * 128
        mT, h_mT = maskT[tb % 2], h_maskT[tb % 2]
        ac, h_ac = acc[tb % NA], h_acc[tb % NA]
        c = chains[tb % 2]
        ops = []
        A = ops.append
        KB = NBIS if L > 1024 else (NBIS - 1 if L > 512 else NBIS - 2)
        if tb >= 2:
            rm, h_rm = rmin[tb % NA], h_rmin[tb % NA]
            lo, cn = c.bs[:, 0:1], c.bs[:, 4:5]
            A(lambda: sc.op("dve", lambda e: e.max(out=c.m8[:], in_=ac[:, 0:L]), reads=[h_ac], writes=[c.h_m8]))
            A(lambda: sc.op("dve", lambda e: e.tensor_tensor(out=c.bs[:, 1:2], in0=c.m8[:, 0:1], in1=rm[:],
                                                             op=ALU.subtract),
                            reads=[c.h_m8, h_rm], writes=[c.h_rng]))
            A(lambda: sc.op("dve", lambda e: e.tensor_scalar(out=c.dl[:], in0=pw[:], scalar1=c.bs[:, 1:2],
                                                             scalar2=None, op0=ALU.mult),
                            reads=[h_pw, c.h_rng], writes=[c.h_dl]))
            A(lambda: sc.op("dve", lambda e: e.tensor_tensor(out=c.Mh[:, 0:1], in0=rm[:], in1=c.dl[:, 0:1],
                                                             op=ALU.add),
                            reads=[h_rm, c.h_dl], writes=[c.h_Mh]))
            A(lambda: sc.op("pool", lambda e: e.memset(c.cand[:], NEG), writes=[c.h_cand]))
            A(lambda: sc.op("pool", lambda e: e.memset(c.candh[:], -NEG), writes=[c.h_candh]))
            for k in range(KB):
                mid = c.Mh[:, k:k + 1]
                A(lambda mid=mid: sc.op("dve", lambda e: e.tensor_scalar(
                    out=c.j16[:, 0:L], in0=ac[:, 0:L], scalar1=mid, scalar2=None, op0=ALU.is_ge, op1=ALU.add,
                    accum_out=cn),
                    reads=[h_ac, c.h_Mh], writes=[c.h_j16, c.h_cnt]))
                A(lambda k=k: sc.op("dve", lambda e: e.scalar_tensor_tensor(
                    out=c.Uh[:, k:k + 1], in0=cn, scalar=float(KTOP), in1=c.dl[:, k:k + 1],
                    op0=ALU.is_ge, op1=ALU.mult),
                    reads=[c.h_cnt, c.h_dl], writes=[c.h_Uh]))
                A(lambda k=k, mid=mid: sc.op("dve", lambda e: e.scalar_tensor_tensor(
                    out=c.Mh[:, k + 1:k + 2], in0=c.Uh[:, k:k + 1], scalar=c.dl[:, k + 1:k + 2], in1=mid,
                    op0=ALU.subtract, op1=ALU.add),
                    reads=[c.h_Uh, c.h_dl], writes=[c.h_Mh]))
            nsplit = 6 + 3 * (KB // 2)
            A(lambda: sc.op("dve", lambda e: e.copy_predicated(out=c.cand[:, 0:KB], mask=c.Uh[:, 0:KB].bitcast(U32),
                                                               data=c.Mh[:, 0:KB]),
                            reads=[c.h_Uh, c.h_Mh], writes=[c.h_cand]))
            A(lambda: sc.op("dve", lambda e: e.tensor_reduce(out=c.bs[:, 6:7], in_=c.cand[:], axis=AX.X, op=ALU.max),
                            reads=[c.h_cand], writes=[c.h_tmp]))
            A(lambda: sc.op("dve", lambda e: e.tensor_tensor(out=lo, in0=c.bs[:, 6:7], in1=rm[:], op=ALU.max),
                            reads=[c.h_tmp, h_rm], writes=[c.h_lo]))
            A(lambda: sc.op("dve", lambda e: e.tensor_scalar(out=c.Vh[:, 0:KB], in0=c.Uh[:, 0:KB], scalar1=0.0,
                                                             scalar2=None, op0=ALU.is_equal),
                            reads=[c.h_Uh], writes=[c.h_Vh]))
            A(lambda: sc.op("dve", lambda e: e.copy_predicated(out=c.candh[:, 0:KB], mask=c.Vh[:, 0:KB].bitcast(U32),
                                                               data=c.Mh[:, 0:KB]),
                            reads=[c.h_Vh, c.h_Mh], writes=[c.h_candh]))
            A(lambda: sc.op("dve", lambda e: e.tensor_reduce(out=c.fs[:, 0:1], in_=c.candh[:], axis=AX.X, op=ALU.min),
                            reads=[c.h_candh], writes=[c.h_fs]))
            A(lambda: sc.op("dve", lambda e: e.tensor_tensor(out=c.fs[:, 1:2], in0=c.fs[:, 0:1], in1=c.m8[:, 0:1],
                                                             op=ALU.min),
                            reads=[c.h_fs, c.h_m8], writes=[c.h_fs]))
            A(lambda: sc.op("pool", lambda e: e.memset(wsel[:, 0:L], NEG), writes=[h_wsel]))
            A(lambda: sc.op("dve", lambda e: e.tensor_scalar(
                out=junk[:, 0:L], in0=ac[:, 0:L], scalar1=c.fs[:, 1:2], scalar2=None, op0=ALU.is_lt, op1=ALU.add,
                accum_out=c.fs[:, 2:3]),
                reads=[h_ac, c.h_fs], writes=[h_junk, c.h_fs]))
            A(lambda: sc.op("dve", lambda e: e.copy_predicated(
                out=wsel[:, 0:L], mask=junk[:, 0:L].bitcast(U32), data=ac[:, 0:L]),
                reads=[h_junk, h_ac], writes=[h_wsel]))
            A(lambda: sc.op("dve", lambda e: e.max(out=c.m8b[:], in_=wsel[:, 0:L]), reads=[h_wsel], writes=[c.h_m8b]))
            A(lambda: sc.op("dve", lambda e: e.tensor_scalar(out=c.fs[:, 3:4], in0=c.fs[:, 2:3],
                                                             scalar1=float(KTOP - 1 - L), scalar2=None, op0=ALU.add),
                            reads=[c.h_fs], writes=[c.h_fs]))
            A(lambda: sc.op("dve", lambda e: e.scalar_tensor_tensor(
                out=c.t8[:], in0=io8[:], scalar=c.fs[:, 3:4], in1=c.m8b[:], op0=ALU.is_equal, op1=ALU.mult,
                accum_out=c.fs[:, 4:5]),
                reads=[h_io8, c.h_fs, c.h_m8b], writes=[c.h_t8, c.h_fs]))
            A(lambda: sc.op("dve", lambda e: e.tensor_scalar(out=c.fs[:, 5:6], in0=c.fs[:, 3:4], scalar1=7.5,
                                                             scalar2=None, op0=ALU.is_gt),
                            reads=[c.h_fs], writes=[c.h_fs]))
            A(lambda: sc.op("dve", lambda e: e.copy_predicated(out=c.fs[:, 4:5], mask=c.fs[:, 5:6].bitcast(U32),
                                                               data=lo),
                            reads=[c.h_fs, c.h_lo], writes=[c.h_fs]))
            thr, hthr = c.fs[:, 4:5], c.h_fs
        else:
            nsplit = 0
            thr, hthr = thrc[:, 0:1], h_thrc
        A(lambda: sc.op("dve", lambda e: e.tensor_scalar(
            out=mask[:, 0:L], in0=ac[:, 0:L], scalar1=thr, scalar2=None, op0=ALU.is_lt),
            reads=[h_ac, hthr], writes=[h_mask]))
        for kb0 in range(0, tb + 1, 8):
            nb = min(8, tb + 1 - kb0)
            pb = kb0 // 8
            pv = P.bank[pb][:].bitcast(BF16)
            def grp(kb0=kb0, nb=nb, pv=pv, pb=pb):
                for kb in range(kb0, kb0 + nb):
                    sc.op("pe", lambda e, kb=kb: e.transpose(
                        out=pv[:, (kb - kb0) * 128:(kb - kb0 + 1) * 128], in_=mask[:, kb * 128:(kb + 1) * 128],
                        identity=P.ident[:]),
                        reads=[h_mask, P.h_const], writes=[P.bankh[pb]] if kb == kb0 else (),
                        wadd=() if kb == kb0 else [P.bankh[pb]])
                sc.op("act", lambda e: e.copy(
                    out=mT[:, kb0:kb0 + nb, :], in_=pv[:, 0:nb * 128].rearrange("p (c t) -> p c t", c=nb)),
                    reads=[P.bankh[pb]], writes=[h_mT] if kb0 == 0 else (), wadd=() if kb0 == 0 else [h_mT])
            A(grp)
        return ops[:nsplit], ops[nsplit:]

    def part_A(tb):
        mT, h_mT = maskT[tb % 2], h_maskT[tb % 2]
        thunks = []
        for gk in range(2):
            ob = 6 + gk
            for kb in range(tb + 1):
                def f(gk=gk, ob=ob, kb=kb):
                    sset = cnt["lb"] % 2
                    cnt["lb"] += 1
                    bxy = (2, 3) if sset == 0 else (4, 5)
                    r = cnt["ep"] % NE
                    cnt["ep"] += 1
                    for half in range(2):
                        p0 = half * 64
                        b = bxy[half]
                        sc.op("pe", lambda e, p0=p0, b=b: e.matmul(
                            P.bank[b][:, 0:256],
                            lhsT=kT2[p0:p0 + 64, gk, kb * 128:(kb + 1) * 128],
                            rhs=dqT[p0:p0 + 64, 2 * gk:2 * gk + 2, tb * 128:(tb + 1) * 128],
                            start=True, stop=False),
                            reads=[hk2[kb], hdq[tb]], writes=[P.bankh[b]])
                    for half in range(2):
                        b = bxy[half]
                        sc.op("pe", lambda e, b=b: e.matmul(
                            P.bank[b][:, 0:256], lhsT=P.nident[:],
                            rhs=mT[:, kb, :].unsqueeze(1).to_broadcast([128, 2, 128]), start=False, stop=True),
                            reads=[h_mT, P.h_const], wadd=[P.bankh[b]])
                    for half in range(2):
                        b = bxy[half]
                        sc.op("act", lambda e, b=b, half=half: e.activation(
                            out=pbuf[r][:, half * 256:(half + 1) * 256], in_=P.bank[b][:, 0:256], func=AF.Exp,
                            scale=scale),
                            reads=[P.bankh[b]], writes=[h_p[r]] if half == 0 else (), wadd=() if half == 0 else [h_p[r]])
                    sc.op("pe", lambda e: e.matmul(
                        P.bank[ob][:, :], lhsT=vp[:, kb, gk * 128:(gk + 1) * 128], rhs=pbuf[r][:, :],
                        start=(kb == 0), stop=(kb == tb)),
                        reads=[h_p[r], hvp[kb]], writes=[P.bankh[ob]] if kb == 0 else (),
                        wadd=() if kb == 0 else [P.bankh[ob]])
                thunks.append(f)

        def fin():
            first = True
            for gk in range(2):
                ob = 6 + gk
                sc.op("act", lambda e, ob=ob, gk=gk: e.activation(
                    out=rec[gk][64:128, :], in_=P.bank[ob][64:128, :], func=AF.Ln),
                    reads=[P.bankh[ob]], writes=[h_rec[gk]])
                sc.op("act", lambda e, gk=gk: e.activation(
                    out=rec[gk][64:128, :], in_=rec[gk][64:128, :], func=AF.Exp, scale=-1.0),
                    writes=[h_rec[gk]])
                for blk in range(4):
                    half, cidx = blk // 2, blk % 2
                    sc.op("dve", lambda e, ob=ob, gk=gk, blk=blk, half=half, cidx=cidx: e.tensor_tensor(
                        out=odT[half * 64:(half + 1) * 64, 2 * gk + cidx, tb * 128:(tb + 1) * 128],
                        in0=P.bank[ob][0:64, blk * 128:(blk + 1) * 128],
                        in1=rec[gk][64:128, blk * 128:(blk + 1) * 128], op=ALU.mult),
                        reads=[P.bankh[ob], h_rec[gk]], writes=[odTh[tb]] if first else (),
                        wadd=() if first else [odTh[tb]])
                    first = False
        thunks.append(fin)
        return thunks

    def run(l):
        for f in l:
            f()

    def merge(lists):
        pos = [0] * len(lists)
        while True:
            best, bi = None, -1
            for i, l in enumerate(lists):
                if pos[i] < len(l):
                    key = (pos[i] + 0.5) / len(l)
                    if best is None or key < best:
                        best, bi = key, i
            if bi < 0:
                break
            lists[bi][pos[bi]]()
            pos[bi] += 1

    def zip2(x, y):
        out = []
        nx, ny = len(x), len(y)
        ix = iy = 0
        while ix < nx or iy < ny:
            if ix < nx and (iy >= ny or ix * ny <= iy * nx):
                out.append(x[ix])
                ix += 1
            else:
                out.append(y[iy])
                iy += 1
        return out

    halves = {}

    def T1(tb):
        halves[tb] = part_T(tb)
        return halves[tb][0]

    def T2(tb):
        if tb not in halves:
            halves[tb] = part_T(tb)
        return halves[tb][1]

    for tb0 in (0, 1, 2):
        run(part_I(tb0))
    run(T1(0))
    run(T2(0))
    for tb in range(NT):
        lists = [part_A(tb)]
        x = T2(tb + 1) if tb + 1 < NT else []
        if tb + 1 < NT and tb + 1 < 2:
            x = T1(tb + 1) + x
        y = T1(tb + 2) if tb + 2 < NT else []
        tl = (x + y) if _os.environ.get("NOZIP") else zip2(x, y)
        if tl:
            lists.append(tl)
        if tb + 3 < NT:
            lists.append(part_I(tb + 3))
        merge(lists)


def merge_phase(P, g, uT, uTh, osT, osTh, odT, odTh):
    nc, sc = P.nc, P.sc
    sb = lambda n, s, d: g.enter_context(nc.sbuf_tensor("mg_" + n, s, d))
    wg = sb("wg", [128, KC, 2 * D], BF16)
    wbs = sb("wbs", [128, 4, D], BF16)
    wbd = sb("wbd", [128, 4, D], BF16)
    bg = sb("bg", [128, 16], F32)
    h_wg = [H() for _ in range(4)]
    h_wbs, h_wbd, h_bg = H(), H(), H()
    sc.op("sp", lambda e: e.dma_start(out=bg[:], in_=P.b_gate), writes=[h_bg], dma=True)
    for q in (0, 2, 1, 3):
        load_w(P, "sp", wg[:, :, q * 512:(q + 1) * 512], "w_gate", q * 512, 512, h_wg[q])
        if q == 2:
            load_w(P, "sp", wbs[:], "w_branch_sb", 0, D, h_wbs)
            load_w(P, "sp", wbd[:], "w_branch_dsa", 0, D, h_wbd)
    g1 = [sb("g1_%d" % i, [128, 512], F32) for i in range(2)]
    g2 = [sb("g2_%d" % i, [128, 512], F32) for i in range(2)]
    m1 = [sb("m1_%d" % i, [128, 512], F32) for i in range(2)]
    m2 = [sb("m2_%d" % i, [128, 512], F32) for i in range(2)]
    h_g1, h_g2, h_m1, h_m2 = [H(), H()], [H(), H()], [H(), H()], [H(), H()]
    tmp = sb("tmp", [128, KC, 512], BF16)
    h_tmp = H()
    it = 0
    for tg in range(4):
        tiles = list(range(tg * 4, tg * 4 + 4))
        for c in range(KC):
            r = it % 2
            it += 1
            bs = [0, 1, 2, 3] if r == 0 else [4, 5, 6, 7]
            specs = [(bs[0], wg, c * 128, uT, KC, [h_wg[c // 4]] + [uTh[i] for i in tiles]),
                     (bs[1], wg, D + c * 128, uT, KC, [h_wg[2 + c // 4]] + [uTh[i] for i in tiles]),
                     (bs[2], wbs, c * 128, osT, 4, [h_wbs] + [osTh[i] for i in tiles]),
                     (bs[3], wbd, c * 128, odT, 4, [h_wbd] + [odTh[i] for i in tiles])]
            for (bk, wt, c0, act, nk, rd) in specs:
                for kc in range(nk):
                    sc.op("pe", lambda e, bk=bk, wt=wt, c0=c0, act=act, kc=kc, nk=nk, tg=tg: e.matmul(
                        P.bank[bk][:, :], lhsT=wt[:, kc, c0:c0 + 128], rhs=act[:, kc, tg * 512:(tg + 1) * 512],
                        start=(kc == 0), stop=(kc == nk - 1)),
                        reads=rd, writes=[P.bankh[bk]] if kc == 0 else (), wadd=() if kc == 0 else [P.bankh[bk]])
            sc.op("act", lambda e, r=r, bk=bs[0], c=c: e.activation(
                out=g1[r][:], in_=P.bank[bk][:, :], func=AF.Sigmoid, bias=bg[:, c:c + 1]),
                reads=[P.bankh[bs[0]], h_bg], writes=[h_g1[r]])
            sc.op("act", lambda e, r=r, bk=bs[1], c=c: e.activation(
                out=g2[r][:], in_=P.bank[bk][:, :], func=AF.Sigmoid, bias=bg[:, 8 + c:9 + c]),
                reads=[P.bankh[bs[1]], h_bg], writes=[h_g2[r]])
            sc.op("dve", lambda e, r=r, bk=bs[2]: e.tensor_tensor(
                out=m1[r][:], in0=P.bank[bk][:, :], in1=g1[r][:], op=ALU.mult),
                reads=[P.bankh[bs[2]], h_g1[r]], writes=[h_m1[r]])
            sc.op("dve", lambda e, r=r, bk=bs[3]: e.tensor_tensor(
                out=m2[r][:], in0=P.bank[bk][:, :], in1=g2[r][:], op=ALU.mult),
                reads=[P.bankh[bs[3]], h_g2[r]], writes=[h_m2[r]])
            sc.op("pool", lambda e, r=r, c=c: e.tensor_tensor(
                out=tmp[:, c, :], in0=m1[r][:], in1=m2[r][:], op=ALU.add),
                reads=[h_m1[r], h_m2[r]], writes=[h_tmp] if c == 0 else (), wadd=() if c == 0 else [h_tmp])
        sc.op("dve", lambda e, tg=tg: e.tensor_copy(out=uT[:, :, tg * 512:(tg + 1) * 512], in_=tmp[:]),
              reads=[h_tmp], writes=[uTh[i] for i in tiles])


def wout_phase(P, g, mT, mTh, hres, hh):
    nc, sc = P.nc, P.sc
    sb = lambda n, s, d: g.enter_context(nc.sbuf_tensor("wo_" + n, s, d))
    wo = sb("wo", [128, KC, D], BF16)
    h_wo = H()
    load_w(P, "sp", wo[:], "w_out", 0, D, h_wo)
    xt = P.ntmp.xt
    h_xt = P.ntmp.h_xt
    banks = Banks(P, [0, 1, 2, 3])
    for i in range(NT):
        j = i % 2
        sc.op("sp", lambda e, i=i, j=j: e.dma_start(out=xt[j][:], in_=P.x[i * 128:(i + 1) * 128, :]),
              writes=[h_xt[j]], dma=True)
        for c0 in (0, 512):
            b = banks.next()
            for kc in range(KC):
                sc.op("pe", lambda e, b=b, i=i, kc=kc, c0=c0: e.matmul(
                    P.bank[b][:, :], lhsT=mT[:, kc, i * 128:(i + 1) * 128], rhs=wo[:, kc, c0:c0 + 512],
                    start=(kc == 0), stop=(kc == KC - 1)),
                    reads=[mTh[i], h_wo], writes=[P.bankh[b]] if kc == 0 else (), wadd=() if kc == 0 else [P.bankh[b]])
            sc.op("dve", lambda e, b=b, i=i, j=j, c0=c0: e.tensor_tensor(
                out=hres[:, i, c0:c0 + 512], in0=P.bank[b][:, :], in1=xt[j][:, c0:c0 + 512], op=ALU.add),
                reads=[P.bankh[b], h_xt[j]], writes=[hh[i]] if c0 == 0 else (), wadd=() if c0 == 0 else [hh[i]])


def cross_phase(P, g, uT, uTh, hres, hh, kcT, hkc, vc, hvc):
    nc, sc = P.nc, P.sc
    sb = lambda n, s, d: g.enter_context(nc.sbuf_tensor("cx_" + n, s, d))
    scale = 128.0 ** -0.5
    wq = sb("wq", [128, KC, 512], BF16)
    wco = sb("wco", [128, 4, D], BF16)
    h_wq, h_wco = H(), H()
    load_w(P, "sp", wq[:], "w_cq", 0, 512, h_wq)
    load_w(P, "sp", wco[:], "w_co", 0, D, h_wco)
    rmsnorm_T(P, None, "sbuf", "norm_cross", uT, uTh, src=hres, srch=hh, tag="n2")
    qcT = sb("qcT", [128, 4, S], BF16)
    hqc = [H() for _ in range(NT)]
    ocT = sb("ocT", [128, 4, S], BF16)
    hoc = [H() for _ in range(NT)]
    banks = Banks(P, [0, 1, 2, 3])
    kk = [0]

    def evq(j, tg, bap, bh):
        kk[0] += 1
        evac_copy(P, kk[0], qcT[:, j, tg * 512:(tg + 1) * 512], bap, [bh], [hqc[i] for i in range(tg * 4, tg * 4 + 4)])
    proj_fm(P, wq, h_wq, 512, uT, uTh, banks, evq)
    pT = [[sb("pT%d_%d" % (h, mb), [128, 512], BF16) for mb in range(2)] for h in range(4)]
    h_pT = [[H() for mb in range(2)] for h in range(4)]
    rden = sb("rden", [128, 4], F32)
    h_rden = H()
    otm = [sb("otm%d" % i, [128, 512], BF16) for i in range(2)]
    h_otm = [H(), H()]
    lbanks = Banks(P, [0, 1, 2, 3])
    for tg in range(4):
        for h in range(4):
            for mb in range(2):
                b = lbanks.next()
                sc.op("pe", lambda e, b=b, h=h, mb=mb, tg=tg: e.matmul(
                    P.bank[b][:, :], lhsT=kcT[:, h, mb * 128:(mb + 1) * 128], rhs=qcT[:, h, tg * 512:(tg + 1) * 512],
                    start=True, stop=True),
                    reads=[hkc[0]] + [hqc[i] for i in range(tg * 4, tg * 4 + 4)], writes=[P.bankh[b]])
                sc.op("act", lambda e, b=b, h=h, mb=mb: e.activation(
                    out=pT[h][mb][:], in_=P.bank[b][:, :], func=AF.Exp, scale=scale),
                    reads=[P.bankh[b]], writes=[h_pT[h][mb]])
        import os
        CXL = int(os.environ.get("CXL", "9"))
        if CXL < 1:
            continue
        for tt in range(4):
            i = tg * 4 + tt
            ro = i % 2
            for h in range(4):
                ob = 6 + h // 2
                oc = (h % 2) * 129
                for mb in range(2):
                    first = (h % 2 == 0 and mb == 0)
                    sc.op("pe", lambda e, ob=ob, oc=oc, h=h, mb=mb, tt=tt: e.matmul(
                        P.bank[ob][:, oc:oc + 129], lhsT=pT[h][mb][:, tt * 128:(tt + 1) * 128],
                        rhs=vc[:, mb, h * 129:(h + 1) * 129], start=(mb == 0), stop=(mb == 1)),
                        reads=[h_pT[h][mb], hvc[mb]], writes=[P.bankh[ob]] if first else (),
                        wadd=() if first else [P.bankh[ob]])
            if CXL < 2:
                continue
            for h in range(4):
                ob = 6 + h // 2
                oc = (h % 2) * 129
                sc.op("dve", lambda e, ob=ob, oc=oc, h=h: e.reciprocal(
                    out=rden[:, h:h + 1], in_=P.bank[ob][:, oc + 128:oc + 129]),
                    reads=[P.bankh[ob]], writes=[h_rden] if h == 0 else (), wadd=() if h == 0 else [h_rden])
            CXR = int(os.environ.get("CXR", "9"))
            for h in range(4):
                ob = 6 + h // 2
                oc = (h % 2) * 129
                if True:
                    sc.op("act", lambda e, ob=ob, oc=oc, h=h, ro=ro: e.activation(
                        out=otm[ro][:, h * 128:(h + 1) * 128], in_=P.bank[ob][:, oc:oc + 128], func=AF.Copy,
                        scale=rden[:, h:h + 1]),
                        reads=[P.bankh[ob], h_rden], writes=[h_otm[ro]] if h == 0 else (),
                        wadd=() if h == 0 else [h_otm[ro]])
                else:
                    sc.op("dve", lambda e, ob=ob, oc=oc, h=h, ro=ro: e.tensor_scalar(
                        out=otm[ro][:, h * 128:(h + 1) * 128], in0=P.bank[ob][:, oc:oc + 128],
                        scalar1=rden[:, h:h + 1], scalar2=None, op0=ALU.mult),
                        reads=[P.bankh[ob], h_rden], writes=[h_otm[ro]] if h == 0 else (),
                        wadd=() if h == 0 else [h_otm[ro]])
            if CXL < 3:
                continue
            b = 4 + (i % 2)
            pv = P.bank[b][:].bitcast(BF16)
            for c in range(4):
                sc.op("pe", lambda e, c=c, ro=ro, pv=pv: e.transpose(
                    out=pv[:, c * 128:(c + 1) * 128], in_=otm[ro][:, c * 128:(c + 1) * 128], identity=P.ident[:]),
                    reads=[h_otm[ro], P.h_const], writes=[P.bankh[b]] if c == 0 else (),
                    wadd=() if c == 0 else [P.bankh[b]])
            sc.op("act", lambda e, i=i, pv=pv: e.copy(
                out=ocT[:, :, i * 128:(i + 1) * 128], in_=pv[:, 0:512].rearrange("p (c t) -> p c t", c=4)),
                reads=[P.bankh[b]], writes=[hoc[i]])
    if P.stage == "H3":
        return
    for i in range(NT):
        for c0 in (0, 512):
            b = lbanks.next()
            for kc in range(4):
                sc.op("pe", lambda e, b=b, i=i, kc=kc, c0=c0: e.matmul(
                    P.bank[b][:, :], lhsT=ocT[:, kc, i * 128:(i + 1) * 128], rhs=wco[:, kc, c0:c0 + 512],
                    start=(kc == 0), stop=(kc == 3)),
                    reads=[hoc[i], h_wco], writes=[P.bankh[b]] if kc == 0 else (), wadd=() if kc == 0 else [P.bankh[b]])
            sc.op("dve", lambda e, b=b, i=i, c0=c0: e.tensor_tensor(
                out=hres[:, i, c0:c0 + 512], in0=P.bank[b][:, :], in1=hres[:, i, c0:c0 + 512], op=ALU.add),
                reads=[P.bankh[b]], writes=[hh[i]])


def mlp_phase(P, g, uT, uTh, hres, hh):
    nc, sc = P.nc, P.sc
    sb = lambda n, s, d: g.enter_context(nc.sbuf_tensor("ml_" + n, s, d))
    from contextlib import ExitStack
    rmsnorm_T(P, None, "sbuf", "norm_mlp", uT, uTh, src=hres, srch=hh, tag="n3")
    hid = sb("hid", [128, 32, 512], BF16)
    h_hid = [H() for _ in range(32)]
    wu = [sb("wu%d" % i, [128, KC, 512], BF16) for i in range(2)]
    h_wu = [H(), H()]
    wd = [sb("wd%d" % i, [128, 4, 512], BF16) for i in range(3)]
    h_wd = [H(), H(), H()]
    rl = [sb("rl%d" % i, [128, 512], F32) for i in range(2)]
    h_rl = [H(), H()]
    iu = 0
    idn = 0
    irl = 0
    ub = Banks(P, [4, 5, 6, 7])
    for tg in range(4):
        tiles = list(range(tg * 4, tg * 4 + 4))
        for cblk in range(8):
            r = iu % 2
            iu += 1
            load_w(P, "sp", wu[r][:], "w_up", cblk * 512, 512, h_wu[r])
            for j in range(4):
                b = ub.next()
                for kc in range(KC):
                    sc.op("pe", lambda e, b=b, r=r, kc=kc, j=j, tg=tg: e.matmul(
                        P.bank[b][:, :], lhsT=wu[r][:, kc, j * 128:(j + 1) * 128],
                        rhs=uT[:, kc, tg * 512:(tg + 1) * 512], start=(kc == 0), stop=(kc == KC - 1)),
                        reads=[h_wu[r]] + [uTh[i] for i in tiles], writes=[P.bankh[b]] if kc == 0 else (),
                        wadd=() if kc == 0 else [P.bankh[b]])
                q = irl % 2
                irl += 1
                sc.op("act", lambda e, b=b, q=q: e.activation(out=rl[q][:], in_=P.bank[b][:, :], func=AF.Relu),
                      reads=[P.bankh[b]], writes=[h_rl[q]])
                sc.op("pool", lambda e, q=q, cblk=cblk, j=j: e.tensor_tensor(
                    out=hid[:, cblk * 4 + j, :], in0=rl[q][:], in1=rl[q][:], op=ALU.mult),
                    reads=[h_rl[q]], writes=[h_hid[cblk * 4 + j]])
        for c0 in (0, 512):
            for rblk in range(8):
                r = idn % 3
                idn += 1
                load_w(P, "sp", wd[r][:], "w_down", c0, 512, h_wd[r], r0=rblk * 512, nk=4)
                for tt in range(4):
                    b = tt
                    for k4 in range(4):
                        kc = rblk * 4 + k4
                        first = (rblk == 0 and k4 == 0)
                        sc.op("pe", lambda e, b=b, r=r, k4=k4, kc=kc, tt=tt, first=first: e.matmul(
                            P.bank[b][:, :], lhsT=hid[:, kc, tt * 128:(tt + 1) * 128], rhs=wd[r][:, k4, :],
                            start=first, stop=(kc == 31)),
                            reads=[h_wd[r], h_hid[kc]], writes=[P.bankh[b]] if first else (),
                            wadd=() if first else [P.bankh[b]])
            for tt in range(4):
                i = tg * 4 + tt
                sc.op("dve", lambda e, tt=tt, i=i, c0=c0: e.tensor_tensor(
                    out=hres[:, i, c0:c0 + 512], in0=P.bank[tt][:, :], in1=hres[:, i, c0:c0 + 512], op=ALU.add),
                    reads=[P.bankh[tt]], writes=[hh[i]])


def final_phase(P, g, hres, hh):
    nc, sc = P.nc, P.sc
    T = P.ntmp
    gbc, junk, ot, st = T.gbc, T.junk, T.xt, T.st
    h_g, h_junk, h_ot, h_st = T.h_g, T.h_junk, T.h_xt, T.h_st
    sc.op("sp", lambda e: e.dma_start(out=gbc[:], in_=P.vec["norm_final"].to_broadcast([128, D])),
          writes=[h_g], dma=True)
    h_out = H()
    for i in range(NT):
        j = i % 2
        xin = hres[:, i, :]
        ss = st[:, 4 * i:4 * i + 1]
        ms = st[:, 4 * i + 1:4 * i + 2]
        sd = st[:, 4 * i + 2:4 * i + 3]
        rs = st[:, 4 * i + 3:4 * i + 4]
        sc.op("act", lambda e, xin=xin, ss=ss: e.activation(out=junk[:], in_=xin, func=AF.Square, accum_out=ss),
              reads=[hh[i]], writes=[h_junk, h_st[i]])
        sc.op("dve", lambda e, ss=ss, ms=ms: e.tensor_scalar(out=ms, in0=ss, scalar1=1.0 / D, scalar2=EPS,
                                                               op0=ALU.mult, op1=ALU.add),
              reads=[h_st[i]], writes=[h_st[i]])
        sc.op("act", lambda e, sd=sd, ms=ms: e.activation(out=sd, in_=ms, func=AF.Sqrt),
              reads=[h_st[i]], writes=[h_st[i]])
        sc.op("dve", lambda e, sd=sd, rs=rs: e.reciprocal(out=rs, in_=sd),
              reads=[h_st[i]], writes=[h_st[i]])
        sc.op("dve", lambda e, xin=xin, rs=rs, j=j: e.scalar_tensor_tensor(
            out=ot[j][:], in0=xin, scalar=rs, in1=gbc[:], op0=ALU.mult, op1=ALU.mult),
            reads=[hh[i], h_st[i], h_g], writes=[h_ot[j]])
        sc.op("sp", lambda e, i=i, j=j: e.dma_start(out=P.out[i * 128:(i + 1) * 128, :], in_=ot[j][:]),
              reads=[h_ot[j]], wadd=[h_out], dma=True)
    sc.op("sp", lambda e: e.nop(), reads=[h_out])


def phases(P, g):
    nc, sc = P.nc, P.sc
    sb = P.sb_global
    from contextlib import ExitStack
    cast_weights(P, ["w_in"], cols=(0, 1536), hname="w_in_sb")
    cast_weights(P, ["w_ckv"])
    cast_weights(P, ["w_in"], cols=(1536, D_IN))
    build_dsa_weights(P)
    P.rest_cast_done = False

    def cast_rest():
        if not P.rest_cast_done:
            cast_weights(P, [n for n, _, _ in W_SPECS if n not in ("w_in", "w_ckv")])
            P.rest_cast_done = True
    uT = sb("uT", [128, KC, S], BF16)
    uTh = [H("uT%d" % i) for i in range(NT)]
    P.ntmp = NormTmp(P, sb)
    kcT = sb("kcT", [128, 4, NMEM], BF16)
    hkc = [H()]
    vc = sb("vc", [128, 2, 4 * 129], BF16)
    hvc = [H(), H()]
    rmsnorm_T(P, None, "dram", "norm_mix", uT, uTh, src=P.x, tag="n1")
    if P.stage == "A":
        dbg_out(P, "uT", uT[:], [128, KC, S], BF16, uTh)
        return
    gmid = g.enter_context(ExitStack())
    sbm = lambda n, s, d: gmid.enter_context(nc.sbuf_tensor(n, s, d))
    osT = sbm("osT", [128, 4, S], BF16)
    osTh = [H() for _ in range(NT)]
    odT = sbm("odT", [128, 4, S], BF16)
    odTh = [H() for _ in range(NT)]
    if P.stage not in ("D", "E"):
      with ExitStack() as g2:
        sb2 = lambda n, s, d: g2.enter_context(nc.sbuf_tensor(n, s, d))
        qT = sb2("sbq", [128, 4, S], BF16)
        kT = sb2("sbk", [128, 4, S], BF16)
        v = sb2("sbv", [128, NT, 512], BF16)
        hq = [H() for _ in range(NT)]
        hk = [H() for _ in range(NT)]
        hv = [H() for _ in range(NT)]
        with ExitStack() as g3:
            wb = [g3.enter_context(nc.sbuf_tensor("wb%d" % i, [128, KC, 512], BF16)) for i in range(2)]
            wbh = [H(), H()]
            banks = Banks(P, [0, 1, 2, 3])
            kk = [0]
            for wi, (dst, hd) in enumerate([(qT, hq), (kT, hk)]):
                load_w(P, "sp", wb[wi % 2][:], "w_in", wi * 512, 512, wbh[wi % 2], hname="w_in_sb")

                def ev(j, tg, bap, bh, dst=dst, hd=hd):
                    kk[0] += 1
                    evac_copy(P, kk[0], dst[:, j, tg * 512:(tg + 1) * 512], bap, [bh],
                              [hd[i] for i in range(tg * 4, tg * 4 + 4)])
                proj_fm(P, wb[wi % 2], wbh[wi % 2], 512, uT, uTh, banks, ev)
            load_w(P, "sp", wb[0][:], "w_in", 1024, 512, wbh[0], hname="w_in_sb")

            def evv(i, c0, cw, bap, bh):
                kk[0] += 1
                evac_copy(P, kk[0], v[:, i, c0:c0 + cw], bap, [bh], [hv[i]])
            proj_tm(P, wb[0], wbh[0], 512, uT, uTh, banks, evv)
            memT = g3.enter_context(nc.sbuf_tensor("memT", [128, KC, NMEM], BF16))
            memTh = [H(), H()]
            wkv = g3.enter_context(nc.sbuf_tensor("wkv", [128, KC, D], BF16))
            h_wkv = H()
            load_w(P, "sp", wkv[:], "w_ckv", 0, D, h_wkv)
            rmsnorm_T(P, None, "dram", "norm_mem", memT, memTh, ntiles=2, src=P.mem, tag="nm")

            def evk(j, tg, bap, bh):
                kk[0] += 1
                evac_copy(P, kk[0], kcT[:, j, :], bap, [bh], (), wadd=[hkc[0]])
            proj_fm(P, wkv, h_wkv, 512, memT, memTh, banks, evk, ntok=NMEM)
            sc.op("pool", lambda e: e.memset(vc[:], 1.0), writes=hvc)

            def evmv(i, c0, cw, bap, bh):
                sc.op("act", lambda e, i=i, bap=bap: e.copy(
                    out=vc[:, i, :].rearrange("p (h c) -> p h c", c=129)[:, :, 0:128],
                    in_=bap[:, 0:512].rearrange("p (h c) -> p h c", c=128)),
                    reads=[bh], writes=[hvc[i]])
            proj_tm(P, wkv[:, :, 512:1024], h_wkv, 512, memT, memTh, banks, evmv, ntiles=2)
            sc.flush()
        if P.stage == "B":
            dbg_out(P, "qT", qT[:], [128, 4, S], BF16, hq)
            dbg_out(P, "kT", kT[:], [128, 4, S], BF16, hk)
            dbg_out(P, "v", v[:], [128, NT, 512], BF16, hv)
            sc.flush()
            return
        with ExitStack() as g3:
            sb_attention(P, g3, qT, kT, v, hq, hk, hv, osT, osTh)
            cast_rest()
            sc.flush()
    if P.stage == "C":
        dbg_out(P, "osT", osT[:], [128, 4, S], BF16, osTh)
        return
    cast_rest()
    with ExitStack() as g2:
        sb2 = lambda n, s, d: g2.enter_context(nc.sbuf_tensor(n, s, d))
        dqT = sb2("dqT", [128, 4, S], BF16)
        kT2 = sb2("kT2", [128, 2, S], BF16)
        iqT = sb2("iqT", [128, 4, S], BF16)
        ikT2 = sb2("ikT2", [128, 1, S], BF16)
        vp = sb2("vp", [128, NT, 256], BF16)
        iwS = sb2("iwS", [128, NT, 8], F32)
        hdq = [H() for _ in range(NT)]
        hk2 = [H() for _ in range(NT)]
        hiq = [H() for _ in range(NT)]
        hik2 = [H() for _ in range(NT)]
        hvp = [H() for _ in range(NT)]
        hiw = [H() for _ in range(NT)]
        with ExitStack() as g3:
            dsa_project(P, g3, uT, uTh, dqT, kT2, iqT, ikT2, vp, iwS, hdq, hk2, hiq, hik2, hvp, hiw)
            sc.flush()
        if P.stage == "D":
            dbg_out(P, "dqT", dqT[:], [128, 4, S], BF16, hdq)
            dbg_out(P, "kT2", kT2[:], [128, 2, S], BF16, hk2)
            dbg_out(P, "iqT", iqT[:], [128, 4, S], BF16, hiq)
            dbg_out(P, "ikT2", ikT2[:], [128, 1, S], BF16, hik2)
            dbg_out(P, "vp", vp[:], [128, NT, 256], BF16, hvp)
            dbg_out(P, "iwS", iwS[:], [128, NT, 8], F32, hiw)
            sc.flush()
            return
        with ExitStack() as g3:
            dsa_attention(P, g3, dqT, kT2, iqT, ikT2, vp, iwS, hdq, hk2, hiq, hik2, hvp, hiw, odT, odTh)
            sc.flush()
    if P.stage == "E":
        dbg_out(P, "odT", odT[:], [128, 4, S], BF16, odTh)
        sc.flush()
        gmid.close()
        return
    with ExitStack() as g2:
        merge_phase(P, g2, uT, uTh, osT, osTh, odT, odTh)
        sc.flush()
    gmid.close()
    if P.stage == "F":
        dbg_out(P, "mT", uT[:], [128, KC, S], BF16, uTh)
        return
    hres = sb("hres", [128, NT, D], F32)
    hh = [H() for _ in range(NT)]
    with ExitStack() as g2:
        wout_phase(P, g2, uT, uTh, hres, hh)
        if P.stage == "G":
            sc.flush()
            dbg_out(P, "h1", hres[:], [128, NT, D], F32, hh)
            sc.flush()
            return
        cross_phase(P, g2, uT, uTh, hres, hh, kcT, hkc, vc, hvc)
        sc.flush()
    if P.stage in ("H", "H1", "H2", "H3"):
        dbg_out(P, "h2", hres[:], [128, NT, D], F32, hh)
        return
    with ExitStack() as g2:
        mlp_phase(P, g2, uT, uTh, hres, hh)
        sc.flush()
    if P.stage == "I":
        dbg_out(P, "h3", hres[:], [128, NT, D], F32, hh)
        return
    with ExitStack() as g2:
        final_phase(P, g2, hres, hh)
        sc.flush()


def dbg_out(P, name, ap, shape, dt, hs):
    nc, sc = P.nc, P.sc
    o = nc.dram_tensor("dbg_" + name, list(shape), dt, kind="ExternalOutput").ap()
    P.dbg[name] = o
    hd = H()
    op = sc.op("sp", lambda e: e.dma_start(out=o, in_=ap), reads=list(hs), writes=[hd], dma=True)
    sc.op("sp", lambda e: e.nop(), reads=[hd])


def make_in_maps(inputs, ncores=8):
    cs = _consts()
    shared = {}
    for name, r, c in W_SPECS:
        shared[name] = np.ascontiguousarray(np.asarray(inputs[name], np.float32).reshape(r, c))
    for name in V_SPECS:
        shared[name] = np.ascontiguousarray(np.asarray(inputs[name], np.float32).reshape(1, D))
    shared["b_gate"] = np.ascontiguousarray(
        np.asarray(inputs["b_gate"], np.float32).reshape(16, 128).T)
    for k, v in cs.items():
        shared["c_" + k] = v
    x = np.asarray(inputs["x"], np.float32)
    mem = np.asarray(inputs["mem"], np.float32)
    maps = []
    for b in range(ncores):
        m = dict(shared)
        m["x"] = np.ascontiguousarray(x[b])
        m["mem"] = np.ascontiguousarray(mem[b])
        maps.append(m)
    return maps


_CACHE = {}


def kernel(**inputs):
    if "nc" not in _CACHE:
        _CACHE["nc"] = build("full")
    nc, P = _CACHE["nc"]
    maps = make_in_maps(inputs, 8)
    res = run_bass_kernel_spmd(nc, maps, core_ids=list(range(8)))
    out = np.stack([np.asarray(r["out"], np.float32) for r in res.results], axis=0)
    return out
```

```python
import numpy as np
import ml_dtypes
import concourse.bass as bass
import concourse.mybir as mybir
from concourse.bass_utils import run_bass_kernel_spmd

F32 = mybir.dt.float32
BF16 = mybir.dt.bfloat16
AF = mybir.ActivationFunctionType
ALU = mybir.AluOpType

S = 2048
D = 1024
NT = S // 128
KC = D // 128
NMEM = 256
DFF = 4096
D_IN = 2888
EPS = 1e-6
NEG = -1.0e30
NBIS = 14
AX = mybir.AxisListType
import os as _os
TK_ACT = 0


class H:
    __slots__ = ("writers", "readers", "name")

    def __init__(self, name=""):
        self.writers = []
        self.readers = []
        self.name = name


class Op:
    __slots__ = ("eng", "fn", "waits", "count", "needed", "is_dma", "dslot", "dval")


class Sched:
    ENG = ["pe", "act", "dve", "pool", "sp"]
    R = 8

    def __init__(self, nc):
        self.nc = nc
        self.ops = {e: [] for e in self.ENG}
        self.ndma = {e: 0 for e in self.ENG}
        self.dma_ops = {e: [] for e in self.ENG}
        self.cnt = {e: 0 for e in self.ENG}
        self.esem = {e: nc.alloc_semaphore("es_" + e) for e in self.ENG}
        self.dsem = {
            e: [nc.alloc_semaphore("ds_%s_%d" % (e, i)) for i in range(self.R)]
            for e in ("sp", "pool", "act")
        }
        self.waited = {e: {} for e in self.ENG}
        self.nblk = 0

    def op(self, eng, fn, reads=(), writes=(), wadd=(), dma=False):
        o = Op()
        o.eng = eng
        o.fn = fn
        o.is_dma = dma
        o.needed = False
        o.count = 0
        deps = []
        for h in reads:
            deps += h.writers
        for h in writes:
            deps += h.writers
            deps += h.readers
        for h in wadd:
            deps += h.readers
        seen = set()
        w = []
        for d in deps:
            if id(d) in seen:
                continue
            seen.add(id(d))
            if d.eng == eng and eng == "pe" and (not d.is_dma) and (not dma):
                continue
            w.append(d)
            d.needed = True
        if dma:
            q = self.ndma[eng]
            self.ndma[eng] += 1
            o.dslot = q % self.R
            o.dval = 16 * (q // self.R + 1)
            o.needed = True
            if q >= self.R:
                w.append(self.dma_ops[eng][q - self.R])
            self.dma_ops[eng].append(o)
        o.waits = w
        self.ops[eng].append(o)
        for h in reads:
            h.readers.append(o)
        for h in writes:
            h.writers = [o]
            h.readers = []
        for h in wadd:
            h.writers.append(o)
        return o

    def flush(self):
        nc = self.nc
        for e in self.ENG:
            c = self.cnt[e]
            for o in self.ops[e]:
                if o.needed and not o.is_dma:
                    c += 1
                    o.count = c
            self.cnt[e] = c
            assert c < 60000, (e, c)
        names = {"pe": "tensor", "act": "scalar", "dve": "vector", "pool": "gpsimd", "sp": "sync"}
        pending = {e: self.ops[e] for e in self.ENG}
        self.ops = {e: [] for e in self.ENG}
        with nc.Block() as blk:
            for e in self.ENG:
                if not pending[e]:
                    continue

                def body(eng, e=e):
                    waited = self.waited[e]
                    for o in pending[e]:
                        for d in o.waits:
                            if d.is_dma:
                                sem = self.dsem[d.eng][d.dslot]
                                key = (d.eng, d.dslot)
                                val = d.dval
                            else:
                                sem = self.esem[d.eng]
                                key = d.eng
                                val = d.count
                            if waited.get(key, 0) >= val:
                                continue
                            eng.wait_ge(sem, val)
                            waited[key] = val
                        ins = o.fn(eng)
                        if o.is_dma:
                            ins.then_inc(self.dsem[e][o.dslot], 16)
                        elif o.needed:
                            ins.then_inc(self.esem[e], 1)

                getattr(blk, names[e])(body)
        self.nblk += 1


class Prog:
    pass


def _consts():
    inv = 10000.0 ** (-np.arange(0, 64, 2, dtype=np.float64) / 64.0)
    pos = np.arange(S, dtype=np.float64)
    ang = inv[:, None] * pos[None, :]
    cos = np.cos(ang)
    sin = np.sin(ang)
    cosT = np.zeros((128, S), np.float32)
    sinT = np.zeros((128, S), np.float32)
    for p in range(128):
        d = p % 64
        f = d % 32
        cosT[p] = cos[f]
        sinT[p] = -sin[f] if d < 32 else sin[f]
    ident = np.eye(128, dtype=np.float32).astype(ml_dtypes.bfloat16)
    t = np.arange(128)[:, None]
    s = np.arange(128)[None, :]
    m_strict = (s < t).astype(np.float32)
    m_strict_inv = (s >= t).astype(np.float32)
    m_caus_add = np.where(s <= t, 0.0, NEG).astype(np.float32)
    m_neg = np.where(s >= t, -2048.0, 0.0).astype(np.float32).astype(ml_dtypes.bfloat16)
    nident = (np.eye(128, dtype=np.float32) * -30000.0).astype(ml_dtypes.bfloat16)
    pw = np.tile((2.0 ** -(np.arange(NBIS + 2, dtype=np.float64) + 1)).astype(np.float32)[None, :], (128, 1))
    iota8 = np.tile(np.arange(8, dtype=np.float32)[None, :], (128, 1))
    pm = np.zeros((128, 128), np.float32)
    for pp in range(128):
        pm[(pp % 64 + 32) % 64 + 64 * (pp // 64), pp] = 1.0
    pm = pm.astype(ml_dtypes.bfloat16)
    return dict(cosT=cosT, sinT=sinT, ident=ident, nident=nident, pw=pw, iota8=iota8, pm=pm,
                m_strict=m_strict.astype(ml_dtypes.bfloat16),
                m_strict_inv=m_strict_inv, m_caus_add=m_caus_add, m_neg=m_neg)


W_SPECS = [
    ("w_in", D, D_IN), ("w_branch_sb", 512, D), ("w_branch_dsa", 512, D),
    ("w_gate", D, 2 * D), ("w_out", D, D), ("w_cq", D, 512), ("w_ckv", D, D),
    ("w_co", 512, D), ("w_up", D, DFF), ("w_down", DFF, D),
]
V_SPECS = ["norm_mix", "norm_cross", "norm_mem", "norm_mlp", "norm_final"]


def build(stage="full"):
    nc = bass.Bass("TRN2", target_bir_lowering=False)
    P = Prog()
    P.nc = nc
    P.stage = stage
    sc = Sched(nc)
    P.sc = sc
    P.dbg = {}

    P.x = nc.dram_tensor("x", [S, D], F32, kind="ExternalInput").ap()
    P.mem = nc.dram_tensor("mem", [NMEM, D], F32, kind="ExternalInput").ap()
    P.w32 = {}
    P.wbf = {}
    P.wh = {}
    for name, r, c in W_SPECS:
        P.w32[name] = nc.dram_tensor(name, [r, c], F32, kind="ExternalInput").ap()
        P.wbf[name] = nc.dram_tensor(name + "_bf", [r, c], BF16, kind="Internal").ap()
    P.vec = {}
    for name in V_SPECS:
        P.vec[name] = nc.dram_tensor(name, [1, D], F32, kind="ExternalInput").ap()
    P.b_gate = nc.dram_tensor("b_gate", [128, 16], F32, kind="ExternalInput").ap()
    cs = _consts()
    P.cdram = {}
    for k, v in cs.items():
        dt = BF16 if v.dtype == ml_dtypes.bfloat16 else F32
        P.cdram[k] = nc.dram_tensor("c_" + k, list(v.shape), dt, kind="ExternalInput").ap()
    P.out = nc.dram_tensor("out", [S, D], F32, kind="ExternalOutput").ap()

    P.bank = [nc.alloc_psum_tensor("bank%d" % i, [128, 512], F32) for i in range(8)]
    P.bankh = [H("bank%d" % i) for i in range(8)]

    from contextlib import ExitStack
    with ExitStack() as g:
        def sb(name, shape, dt):
            return g.enter_context(nc.sbuf_tensor(name, shape, dt))
        P.sb_global = sb
        P.ident = sb("ident", [128, 128], BF16)
        P.h_const = H("const")
        sc.op("sp", lambda e: e.dma_start(out=P.ident[:], in_=P.cdram["ident"]),
              writes=[P.h_const], dma=True)
        P.nident = sb("nident", [128, 128], BF16)
        sc.op("sp", lambda e: e.dma_start(out=P.nident[:], in_=P.cdram["nident"]),
              wadd=[P.h_const], dma=True)
        phases(P, g)
        sc.flush()
    return nc, P


def cast_dma(P, fn, wh, first):
    sc = P.sc
    hist = P.__dict__.setdefault("cast_hist", [])
    hc = H()
    rd = [hist[-3]] if len(hist) >= 3 else []
    sc.op("pool", fn, reads=rd, writes=[hc] + ([wh] if first else []), wadd=() if first else [wh], dma=True)
    hist.append(hc)


def cast_weights(P, names, cols=None, hname=None):
    for name in names:
        w = P.w32[name]
        o = P.wbf[name]
        r, c = w.shape
        ca, cb = cols if cols is not None else (0, c)
        hn = hname or name
        P.wh[hn] = H("w_" + hn)
        ncs = (cb - ca + 2047) // 2048
        cw = (cb - ca + ncs - 1) // ncs
        first = True
        for r0 in range(0, r, 1024):
            r1 = min(r, r0 + 1024)
            for c0 in range(ca, cb, cw):
                c1 = min(cb, c0 + cw)
                cast_dma(P, lambda e, w=w, o=o, r0=r0, r1=r1, c0=c0, c1=c1: e.dma_start(
                    out=o[r0:r1, c0:c1], in_=w[r0:r1, c0:c1]), P.wh[hn], first)
                first = False


def load_w(P, eng, dst, name, c0, ncols, hdst, r0=0, nk=None, hname=None):
    w = P.wbf[name]
    rows = w.shape[0]
    if nk is None:
        nk = rows // 128
    src = w.rearrange("(kc p) n -> p kc n", p=128)[:, r0 // 128:r0 // 128 + nk, c0:c0 + ncols]
    return P.sc.op(eng, lambda e: e.dma_start(out=dst, in_=src),
                   reads=[P.wh[hname or name]], writes=[hdst], dma=True)


class NormTmp:
    def __init__(self, P, sb):
        self.gbc = sb("nt_gbc", [128, D], F32)
        self.xt = [sb("nt_xt%d" % i, [128, D], F32) for i in range(2)]
        self.junk = sb("nt_junk", [128, D], BF16)
        self.ub = [sb("nt_ub%d" % i, [128, D], BF16) for i in range(2)]
        self.st = sb("nt_st", [128, 4 * NT], F32)
        self.h_g = H()
        self.h_xt = [H(), H()]
        self.h_junk = H()
        self.h_ub = [H(), H()]
        self.h_st = [H() for _ in range(NT)]


def rmsnorm_T(P, g, src_kind, gname, uT, uTh, ntiles=NT, src=None, srch=None, tag="n"):
    nc, sc = P.nc, P.sc
    T = P.ntmp
    gbc, xt, junk, ub, st = T.gbc, T.xt, T.junk, T.ub, T.st
    h_g, h_xt, h_junk, h_ub, h_st = T.h_g, T.h_xt, T.h_junk, T.h_ub, T.h_st
    sc.op("sp", lambda e: e.dma_start(out=gbc[:], in_=P.vec[gname].to_broadcast([128, D])),
          writes=[h_g], dma=True)
    pbank = [6, 7]
    for i in range(ntiles):
        j = i % 2
        if src_kind == "dram":
            xin = xt[j][:]
            hx = h_xt[j]
            sc.op("sp", lambda e, i=i, j=j: e.dma_start(out=xt[j][:], in_=src[i * 128:(i + 1) * 128, :]),
                  writes=[hx], dma=True)
        else:
            xin = src[:, i, :]
            hx = srch[i]
        ss = st[:, 4 * i:4 * i + 1]
        ms = st[:, 4 * i + 1:4 * i + 2]
        sd = st[:, 4 * i + 2:4 * i + 3]
        rs = st[:, 4 * i + 3:4 * i + 4]
        sc.op("act", lambda e, xin=xin, ss=ss: e.activation(out=junk[:], in_=xin, func=AF.Square, accum_out=ss),
              reads=[hx], writes=[h_junk, h_st[i]])
        sc.op("dve", lambda e, ss=ss, ms=ms: e.tensor_scalar(out=ms, in0=ss, scalar1=1.0 / D, scalar2=EPS,
                                                               op0=ALU.mult, op1=ALU.add),
              reads=[h_st[i]], writes=[h_st[i]])
        sc.op("act", lambda e, sd=sd, ms=ms: e.activation(out=sd, in_=ms, func=AF.Sqrt),
              reads=[h_st[i]], writes=[h_st[i]])
        sc.op("dve", lambda e, sd=sd, rs=rs: e.reciprocal(out=rs, in_=sd),
              reads=[h_st[i]], writes=[h_st[i]])
        sc.op("dve", lambda e, xin=xin, rs=rs, j=j: e.scalar_tensor_tensor(
            out=ub[j][:], in0=xin, scalar=rs, in1=gbc[:], op0=ALU.mult, op1=ALU.mult),
            reads=[hx, h_st[i], h_g], writes=[h_ub[j]])
        pb = pbank[j]
        pv = P.bank[pb][:].bitcast(BF16)
        for c in range(KC):
            sc.op("pe", lambda e, c=c, j=j, pv=pv: e.transpose(
                out=pv[:, c * 128:(c + 1) * 128], in_=ub[j][:, c * 128:(c + 1) * 128], identity=P.ident[:]),
                reads=[h_ub[j], P.h_const], writes=[P.bankh[pb]] if c == 0 else (),
                wadd=() if c == 0 else [P.bankh[pb]])
        sc.op("act", lambda e, i=i, pv=pv: e.copy(
            out=uT[:, :, i * 128:(i + 1) * 128], in_=pv.rearrange("p (c t) -> p c t", c=KC)),
            reads=[P.bankh[pb]], writes=[uTh[i]])


class Banks:
    def __init__(self, P, ids):
        self.P = P
        self.ids = list(ids)
        self.i = 0

    def next(self):
        b = self.ids[self.i % len(self.ids)]
        self.i += 1
        return b


def evac_copy(P, k, out, in_, reads, writes, wadd=()):
    if k % 2 == 0:
        return P.sc.op("act", lambda e: e.copy(out=out, in_=in_), reads=reads, writes=writes, wadd=wadd)
    return P.sc.op("dve", lambda e: e.tensor_copy(out=out, in_=in_), reads=reads, writes=writes, wadd=wadd)


def proj_fm(P, wt, wth, ncols, uT, uTh, banks, evac, nk=KC, ntok=S):
    sc = P.sc
    tgw = min(512, ntok)
    for j in range(ncols // 128):
        for tg in range(ntok // tgw):
            b = banks.next()
            rd = [wth] + [uTh[i] for i in range(tg * tgw // 128, (tg + 1) * tgw // 128)]
            for kc in range(nk):
                sc.op("pe", lambda e, j=j, tg=tg, kc=kc, b=b: e.matmul(
                    P.bank[b][:, 0:tgw], lhsT=wt[:, kc, j * 128:(j + 1) * 128],
                    rhs=uT[:, kc, tg * tgw:(tg + 1) * tgw], start=(kc == 0), stop=(kc == nk - 1)),
                    reads=rd, writes=[P.bankh[b]] if kc == 0 else (), wadd=() if kc == 0 else [P.bankh[b]])
            evac(j, tg, P.bank[b][:, 0:tgw], P.bankh[b])


def proj_tm(P, wt, wth, ncols, uT, uTh, banks, evac, nk=KC, ntiles=NT):
    sc = P.sc
    for i in range(ntiles):
        for c0 in range(0, ncols, 512):
            cw = min(512, ncols - c0)
            b = banks.next()
            for kc in range(nk):
                sc.op("pe", lambda e, i=i, kc=kc, b=b, c0=c0, cw=cw: e.matmul(
                    P.bank[b][:, 0:cw], lhsT=uT[:, kc, i * 128:(i + 1) * 128],
                    rhs=wt[:, kc, c0:c0 + cw], start=(kc == 0), stop=(kc == nk - 1)),
                    reads=[wth, uTh[i]], writes=[P.bankh[b]] if kc == 0 else (),
                    wadd=() if kc == 0 else [P.bankh[b]])
            evac(i, c0, cw, P.bank[b][:, 0:cw], P.bankh[b])


def sb_attention(P, g, qT, kT, v, hq, hk, hv, osT, osTh):
    nc, sc = P.nc, P.sc
    sb = lambda n, s, d: g.enter_context(nc.sbuf_tensor("sb_" + n, s, d))
    scale = 0.125
    ones = sb("ones", [128, S], BF16)
    h_ones = H()
    sc.op("pool", lambda e: e.memset(ones[:], 1.0), writes=[h_ones])
    negm = sb("negm", [128, 128], BF16)
    h_m = H()
    sc.op("sp", lambda e: e.dma_start(out=negm[:], in_=P.cdram["m_neg"]), writes=[h_m], dma=True)
    NB = 2
    beta = [sb("beta%d" % i, [128, S], BF16) for i in range(NB)]
    omb = [sb("omb%d" % i, [128, S], F32) for i in range(NB)]
    Q = [sb("Q%d" % i, [128, S], BF16) for i in range(NB)]
    a = [sb("a%d" % i, [128, S], BF16) for i in range(NB)]
    aT = [sb("aT%d" % i, [128, NT, 128], BF16) for i in range(NB)]
    otm = [sb("otm%d" % i, [128, 512], BF16) for i in range(NB)]
    h_beta = [H() for _ in range(NB)]
    h_omb = [H() for _ in range(NB)]
    h_Q = [H() for _ in range(NB)]
    h_a = [H() for _ in range(NB)]
    h_aT = [H() for _ in range(NB)]
    h_otm = [H() for _ in range(NB)]
    for r in range(NB):
        sc.op("pool", lambda e, r=r: e.memset(Q[r][:], 1.0), writes=[h_Q[r]])
    its = [(tb, h) for tb in range(NT) for h in range(8)]

    def zbank(n, npc, p):
        return p + (2 * (n % 2) if npc <= 2 else 0)

    def stage1(n):
        tb, h = its[n]
        r = n % NB
        L = (tb + 1) * 128
        npc = (L + 511) // 512
        ch, p0 = h // 2, (h % 2) * 64
        for p in range(npc):
            n_ = min(512, L - p * 512)
            b = zbank(n, npc, p)
            last = (p == npc - 1)
            sc.op("pe", lambda e, p=p, n_=n_, b=b, last=last: e.matmul(
                P.bank[b][:, 0:n_], lhsT=qT[p0:p0 + 64, ch, tb * 128:(tb + 1) * 128],
                rhs=kT[p0:p0 + 64, ch, p * 512:p * 512 + n_], start=True, stop=not last),
                reads=[hq[tb]] + [hk[i] for i in range(p * 4, p * 4 + n_ // 128)], writes=[P.bankh[b]])
            if last:
                sc.op("pe", lambda e, n_=n_, b=b: e.matmul(
                    P.bank[b][:, n_ - 128:n_], lhsT=P.ident[:], rhs=negm[:], start=False, stop=True),
                    reads=[h_m, P.h_const], wadd=[P.bankh[b]])
        for p in range(npc):
            n_ = min(512, L - p * 512)
            b = zbank(n, npc, p)
            sc.op("act", lambda e, p=p, n_=n_, b=b: e.activation(
                out=beta[r][:, p * 512:p * 512 + n_], in_=P.bank[b][:, 0:n_], func=AF.Sigmoid, scale=scale),
                reads=[P.bankh[b]], writes=[h_beta[r]] if p == 0 else (), wadd=() if p == 0 else [h_beta[r]])
            sc.op("act", lambda e, p=p, n_=n_, b=b: e.activation(
                out=omb[r][:, p * 512:p * 512 + n_], in_=P.bank[b][:, 0:n_], func=AF.Sigmoid, scale=-scale),
                reads=[P.bankh[b]], writes=[h_omb[r]] if p == 0 else (), wadd=() if p == 0 else [h_omb[r]])
    def stage1b(n):
        tb, h = its[n]
        r = n % NB
        L = (tb + 1) * 128
        sc.op("dve", lambda e: e.tensor_tensor_scan(
            out=Q[r][:, L - 2::-1], data0=omb[r][:, L - 1:0:-1], data1=ones[:, 0:L - 1],
            initial=1.0, op0=ALU.mult, op1=ALU.mult),
            reads=[h_omb[r], h_ones], writes=[h_Q[r]])
        sc.op("dve", lambda e: e.tensor_tensor(
            out=a[r][:, 0:L], in0=beta[r][:, 0:L], in1=Q[r][:, 0:L], op=ALU.mult),
            reads=[h_beta[r], h_Q[r]], writes=[h_a[r]])

    def stage2(n):
        tb, h = its[n]
        r = n % NB
        for kb0 in range(0, tb + 1, 8):
            nb = min(8, tb + 1 - kb0)
            pb = 4 + (kb0 // 8)
            pv = P.bank[pb][:].bitcast(BF16)
            for kb in range(kb0, kb0 + nb):
                sc.op("pe", lambda e, kb=kb, kb0=kb0, pv=pv: e.transpose(
                    out=pv[:, (kb - kb0) * 128:(kb - kb0 + 1) * 128], in_=a[r][:, kb * 128:(kb + 1) * 128],
                    identity=P.ident[:]),
                    reads=[h_a[r], P.h_const], writes=[P.bankh[pb]] if kb == kb0 else (),
                    wadd=() if kb == kb0 else [P.bankh[pb]])
            sc.op("act", lambda e, kb0=kb0, nb=nb, pv=pv: e.copy(
                out=aT[r][:, kb0:kb0 + nb, :], in_=pv[:, 0:nb * 128].rearrange("p (c t) -> p c t", c=nb)),
                reads=[P.bankh[pb]], writes=[h_aT[r]] if kb0 == 0 else (), wadd=() if kb0 == 0 else [h_aT[r]])
    def stage2b(n):
        tb, h = its[n]
        r = n % NB
        for kb in range(tb + 1):
            sc.op("pe", lambda e, kb=kb: e.matmul(
                P.bank[6][:, h * 64:(h + 1) * 64], lhsT=aT[r][:, kb, :], rhs=v[:, kb, h * 64:(h + 1) * 64],
                start=(kb == 0), stop=(kb == tb)),
                reads=[h_aT[r], hv[kb]], writes=[P.bankh[6]] if (kb == 0 and h == 0) else (),
                wadd=() if (kb == 0 and h == 0) else [P.bankh[6]])
        if h == 7:
            ro = tb % NB
            sc.op("dve", lambda e: e.tensor_copy(out=otm[ro][:], in_=P.bank[6][:, :]),
                  reads=[P.bankh[6]], writes=[h_otm[ro]])
            pv = P.bank[7][:].bitcast(BF16)
            for c in range(4):
                sc.op("pe", lambda e, c=c, pv=pv: e.transpose(
                    out=pv[:, c * 128:(c + 1) * 128], in_=otm[ro][:, c * 128:(c + 1) * 128], identity=P.ident[:]),
                    reads=[h_otm[ro], P.h_const], writes=[P.bankh[7]] if c == 0 else (),
                    wadd=() if c == 0 else [P.bankh[7]])
            sc.op("dve", lambda e, pv=pv: e.tensor_copy(
                out=osT[:, :, tb * 128:(tb + 1) * 128], in_=pv[:, 0:512].rearrange("p (c t) -> p c t", c=4)),
                reads=[P.bankh[7]], writes=[osTh[tb]])

    N = len(its)
    for n in range(N + 3):
        if n < N:
            stage1(n)
        if 1 <= n <= N:
            stage1b(n - 1)
        if 2 <= n <= N + 1:
            stage2(n - 2)
        if n >= 3:
            stage2b(n - 3)


DSA_COLS = 1408


def build_dsa_weights(P):
    nc, sc = P.nc, P.sc
    w = P.w32["w_in"]
    P.dsaA = nc.dram_tensor("dsaA_bf", [D, DSA_COLS], BF16, kind="Internal").ap()
    P.wh["dsaA"] = H()
    blocks = [(0, 1536, 8), (512, 2048, 1), (576, 2048, 1), (640, 2112, 1), (704, 2112, 1),
              (768, 2304, 8), (1280, 2816, 1), (1344, 2816, 1)]
    for d0, s0, nh in blocks:
        cast_dma(P, lambda e, d0=d0, s0=s0, nh=nh: e.dma_start(
            out=P.dsaA[:, d0:d0 + nh * 64], in_=w[:, s0:s0 + nh * 64]), P.wh["dsaA"], False)


def dsa_project(P, g, uT, uTh, dqT, kT2, iqT, ikT2, vp, iwS, hdq, hk2, hiq, hik2, hvp, hiw):
    nc, sc = P.nc, P.sc
    sb = lambda n, s, d: g.enter_context(nc.sbuf_tensor("dp_" + n, s, d))
    cosT = sb("cos", [128, S], F32)
    sinT = sb("sin", [128, S], F32)
    h_cs = H()
    sc.op("sp", lambda e: e.dma_start(out=cosT[:], in_=P.cdram["cosT"]), wadd=[h_cs], dma=True)
    sc.op("sp", lambda e: e.dma_start(out=sinT[:], in_=P.cdram["sinT"]), wadd=[h_cs], dma=True)
    wv = sb("wv", [128, KC, 136], BF16)
    h_wv = H()
    load_w(P, "sp", wv[:, :, 0:128], "w_in", 2176, 128, h_wv)
    wsrc = P.wbf["w_in"].rearrange("(kc p) n -> p kc n", p=128)[:, :, 2880:2888]
    sc.op("sp", lambda e: e.dma_start(out=wv[:, :, 128:136], in_=wsrc), reads=[P.wh["w_in"]], wadd=[h_wv], dma=True)
    h_one = H()
    sc.op("pool", lambda e: e.memset(vp[:], 1.0), writes=hvp)
    banks = Banks(P, [4, 5])
    wsc = 0.125 * (8.0 ** -0.5)

    def evv(i, c0, cw, bap, bh):
        sc.op("act", lambda e, i=i, bap=bap: e.copy(
            out=vp[:, i, :].rearrange("p (g c) -> p g c", c=128)[:, :, 0:64],
            in_=bap[:, 0:128].rearrange("p (g c) -> p g c", c=64)),
            reads=[bh], writes=[hvp[i]])
        sc.op("act", lambda e, i=i, bap=bap: e.mul(out=iwS[:, i, :], in_=bap[:, 128:136], mul=wsc),
              reads=[bh], writes=[hiw[i]])
    proj_tm(P, wv, h_wv, 136, uT, uTh, banks, evv)
    wA = sb("wA", [128, KC, 512], BF16)
    h_wA = H()
    pmt = sb("pm", [128, 128], BF16)
    h_pm = H()
    sc.op("sp", lambda e: e.dma_start(out=pmt[:], in_=P.cdram["pm"]), writes=[h_pm], dma=True)
    xb = [sb("xb%d" % i, [128, 512], BF16) for i in range(2)]
    h_xb = [H(), H()]
    t1 = [sb("t1_%d" % i, [128, 512], F32) for i in range(2)]
    t2 = [sb("t2_%d" % i, [128, 512], F32) for i in range(2)]
    h_t1 = [H(), H()]
    h_t2 = [H(), H()]
    dests = ([(dqT, c, hdq) for c in range(4)] + [(kT2, 0, hk2), (kT2, 1, hk2)] +
             [(iqT, c, hiq) for c in range(4)] + [(ikT2, 0, hik2)])
    it = [0]
    for grp, (c0, ncols) in enumerate([(0, 512), (512, 512), (1024, 384)]):
        sc.op("sp", lambda e, c0=c0, ncols=ncols: e.dma_start(
            out=wA[:, :, 0:ncols], in_=P.dsaA.rearrange("(kc p) n -> p kc n", p=128)[:, :, c0:c0 + ncols]),
            reads=[P.wh["dsaA"]], writes=[h_wA], dma=True)
        for jj in range(ncols // 128):
            dst, dc, hd = dests[c0 // 128 + jj]
            for tg in range(4):
                r = it[0] % 2
                it[0] += 1
                bA, bB = (0, 1) if r == 0 else (2, 3)
                rd = [uTh[i] for i in range(tg * 4, tg * 4 + 4)]
                for kc in range(KC):
                    sc.op("pe", lambda e, bA=bA, kc=kc, jj=jj, tg=tg: e.matmul(
                        P.bank[bA][:, :], lhsT=wA[:, kc, jj * 128:(jj + 1) * 128],
                        rhs=uT[:, kc, tg * 512:(tg + 1) * 512], start=(kc == 0), stop=(kc == KC - 1)),
                        reads=rd + [h_wA], writes=[P.bankh[bA]] if kc == 0 else (),
                        wadd=() if kc == 0 else [P.bankh[bA]])
                sc.op("act", lambda e, bA=bA, r=r: e.copy(out=xb[r][:], in_=P.bank[bA][:, :]),
                      reads=[P.bankh[bA]], writes=[h_xb[r]])
                sc.op("pe", lambda e, bB=bB, r=r: e.matmul(
                    P.bank[bB][:, :], lhsT=pmt[:], rhs=xb[r][:], start=True, stop=True),
                    reads=[h_xb[r], h_pm], writes=[P.bankh[bB]])
                sc.op("dve", lambda e, bA=bA, r=r, tg=tg: e.tensor_tensor(
                    out=t1[r][:], in0=P.bank[bA][:, :], in1=cosT[:, tg * 512:(tg + 1) * 512], op=ALU.mult),
                    reads=[P.bankh[bA], h_cs, h_xb[r]], writes=[h_t1[r]])
                sc.op("dve", lambda e, bB=bB, r=r, tg=tg: e.tensor_tensor(
                    out=t2[r][:], in0=P.bank[bB][:, :], in1=sinT[:, tg * 512:(tg + 1) * 512], op=ALU.mult),
                    reads=[P.bankh[bB], h_cs], writes=[h_t2[r]])
                sc.op("pool", lambda e, r=r, dst=dst, dc=dc, tg=tg: e.tensor_tensor(
                    out=dst[:, dc, tg * 512:(tg + 1) * 512], in0=t1[r][:], in1=t2[r][:], op=ALU.add),
                    reads=[h_t1[r], h_t2[r]], writes=[hd[i] for i in range(tg * 4, tg * 4 + 4)])


def dsa_attention(P, g, dqT, kT2, iqT, ikT2, vp, iwS, hdq, hk2, hiq, hik2, hvp, hiw, odT, odTh):
    nc, sc = P.nc, P.sc
    sb = lambda n, s, d: g.enter_context(nc.sbuf_tensor("da_" + n, s, d))
    scale = 0.125
    KTOP = 256
    U32 = mybir.dt.uint32
    mca = sb("mca", [128, 128], F32)
    h_m = H()
    sc.op("sp", lambda e: e.dma_start(out=mca[:], in_=P.cdram["m_caus_add"]), writes=[h_m], dma=True)
    thrc = sb("thrc", [128, 1], F32)
    h_thrc = H()
    sc.op("pool", lambda e: e.memset(thrc[:], -1.0e29), writes=[h_thrc])
    io8 = sb("io8", [128, 8], F32)
    h_io8 = H()
    sc.op("sp", lambda e: e.dma_start(out=io8[:], in_=P.cdram["iota8"]), writes=[h_io8], dma=True)
    pw = sb("pw", [128, NBIS + 2], F32)
    h_pw = H()
    sc.op("sp", lambda e: e.dma_start(out=pw[:], in_=P.cdram["pw"]), writes=[h_pw], dma=True)
    NA = 3
    acc = [sb("acc%d" % i, [128, S], F32) for i in range(NA)]
    h_acc = [H() for _ in range(NA)]
    rmin = [sb("rmin%d" % i, [128, 1], F32) for i in range(NA)]
    h_rmin = [H() for _ in range(NA)]
    junk = sb("junk", [128, S], F32)
    wsel = sb("wsel", [128, S], F32)
    h_junk, h_wsel = H(), H()
    NTH = 4
    th = [sb("th%d" % i, [128, 512], F32) for i in range(2)] + [P.ntmp.gbc[:, 0:512], P.ntmp.gbc[:, 512:1024]]
    h_th = [H() for _ in range(NTH)]
    mask = sb("mask", [128, S], BF16)
    h_mask = H()
    maskT = [sb("maskT%d" % i, [128, NT, 128], BF16) for i in range(2)]
    h_maskT = [H(), H()]
    NE = 3
    pbuf = [sb("p%d" % i, [128, 512], BF16) for i in range(NE)]
    h_p = [H() for _ in range(NE)]
    rec = [sb("rec%d" % i, [128, 512], F32) for i in range(2)]
    h_rec = [H(), H()]

    class Chain:
        pass
    chains = []
    for ci in range(2):
        c = Chain()
        c.j16 = P.ntmp.xt[ci][:].bitcast(BF16)
        c.dl = sb("dl%d" % ci, [128, NBIS + 2], F32)
        c.bs = sb("bs%d" % ci, [128, 8], F32)
        c.Mh = sb("Mh%d" % ci, [128, NBIS + 1], F32)
        c.Uh = sb("Uh%d" % ci, [128, NBIS], F32)
        c.cand = sb("cand%d" % ci, [128, NBIS], F32)
        c.candh = sb("candh%d" % ci, [128, NBIS], F32)
        c.Vh = sb("Vh%d" % ci, [128, NBIS], F32)
        c.fs = sb("fs%d" % ci, [128, 8], F32)
        c.m8 = sb("m8_%d" % ci, [128, 8], F32)
        c.m8b = sb("m8b%d" % ci, [128, 8], F32)
        c.t8 = sb("t8_%d" % ci, [128, 8], F32)
        for nm in ("j16", "dl", "rng", "Mh", "Uh", "cand", "candh", "Vh", "fs", "m8", "m8b", "t8", "cnt", "lo", "tmp"):
            setattr(c, "h_" + nm, H())
        chains.append(c)
    cnt = {"th": 0, "ep": 0, "lb": 0}

    def part_I(tb):
        L = (tb + 1) * 128
        npc = (L + 511) // 512
        ac, h_ac = acc[tb % NA], h_acc[tb % NA]
        thunks = []
        for h in range(8):
            def f(h=h):
                ch, p0 = h // 2, (h % 2) * 64
                for p in range(npc):
                    n = min(512, L - p * 512)
                    b = p % 2
                    r = cnt["th"] % NTH
                    cnt["th"] += 1
                    sc.op("pe", lambda e, p=p, n=n, b=b: e.matmul(
                        P.bank[b][:, 0:n], lhsT=iqT[p0:p0 + 64, ch, tb * 128:(tb + 1) * 128],
                        rhs=ikT2[p0:p0 + 64, 0, p * 512:p * 512 + n], start=True, stop=True),
                        reads=[hiq[tb]] + [hik2[i] for i in range(p * 4, p * 4 + n // 128)], writes=[P.bankh[b]])
                    if h == 0:
                        sc.op("act", lambda e, n=n, b=b, r=r: e.activation(
                            out=th[r][:, 0:n], in_=P.bank[b][:, 0:n], func=AF.Relu),
                            reads=[P.bankh[b]], writes=[h_th[r]])
                        sc.op("pool", lambda e, p=p, n=n, r=r: e.tensor_scalar(
                            out=ac[:, p * 512:p * 512 + n], in0=th[r][:, 0:n], scalar1=iwS[:, tb, h:h + 1],
                            scalar2=0.0, op0=ALU.mult, op1=ALU.add),
                            reads=[h_th[r], hiw[tb]], writes=[h_ac] if p == 0 else (), wadd=() if p == 0 else [h_ac])
                    else:
                        sc.op("act", lambda e, n=n, b=b, r=r: e.activation(
                            out=th[r][:, 0:n], in_=P.bank[b][:, 0:n], func=AF.Relu),
                            reads=[P.bankh[b]], writes=[h_th[r]])
                        sc.op("pool", lambda e, n=n, r=r: e.tensor_scalar(
                            out=th[r][:, 0:n], in0=th[r][:, 0:n], scalar1=iwS[:, tb, h:h + 1], scalar2=0.0,
                            op0=ALU.mult, op1=ALU.add),
                            reads=[hiw[tb]], writes=[h_th[r]])
                        sc.op("pool", lambda e, p=p, n=n, r=r: e.tensor_tensor(
                            out=ac[:, p * 512:p * 512 + n], in0=ac[:, p * 512:p * 512 + n], in1=th[r][:, 0:n],
                            op=ALU.add),
                            reads=[h_th[r]], writes=[h_ac])
                if h == 7:
                    if tb >= 2:
                        sc.op("dve", lambda e: e.tensor_reduce(
                            out=rmin[tb % NA][:], in_=ac[:, 0:L], axis=AX.X, op=ALU.min),
                            reads=[h_ac], writes=[h_rmin[tb % NA]])
                    sc.op("pool", lambda e: e.tensor_tensor(
                        out=ac[:, L - 128:L], in0=ac[:, L - 128:L], in1=mca[:], op=ALU.add),
                        reads=[h_m], writes=[h_ac])
            thunks.append(f)
        return thunks

    def part_T(tb):
        L = (tb + 1) * 128
        mT, h_mT = maskT[tb % 2], h_maskT[tb % 2]
        ac, h_ac = acc[tb % NA], h_acc[tb % NA]
        c = chains[tb % 2]
        ops = []
        A = ops.append
        KB = NBIS if L > 1024 else (NBIS - 1 if L > 512 else NBIS - 2)
        if tb >= 2:
            rm, h_rm = rmin[tb % NA], h_rmin[tb % NA]
            lo, cn = c.bs[:, 0:1], c.bs[:, 4:5]
            A(lambda: sc.op("dve", lambda e: e.max(out=c.m8[:], in_=ac[:, 0:L]), reads=[h_ac], writes=[c.h_m8]))
            A(lambda: sc.op("dve", lambda e: e.tensor_tensor(out=c.bs[:, 1:2], in0=c.m8[:, 0:1], in1=rm[:],
                                                             op=ALU.subtract),
                            reads=[c.h_m8, h_rm], writes=[c.h_rng]))
            A(lambda: sc.op("dve", lambda e: e.tensor_scalar(out=c.dl[:], in0=pw[:], scalar1=c.bs[:, 1:2],
                                                             scalar2=None, op0=ALU.mult),
                            reads=[h_pw, c.h_rng], writes=[c.h_dl]))
            A(lambda: sc.op("dve", lambda e: e.tensor_tensor(out=c.Mh[:, 0:1], in0=rm[:], in1=c.dl[:, 0:1],
                                                             op=ALU.add),
                            reads=[h_rm, c.h_dl], writes=[c.h_Mh]))
            A(lambda: sc.op("pool", lambda e: e.memset(c.cand[:], NEG), writes=[c.h_cand]))
            A(lambda: sc.op("pool", lambda e: e.memset(c.candh[:], -NEG), writes=[c.h_candh]))
            for k in range(KB):
                mid = c.Mh[:, k:k + 1]
                A(lambda mid=mid: sc.op("dve", lambda e: e.tensor_scalar(
                    out=c.j16[:, 0:L], in0=ac[:, 0:L], scalar1=mid, scalar2=None, op0=ALU.is_ge, op1=ALU.add,
                    accum_out=cn),
                    reads=[h_ac, c.h_Mh], writes=[c.h_j16, c.h_cnt]))
                A(lambda k=k: sc.op("dve", lambda e: e.scalar_tensor_tensor(
                    out=c.Uh[:, k:k + 1], in0=cn, scalar=float(KTOP), in1=c.dl[:, k:k + 1],
                    op0=ALU.is_ge, op1=ALU.mult),
                    reads=[c.h_cnt, c.h_dl], writes=[c.h_Uh]))
                A(lambda k=k, mid=mid: sc.op("dve", lambda e: e.scalar_tensor_tensor(
                    out=c.Mh[:, k + 1:k + 2], in0=c.Uh[:, k:k + 1], scalar=c.dl[:, k + 1:k + 2], in1=mid,
                    op0=ALU.subtract, op1=ALU.add),
                    reads=[c.h_Uh, c.h_dl], writes=[c.h_Mh]))
            nsplit = 6 + 3 * (KB // 2)
            A(lambda: sc.op("dve", lambda e: e.copy_predicated(out=c.cand[:, 0:KB], mask=c.Uh[:, 0:KB].bitcast(U32),
                                                               data=c.Mh[:, 0:KB]),
                            reads=[c.h_Uh, c.h_Mh], writes=[c.h_cand]))
            A(lambda: sc.op("dve", lambda e: e.tensor_reduce(out=c.bs[:, 6:7], in_=c.cand[:], axis=AX.X, op=ALU.max),
                            reads=[c.h_cand], writes=[c.h_tmp]))
            A(lambda: sc.op("dve", lambda e: e.tensor_tensor(out=lo, in0=c.bs[:, 6:7], in1=rm[:], op=ALU.max),
                            reads=[c.h_tmp, h_rm], writes=[c.h_lo]))
            A(lambda: sc.op("dve", lambda e: e.tensor_scalar(out=c.Vh[:, 0:KB], in0=c.Uh[:, 0:KB], scalar1=0.0,
                                                             scalar2=None, op0=ALU.is_equal),
                            reads=[c.h_Uh], writes=[c.h_Vh]))
            A(lambda: sc.op("dve", lambda e: e.copy_predicated(out=c.candh[:, 0:KB], mask=c.Vh[:, 0:KB].bitcast(U32),
                                                               data=c.Mh[:, 0:KB]),
                            reads=[c.h_Vh, c.h_Mh], writes=[c.h_candh]))
            A(lambda: sc.op("dve", lambda e: e.tensor_reduce(out=c.fs[:, 0:1], in_=c.candh[:], axis=AX.X, op=ALU.min),
                            reads=[c.h_candh], writes=[c.h_fs]))
            A(lambda: sc.op("dve", lambda e: e.tensor_tensor(out=c.fs[:, 1:2], in0=c.fs[:, 0:1], in1=c.m8[:, 0:1],
                                                             op=ALU.min),
                            reads=[c.h_fs, c.h_m8], writes=[c.h_fs]))
            A(lambda: sc.op("pool", lambda e: e.memset(wsel[:, 0:L], NEG), writes=[h_wsel]))
            A(lambda: sc.op("dve", lambda e: e.tensor_scalar(
                out=junk[:, 0:L], in0=ac[:, 0:L], scalar1=c.fs[:, 1:2], scalar2=None, op0=ALU.is_lt, op1=ALU.add,
                accum_out=c.fs[:, 2:3]),
                reads=[h_ac, c.h_fs], writes=[h_junk, c.h_fs]))
            A(lambda: sc.op("dve", lambda e: e.copy_predicated(
                out=wsel[:, 0:L], mask=junk[:, 0:L].bitcast(U32), data=ac[:, 0:L]),
                reads=[h_junk, h_ac], writes=[h_wsel]))
            A(lambda: sc.op("dve", lambda e: e.max(out=c.m8b[:], in_=wsel[:, 0:L]), reads=[h_wsel], writes=[c.h_m8b]))
            A(lambda: sc.op("dve", lambda e: e.tensor_scalar(out=c.fs[:, 3:4], in0=c.fs[:, 2:3],
                                                             scalar1=float(KTOP - 1 - L), scalar2=None, op0=ALU.add),
                            reads=[c.h_fs], writes=[c.h_fs]))
            A(lambda: sc.op("dve", lambda e: e.scalar_tensor_tensor(
                out=c.t8[:], in0=io8[:], scalar=c.fs[:, 3:4], in1=c.m8b[:], op0=ALU.is_equal, op1=ALU.mult,
                accum_out=c.fs[:, 4:5]),
                reads=[h_io8, c.h_fs, c.h_m8b], writes=[c.h_t8, c.h_fs]))
            A(lambda: sc.op("dve", lambda e: e.tensor_scalar(out=c.fs[:, 5:6], in0=c.fs[:, 3:4], scalar1=7.5,
                                                             scalar2=None, op0=ALU.is_gt),
                            reads=[c.h_fs], writes=[c.h_fs]))
            A(lambda: sc.op("dve", lambda e: e.copy_predicated(out=c.fs[:, 4:5], mask=c.fs[:, 5:6].bitcast(U32),
                                                               data=lo),
                            reads=[c.h_fs, c.h_lo], writes=[c.h_fs]))
            thr, hthr = c.fs[:, 4:5], c.h_fs
        else:
            nsplit = 0
            thr, hthr = thrc[:, 0:1], h_thrc
        A(lambda: sc.op("dve", lambda e: e.tensor_scalar(
            out=mask[:, 0:L], in0=ac[:, 0:L], scalar1=thr, scalar2=None, op0=ALU.is_lt),
            reads=[h_ac, hthr], writes=[h_mask]))
        for kb0 in range(0, tb + 1, 8):
            nb = min(8, tb + 1 - kb0)
            pb = kb0 // 8
            pv = P.bank[pb][:].bitcast(BF16)
            def grp(kb0=kb0, nb=nb, pv=pv, pb=pb):
                for kb in range(kb0, kb0 + nb):
                    sc.op("pe", lambda e, kb=kb: e.transpose(
                        out=pv[:, (kb - kb0) * 128:(kb - kb0 + 1) * 128], in_=mask[:, kb * 128:(kb + 1) * 128],
                        identity=P.ident[:]),
                        reads=[h_mask, P.h_const], writes=[P.bankh[pb]] if kb == kb0 else (),
                        wadd=() if kb == kb0 else [P.bankh[pb]])
                sc.op("act", lambda e: e.copy(
                    out=mT[:, kb0:kb0 + nb, :], in_=pv[:, 0:nb * 128].rearrange("p (c t) -> p c t", c=nb)),
                    reads=[P.bankh[pb]], writes=[h_mT] if kb0 == 0 else (), wadd=() if kb0 == 0 else [h_mT])
            A(grp)
        return ops[:nsplit], ops[nsplit:]

    def part_A(tb):
        mT, h_mT = maskT[tb % 2], h_maskT[tb % 2]
        thunks = []
        for gk in range(2):
            ob = 6 + gk
            for kb in range(tb + 1):
                def f(gk=gk, ob=ob, kb=kb):
                    sset = cnt["lb"] % 2
                    cnt["lb"] += 1
                    bxy = (2, 3) if sset == 0 else (4, 5)
                    r = cnt["ep"] % NE
                    cnt["ep"] += 1
                    for half in range(2):
                        p0 = half * 64
                        b = bxy[half]
                        sc.op("pe", lambda e, p0=p0, b=b: e.matmul(
                            P.bank[b][:, 0:256],
                            lhsT=kT2[p0:p0 + 64, gk, kb * 128:(kb + 1) * 128],
                            rhs=dqT[p0:p0 + 64, 2 * gk:2 * gk + 2, tb * 128:(tb + 1) * 128],
                            start=True, stop=False),
                            reads=[hk2[kb], hdq[tb]], writes=[P.bankh[b]])
                    for half in range(2):
                        b = bxy[half]
                        sc.op("pe", lambda e, b=b: e.matmul(
                            P.bank[b][:, 0:256], lhsT=P.nident[:],
                            rhs=mT[:, kb, :].unsqueeze(1).to_broadcast([128, 2, 128]), start=False, stop=True),
                            reads=[h_mT, P.h_const], wadd=[P.bankh[b]])
                    for half in range(2):
                        b = bxy[half]
                        sc.op("act", lambda e, b=b, half=half: e.activation(
                            out=pbuf[r][:, half * 256:(half + 1) * 256], in_=P.bank[b][:, 0:256], func=AF.Exp,
                            scale=scale),
                            reads=[P.bankh[b]], writes=[h_p[r]] if half == 0 else (), wadd=() if half == 0 else [h_p[r]])
                    sc.op("pe", lambda e: e.matmul(
                        P.bank[ob][:, :], lhsT=vp[:, kb, gk * 128:(gk + 1) * 128], rhs=pbuf[r][:, :],
                        start=(kb == 0), stop=(kb == tb)),
                        reads=[h_p[r], hvp[kb]], writes=[P.bankh[ob]] if kb == 0 else (),
                        wadd=() if kb == 0 else [P.bankh[ob]])
                thunks.append(f)

        def fin():
            first = True
            for gk in range(2):
                ob = 6 + gk
                sc.op("act", lambda e, ob=ob, gk=gk: e.activation(
                    out=rec[gk][64:128, :], in_=P.bank[ob][64:128, :], func=AF.Ln),
                    reads=[P.bankh[ob]], writes=[h_rec[gk]])
                sc.op("act", lambda e, gk=gk: e.activation(
                    out=rec[gk][64:128, :], in_=rec[gk][64:128, :], func=AF.Exp, scale=-1.0),
                    writes=[h_rec[gk]])
                for blk in range(4):
                    half, cidx = blk // 2, blk % 2
                    sc.op("dve", lambda e, ob=ob, gk=gk, blk=blk, half=half, cidx=cidx: e.tensor_tensor(
                        out=odT[half * 64:(half + 1) * 64, 2 * gk + cidx, tb * 128:(tb + 1) * 128],
                        in0=P.bank[ob][0:64, blk * 128:(blk + 1) * 128],
                        in1=rec[gk][64:128, blk * 128:(blk + 1) * 128], op=ALU.mult),
                        reads=[P.bankh[ob], h_rec[gk]], writes=[odTh[tb]] if first else (),
                        wadd=() if first else [odTh[tb]])
                    first = False
        thunks.append(fin)
        return thunks

    def run(l):
        for f in l:
            f()

    def merge(lists):
        pos = [0] * len(lists)
        while True:
            best, bi = None, -1
            for i, l in enumerate(lists):
                if pos[i] < len(l):
                    key = (pos[i] + 0.5) / len(l)
                    if best is None or key < best:
                        best, bi = key, i
            if bi < 0:
                break
            lists[bi][pos[bi]]()
            pos[bi] += 1

    def zip2(x, y):
        out = []
        nx, ny = len(x), len(y)
        ix = iy = 0
        while ix < nx or iy < ny:
            if ix < nx and (iy >= ny or ix * ny <= iy * nx):
                out.append(x[ix])
                ix += 1
            else:
                out.append(y[iy])
                iy += 1
        return out

    halves = {}

    def T1(tb):
        halves[tb] = part_T(tb)
        return halves[tb][0]

    def T2(tb):
        if tb not in halves:
            halves[tb] = part_T(tb)
        return halves[tb][1]

    run(part_I(0))
    run(T1(0))
    run(T2(0))
    run(part_I(1))
    run(part_I(2))
    for tb in range(NT):
        lists = [part_A(tb)]
        x = T2(tb + 1) if tb + 1 < NT else []
        if tb + 1 < NT and tb + 1 < 2:
            x = T1(tb + 1) + x
        y = T1(tb + 2) if tb + 2 < NT else []
        tl = (x + y) if _os.environ.get("NOZIP") else zip2(x, y)
        if tl:
            lists.append(tl)
        if tb + 3 < NT:
            lists.append(part_I(tb + 3))
        merge(lists)


def merge_phase(P, g, uT, uTh, osT, osTh, odT, odTh):
    nc, sc = P.nc, P.sc
    sb = lambda n, s, d: g.enter_context(nc.sbuf_tensor("mg_" + n, s, d))
    wg = sb("wg", [128, KC, 2 * D], BF16)
    wbs = sb("wbs", [128, 4, D], BF16)
    wbd = sb("wbd", [128, 4, D], BF16)
    bg = sb("bg", [128, 16], F32)
    h_wg = [H() for _ in range(4)]
    h_wbs, h_wbd, h_bg = H(), H(), H()
    sc.op("sp", lambda e: e.dma_start(out=bg[:], in_=P.b_gate), writes=[h_bg], dma=True)
    for q in (0, 2, 1, 3):
        load_w(P, "sp", wg[:, :, q * 512:(q + 1) * 512], "w_gate", q * 512, 512, h_wg[q])
        if q == 2:
            load_w(P, "sp", wbs[:], "w_branch_sb", 0, D, h_wbs)
            load_w(P, "sp", wbd[:], "w_branch_dsa", 0, D, h_wbd)
    g1 = [sb("g1_%d" % i, [128, 512], F32) for i in range(2)]
    g2 = [sb("g2_%d" % i, [128, 512], F32) for i in range(2)]
    m1 = [sb("m1_%d" % i, [128, 512], F32) for i in range(2)]
    m2 = [sb("m2_%d" % i, [128, 512], F32) for i in range(2)]
    h_g1, h_g2, h_m1, h_m2 = [H(), H()], [H(), H()], [H(), H()], [H(), H()]
    tmp = sb("tmp", [128, KC, 512], BF16)
    h_tmp = H()
    it = 0
    for tg in range(4):
        tiles = list(range(tg * 4, tg * 4 + 4))
        for c in range(KC):
            r = it % 2
            it += 1
            bs = [0, 1, 2, 3] if r == 0 else [4, 5, 6, 7]
            specs = [(bs[0], wg, c * 128, uT, KC, [h_wg[c // 4]] + [uTh[i] for i in tiles]),
                     (bs[1], wg, D + c * 128, uT, KC, [h_wg[2 + c // 4]] + [uTh[i] for i in tiles]),
                     (bs[2], wbs, c * 128, osT, 4, [h_wbs] + [osTh[i] for i in tiles]),
                     (bs[3], wbd, c * 128, odT, 4, [h_wbd] + [odTh[i] for i in tiles])]
            for (bk, wt, c0, act, nk, rd) in specs:
                for kc in range(nk):
                    sc.op("pe", lambda e, bk=bk, wt=wt, c0=c0, act=act, kc=kc, nk=nk, tg=tg: e.matmul(
                        P.bank[bk][:, :], lhsT=wt[:, kc, c0:c0 + 128], rhs=act[:, kc, tg * 512:(tg + 1) * 512],
                        start=(kc == 0), stop=(kc == nk - 1)),
                        reads=rd, writes=[P.bankh[bk]] if kc == 0 else (), wadd=() if kc == 0 else [P.bankh[bk]])
            sc.op("act", lambda e, r=r, bk=bs[0], c=c: e.activation(
                out=g1[r][:], in_=P.bank[bk][:, :], func=AF.Sigmoid, bias=bg[:, c:c + 1]),
                reads=[P.bankh[bs[0]], h_bg], writes=[h_g1[r]])
            sc.op("act", lambda e, r=r, bk=bs[1], c=c: e.activation(
                out=g2[r][:], in_=P.bank[bk][:, :], func=AF.Sigmoid, bias=bg[:, 8 + c:9 + c]),
                reads=[P.bankh[bs[1]], h_bg], writes=[h_g2[r]])
            sc.op("dve", lambda e, r=r, bk=bs[2]: e.tensor_tensor(
                out=m1[r][:], in0=P.bank[bk][:, :], in1=g1[r][:], op=ALU.mult),
                reads=[P.bankh[bs[2]], h_g1[r]], writes=[h_m1[r]])
            sc.op("dve", lambda e, r=r, bk=bs[3]: e.tensor_tensor(
                out=m2[r][:], in0=P.bank[bk][:, :], in1=g2[r][:], op=ALU.mult),
                reads=[P.bankh[bs[3]], h_g2[r]], writes=[h_m2[r]])
            sc.op("pool", lambda e, r=r, c=c: e.tensor_tensor(
                out=tmp[:, c, :], in0=m1[r][:], in1=m2[r][:], op=ALU.add),
                reads=[h_m1[r], h_m2[r]], writes=[h_tmp] if c == 0 else (), wadd=() if c == 0 else [h_tmp])
        sc.op("dve", lambda e, tg=tg: e.tensor_copy(out=uT[:, :, tg * 512:(tg + 1) * 512], in_=tmp[:]),
              reads=[h_tmp], writes=[uTh[i] for i in tiles])


def wout_phase(P, g, mT, mTh, hres, hh):
    nc, sc = P.nc, P.sc
    sb = lambda n, s, d: g.enter_context(nc.sbuf_tensor("wo_" + n, s, d))
    wo = sb("wo", [128, KC, D], BF16)
    h_wo = H()
    load_w(P, "sp", wo[:], "w_out", 0, D, h_wo)
    xt = P.ntmp.xt
    h_xt = P.ntmp.h_xt
    banks = Banks(P, [0, 1, 2, 3])
    for i in range(NT):
        j = i % 2
        sc.op("sp", lambda e, i=i, j=j: e.dma_start(out=xt[j][:], in_=P.x[i * 128:(i + 1) * 128, :]),
              writes=[h_xt[j]], dma=True)
        for c0 in (0, 512):
            b = banks.next()
            for kc in range(KC):
                sc.op("pe", lambda e, b=b, i=i, kc=kc, c0=c0: e.matmul(
                    P.bank[b][:, :], lhsT=mT[:, kc, i * 128:(i + 1) * 128], rhs=wo[:, kc, c0:c0 + 512],
                    start=(kc == 0), stop=(kc == KC - 1)),
                    reads=[mTh[i], h_wo], writes=[P.bankh[b]] if kc == 0 else (), wadd=() if kc == 0 else [P.bankh[b]])
            sc.op("dve", lambda e, b=b, i=i, j=j, c0=c0: e.tensor_tensor(
                out=hres[:, i, c0:c0 + 512], in0=P.bank[b][:, :], in1=xt[j][:, c0:c0 + 512], op=ALU.add),
                reads=[P.bankh[b], h_xt[j]], writes=[hh[i]] if c0 == 0 else (), wadd=() if c0 == 0 else [hh[i]])


def cross_phase(P, g, uT, uTh, hres, hh, kcT, hkc, vc, hvc):
    nc, sc = P.nc, P.sc
    sb = lambda n, s, d: g.enter_context(nc.sbuf_tensor("cx_" + n, s, d))
    scale = 128.0 ** -0.5
    wq = sb("wq", [128, KC, 512], BF16)
    wco = sb("wco", [128, 4, D], BF16)
    h_wq, h_wco = H(), H()
    load_w(P, "sp", wq[:], "w_cq", 0, 512, h_wq)
    load_w(P, "sp", wco[:], "w_co", 0, D, h_wco)
    rmsnorm_T(P, None, "sbuf", "norm_cross", uT, uTh, src=hres, srch=hh, tag="n2")
    qcT = sb("qcT", [128, 4, S], BF16)
    hqc = [H() for _ in range(NT)]
    ocT = sb("ocT", [128, 4, S], BF16)
    hoc = [H() for _ in range(NT)]
    banks = Banks(P, [0, 1, 2, 3])
    kk = [0]

    def evq(j, tg, bap, bh):
        kk[0] += 1
        evac_copy(P, kk[0], qcT[:, j, tg * 512:(tg + 1) * 512], bap, [bh], [hqc[i] for i in range(tg * 4, tg * 4 + 4)])
    proj_fm(P, wq, h_wq, 512, uT, uTh, banks, evq)
    pT = [[sb("pT%d_%d" % (h, mb), [128, 512], BF16) for mb in range(2)] for h in range(4)]
    h_pT = [[H() for mb in range(2)] for h in range(4)]
    rden = sb("rden", [128, 4], F32)
    h_rden = H()
    otm = [sb("otm%d" % i, [128, 512], BF16) for i in range(2)]
    h_otm = [H(), H()]
    lbanks = Banks(P, [0, 1, 2, 3])
    for tg in range(4):
        for h in range(4):
            for mb in range(2):
                b = lbanks.next()
                sc.op("pe", lambda e, b=b, h=h, mb=mb, tg=tg: e.matmul(
                    P.bank[b][:, :], lhsT=kcT[:, h, mb * 128:(mb + 1) * 128], rhs=qcT[:, h, tg * 512:(tg + 1) * 512],
                    start=True, stop=True),
                    reads=[hkc[0]] + [hqc[i] for i in range(tg * 4, tg * 4 + 4)], writes=[P.bankh[b]])
                sc.op("act", lambda e, b=b, h=h, mb=mb: e.activation(
                    out=pT[h][mb][:], in_=P.bank[b][:, :], func=AF.Exp, scale=scale),
                    reads=[P.bankh[b]], writes=[h_pT[h][mb]])
        import os
        CXL = int(os.environ.get("CXL", "9"))
        if CXL < 1:
            continue
        for tt in range(4):
            i = tg * 4 + tt
            ro = i % 2
            for h in range(4):
                ob = 6 + h // 2
                oc = (h % 2) * 129
                for mb in range(2):
                    first = (h % 2 == 0 and mb == 0)
                    sc.op("pe", lambda e, ob=ob, oc=oc, h=h, mb=mb, tt=tt: e.matmul(
                        P.bank[ob][:, oc:oc + 129], lhsT=pT[h][mb][:, tt * 128:(tt + 1) * 128],
                        rhs=vc[:, mb, h * 129:(h + 1) * 129], start=(mb == 0), stop=(mb == 1)),
                        reads=[h_pT[h][mb], hvc[mb]], writes=[P.bankh[ob]] if first else (),
                        wadd=() if first else [P.bankh[ob]])
            if CXL < 2:
                continue
            for h in range(4):
                ob = 6 + h // 2
                oc = (h % 2) * 129
                sc.op("dve", lambda e, ob=ob, oc=oc, h=h: e.reciprocal(
                    out=rden[:, h:h + 1], in_=P.bank[ob][:, oc + 128:oc + 129]),
                    reads=[P.bankh[ob]], writes=[h_rden] if h == 0 else (), wadd=() if h == 0 else [h_rden])
            CXR = int(os.environ.get("CXR", "9"))
            for h in range(4):
                ob = 6 + h // 2
                oc = (h % 2) * 129
                if True:
                    sc.op("act", lambda e, ob=ob, oc=oc, h=h, ro=ro: e.activation(
                        out=otm[ro][:, h * 128:(h + 1) * 128], in_=P.bank[ob][:, oc:oc + 128], func=AF.Copy,
                        scale=rden[:, h:h + 1]),
                        reads=[P.bankh[ob], h_rden], writes=[h_otm[ro]] if h == 0 else (),
                        wadd=() if h == 0 else [h_otm[ro]])
                else:
                    sc.op("dve", lambda e, ob=ob, oc=oc, h=h, ro=ro: e.tensor_scalar(
                        out=otm[ro][:, h * 128:(h + 1) * 128], in0=P.bank[ob][:, oc:oc + 128],
                        scalar1=rden[:, h:h + 1], scalar2=None, op0=ALU.mult),
                        reads=[P.bankh[ob], h_rden], writes=[h_otm[ro]] if h == 0 else (),
                        wadd=() if h == 0 else [h_otm[ro]])
            if CXL < 3:
                continue
            b = 4 + (i % 2)
            pv = P.bank[b][:].bitcast(BF16)
            for c in range(4):
                sc.op("pe", lambda e, c=c, ro=ro, pv=pv: e.transpose(
                    out=pv[:, c * 128:(c + 1) * 128], in_=otm[ro][:, c * 128:(c + 1) * 128], identity=P.ident[:]),
                    reads=[h_otm[ro], P.h_const], writes=[P.bankh[b]] if c == 0 else (),
                    wadd=() if c == 0 else [P.bankh[b]])
            sc.op("act", lambda e, i=i, pv=pv: e.copy(
                out=ocT[:, :, i * 128:(i + 1) * 128], in_=pv[:, 0:512].rearrange("p (c t) -> p c t", c=4)),
                reads=[P.bankh[b]], writes=[hoc[i]])
    if P.stage == "H3":
        return
    for i in range(NT):
        for c0 in (0, 512):
            b = lbanks.next()
            for kc in range(4):
                sc.op("pe", lambda e, b=b, i=i, kc=kc, c0=c0: e.matmul(
                    P.bank[b][:, :], lhsT=ocT[:, kc, i * 128:(i + 1) * 128], rhs=wco[:, kc, c0:c0 + 512],
                    start=(kc == 0), stop=(kc == 3)),
                    reads=[hoc[i], h_wco], writes=[P.bankh[b]] if kc == 0 else (), wadd=() if kc == 0 else [P.bankh[b]])
            sc.op("dve", lambda e, b=b, i=i, c0=c0: e.tensor_tensor(
                out=hres[:, i, c0:c0 + 512], in0=P.bank[b][:, :], in1=hres[:, i, c0:c0 + 512], op=ALU.add),
                reads=[P.bankh[b]], writes=[hh[i]])


def mlp_phase(P, g, uT, uTh, hres, hh):
    nc, sc = P.nc, P.sc
    sb = lambda n, s, d: g.enter_context(nc.sbuf_tensor("ml_" + n, s, d))
    from contextlib import ExitStack
    rmsnorm_T(P, None, "sbuf", "norm_mlp", uT, uTh, src=hres, srch=hh, tag="n3")
    hid = sb("hid", [128, 32, 512], BF16)
    h_hid = [H() for _ in range(32)]
    wu = [sb("wu%d" % i, [128, KC, 512], BF16) for i in range(2)]
    h_wu = [H(), H()]
    wd = [sb("wd%d" % i, [128, 4, 512], BF16) for i in range(3)]
    h_wd = [H(), H(), H()]
    rl = [sb("rl%d" % i, [128, 512], F32) for i in range(2)]
    h_rl = [H(), H()]
    iu = 0
    idn = 0
    irl = 0
    ub = Banks(P, [4, 5, 6, 7])
    for tg in range(4):
        tiles = list(range(tg * 4, tg * 4 + 4))
        for cblk in range(8):
            r = iu % 2
            iu += 1
            load_w(P, "sp", wu[r][:], "w_up", cblk * 512, 512, h_wu[r])
            for j in range(4):
                b = ub.next()
                for kc in range(KC):
                    sc.op("pe", lambda e, b=b, r=r, kc=kc, j=j, tg=tg: e.matmul(
                        P.bank[b][:, :], lhsT=wu[r][:, kc, j * 128:(j + 1) * 128],
                        rhs=uT[:, kc, tg * 512:(tg + 1) * 512], start=(kc == 0), stop=(kc == KC - 1)),
                        reads=[h_wu[r]] + [uTh[i] for i in tiles], writes=[P.bankh[b]] if kc == 0 else (),
                        wadd=() if kc == 0 else [P.bankh[b]])
                q = irl % 2
                irl += 1
                sc.op("act", lambda e, b=b, q=q: e.activation(out=rl[q][:], in_=P.bank[b][:, :], func=AF.Relu),
                      reads=[P.bankh[b]], writes=[h_rl[q]])
                sc.op("pool", lambda e, q=q, cblk=cblk, j=j: e.tensor_tensor(
                    out=hid[:, cblk * 4 + j, :], in0=rl[q][:], in1=rl[q][:], op=ALU.mult),
                    reads=[h_rl[q]], writes=[h_hid[cblk * 4 + j]])
        for c0 in (0, 512):
            for rblk in range(8):
                r = idn % 3
                idn += 1
                load_w(P, "sp", wd[r][:], "w_down", c0, 512, h_wd[r], r0=rblk * 512, nk=4)
                for tt in range(4):
                    b = tt
                    for k4 in range(4):
                        kc = rblk * 4 + k4
                        first = (rblk == 0 and k4 == 0)
                        sc.op("pe", lambda e, b=b, r=r, k4=k4, kc=kc, tt=tt, first=first: e.matmul(
                            P.bank[b][:, :], lhsT=hid[:, kc, tt * 128:(tt + 1) * 128], rhs=wd[r][:, k4, :],
                            start=first, stop=(kc == 31)),
                            reads=[h_wd[r], h_hid[kc]], writes=[P.bankh[b]] if first else (),
                            wadd=() if first else [P.bankh[b]])
            for tt in range(4):
                i = tg * 4 + tt
                sc.op("dve", lambda e, tt=tt, i=i, c0=c0: e.tensor_tensor(
                    out=hres[:, i, c0:c0 + 512], in0=P.bank[tt][:, :], in1=hres[:, i, c0:c0 + 512], op=ALU.add),
                    reads=[P.bankh[tt]], writes=[hh[i]])


def final_phase(P, g, hres, hh):
    nc, sc = P.nc, P.sc
    T = P.ntmp
    gbc, junk, ot, st = T.gbc, T.junk, T.xt, T.st
    h_g, h_junk, h_ot, h_st = T.h_g, T.h_junk, T.h_xt, T.h_st
    sc.op("sp", lambda e: e.dma_start(out=gbc[:], in_=P.vec["norm_final"].to_broadcast([128, D])),
          writes=[h_g], dma=True)
    h_out = H()
    for i in range(NT):
        j = i % 2
        xin = hres[:, i, :]
        ss = st[:, 4 * i:4 * i + 1]
        ms = st[:, 4 * i + 1:4 * i + 2]
        sd = st[:, 4 * i + 2:4 * i + 3]
        rs = st[:, 4 * i + 3:4 * i + 4]
        sc.op("act", lambda e, xin=xin, ss=ss: e.activation(out=junk[:], in_=xin, func=AF.Square, accum_out=ss),
              reads=[hh[i]], writes=[h_junk, h_st[i]])
        sc.op("dve", lambda e, ss=ss, ms=ms: e.tensor_scalar(out=ms, in0=ss, scalar1=1.0 / D, scalar2=EPS,
                                                               op0=ALU.mult, op1=ALU.add),
              reads=[h_st[i]], writes=[h_st[i]])
        sc.op("act", lambda e, sd=sd, ms=ms: e.activation(out=sd, in_=ms, func=AF.Sqrt),
              reads=[h_st[i]], writes=[h_st[i]])
        sc.op("dve", lambda e, sd=sd, rs=rs: e.reciprocal(out=rs, in_=sd),
              reads=[h_st[i]], writes=[h_st[i]])
        sc.op("dve", lambda e, xin=xin, rs=rs, j=j: e.scalar_tensor_tensor(
            out=ot[j][:], in0=xin, scalar=rs, in1=gbc[:], op0=ALU.mult, op1=ALU.mult),
            reads=[hh[i], h_st[i], h_g], writes=[h_ot[j]])
        sc.op("sp", lambda e, i=i, j=j: e.dma_start(out=P.out[i * 128:(i + 1) * 128, :], in_=ot[j][:]),
              reads=[h_ot[j]], wadd=[h_out], dma=True)
    sc.op("sp", lambda e: e.nop(), reads=[h_out])


def phases(P, g):
    nc, sc = P.nc, P.sc
    sb = P.sb_global
    from contextlib import ExitStack
    cast_weights(P, ["w_in"], cols=(0, 1536), hname="w_in_sb")
    cast_weights(P, ["w_ckv"])
    cast_weights(P, ["w_in"], cols=(1536, D_IN))
    build_dsa_weights(P)
    P.rest_cast_done = False

    def cast_rest():
        if not P.rest_cast_done:
            cast_weights(P, [n for n, _, _ in W_SPECS if n not in ("w_in", "w_ckv")])
            P.rest_cast_done = True
    uT = sb("uT", [128, KC, S], BF16)
    uTh = [H("uT%d" % i) for i in range(NT)]
    P.ntmp = NormTmp(P, sb)
    kcT = sb("kcT", [128, 4, NMEM], BF16)
    hkc = [H()]
    vc = sb("vc", [128, 2, 4 * 129], BF16)
    hvc = [H(), H()]
    rmsnorm_T(P, None, "dram", "norm_mix", uT, uTh, src=P.x, tag="n1")
    if P.stage == "A":
        dbg_out(P, "uT", uT[:], [128, KC, S], BF16, uTh)
        return
    gmid = g.enter_context(ExitStack())
    sbm = lambda n, s, d: gmid.enter_context(nc.sbuf_tensor(n, s, d))
    osT = sbm("osT", [128, 4, S], BF16)
    osTh = [H() for _ in range(NT)]
    odT = sbm("odT", [128, 4, S], BF16)
    odTh = [H() for _ in range(NT)]
    if P.stage not in ("D", "E"):
      with ExitStack() as g2:
        sb2 = lambda n, s, d: g2.enter_context(nc.sbuf_tensor(n, s, d))
        qT = sb2("sbq", [128, 4, S], BF16)
        kT = sb2("sbk", [128, 4, S], BF16)
        v = sb2("sbv", [128, NT, 512], BF16)
        hq = [H() for _ in range(NT)]
        hk = [H() for _ in range(NT)]
        hv = [H() for _ in range(NT)]
        with ExitStack() as g3:
            wb = [g3.enter_context(nc.sbuf_tensor("wb%d" % i, [128, KC, 512], BF16)) for i in range(2)]
            wbh = [H(), H()]
            banks = Banks(P, [0, 1, 2, 3])
            kk = [0]
            for wi, (dst, hd) in enumerate([(qT, hq), (kT, hk)]):
                load_w(P, "sp", wb[wi % 2][:], "w_in", wi * 512, 512, wbh[wi % 2], hname="w_in_sb")

                def ev(j, tg, bap, bh, dst=dst, hd=hd):
                    kk[0] += 1
                    evac_copy(P, kk[0], dst[:, j, tg * 512:(tg + 1) * 512], bap, [bh],
                              [hd[i] for i in range(tg * 4, tg * 4 + 4)])
                proj_fm(P, wb[wi % 2], wbh[wi % 2], 512, uT, uTh, banks, ev)
            load_w(P, "sp", wb[0][:], "w_in", 1024, 512, wbh[0], hname="w_in_sb")

            def evv(i, c0, cw, bap, bh):
                kk[0] += 1
                evac_copy(P, kk[0], v[:, i, c0:c0 + cw], bap, [bh], [hv[i]])
            proj_tm(P, wb[0], wbh[0], 512, uT, uTh, banks, evv)
            memT = g3.enter_context(nc.sbuf_tensor("memT", [128, KC, NMEM], BF16))
            memTh = [H(), H()]
            wkv = g3.enter_context(nc.sbuf_tensor("wkv", [128, KC, D], BF16))
            h_wkv = H()
            load_w(P, "sp", wkv[:], "w_ckv", 0, D, h_wkv)
            rmsnorm_T(P, None, "dram", "norm_mem", memT, memTh, ntiles=2, src=P.mem, tag="nm")

            def evk(j, tg, bap, bh):
                kk[0] += 1
                evac_copy(P, kk[0], kcT[:, j, :], bap, [bh], (), wadd=[hkc[0]])
            proj_fm(P, wkv, h_wkv, 512, memT, memTh, banks, evk, ntok=NMEM)
            sc.op("pool", lambda e: e.memset(vc[:], 1.0), writes=hvc)

            def evmv(i, c0, cw, bap, bh):
                sc.op("act", lambda e, i=i, bap=bap: e.copy(
                    out=vc[:, i, :].rearrange("p (h c) -> p h c", c=129)[:, :, 0:128],
                    in_=bap[:, 0:512].rearrange("p (h c) -> p h c", c=128)),
                    reads=[bh], writes=[hvc[i]])
            proj_tm(P, wkv[:, :, 512:1024], h_wkv, 512, memT, memTh, banks, evmv, ntiles=2)
            sc.flush()
        if P.stage == "B":
            dbg_out(P, "qT", qT[:], [128, 4, S], BF16, hq)
            dbg_out(P, "kT", kT[:], [128, 4, S], BF16, hk)
            dbg_out(P, "v", v[:], [128, NT, 512], BF16, hv)
            sc.flush()
            return
        with ExitStack() as g3:
            sb_attention(P, g3, qT, kT, v, hq, hk, hv, osT, osTh)
            cast_rest()
            sc.flush()
    if P.stage == "C":
        dbg_out(P, "osT", osT[:], [128, 4, S], BF16, osTh)
        return
    cast_rest()
    with ExitStack() as g2:
        sb2 = lambda n, s, d: g2.enter_context(nc.sbuf_tensor(n, s, d))
        dqT = sb2("dqT", [128, 4, S], BF16)
        kT2 = sb2("kT2", [128, 2, S], BF16)
        iqT = sb2("iqT", [128, 4, S], BF16)
        ikT2 = sb2("ikT2", [128, 1, S], BF16)
        vp = sb2("vp", [128, NT, 256], BF16)
        iwS = sb2("iwS", [128, NT, 8], F32)
        hdq = [H() for _ in range(NT)]
        hk2 = [H() for _ in range(NT)]
        hiq = [H() for _ in range(NT)]
        hik2 = [H() for _ in range(NT)]
        hvp = [H() for _ in range(NT)]
        hiw = [H() for _ in range(NT)]
        with ExitStack() as g3:
            dsa_project(P, g3, uT, uTh, dqT, kT2, iqT, ikT2, vp, iwS, hdq, hk2, hiq, hik2, hvp, hiw)
            sc.flush()
        if P.stage == "D":
            dbg_out(P, "dqT", dqT[:], [128, 4, S], BF16, hdq)
            dbg_out(P, "kT2", kT2[:], [128, 2, S], BF16, hk2)
            dbg_out(P, "iqT", iqT[:], [128, 4, S], BF16, hiq)
            dbg_out(P, "ikT2", ikT2[:], [128, 1, S], BF16, hik2)
            dbg_out(P, "vp", vp[:], [128, NT, 256], BF16, hvp)
            dbg_out(P, "iwS", iwS[:], [128, NT, 8], F32, hiw)
            sc.flush()
            return
        with ExitStack() as g3:
            dsa_attention(P, g3, dqT, kT2, iqT, ikT2, vp, iwS, hdq, hk2, hiq, hik2, hvp, hiw, odT, odTh)
            sc.flush()
    if P.stage == "E":
        dbg_out(P, "odT", odT[:], [128, 4, S], BF16, odTh)
        sc.flush()
        gmid.close()
        return
    with ExitStack() as g2:
        merge_phase(P, g2, uT, uTh, osT, osTh, odT, odTh)
        sc.flush()
    gmid.close()
    if P.stage == "F":
        dbg_out(P, "mT", uT[:], [128, KC, S], BF16, uTh)
        return
    hres = sb("hres", [128, NT, D], F32)
    hh = [H() for _ in range(NT)]
    with ExitStack() as g2:
        wout_phase(P, g2, uT, uTh, hres, hh)
        if P.stage == "G":
            sc.flush()
            dbg_out(P, "h1", hres[:], [128, NT, D], F32, hh)
            sc.flush()
            return
        cross_phase(P, g2, uT, uTh, hres, hh, kcT, hkc, vc, hvc)
        sc.flush()
    if P.stage in ("H", "H1", "H2", "H3"):
        dbg_out(P, "h2", hres[:], [128, NT, D], F32, hh)
        return
    with ExitStack() as g2:
        mlp_phase(P, g2, uT, uTh, hres, hh)
        sc.flush()
    if P.stage == "I":
        dbg_out(P, "h3", hres[:], [128, NT, D], F32, hh)
        return
    with ExitStack() as g2:
        final_phase(P, g2, hres, hh)
        sc.flush()


def dbg_out(P, name, ap, shape, dt, hs):
    nc, sc = P.nc, P.sc
    o = nc.dram_tensor("dbg_" + name, list(shape), dt, kind="ExternalOutput").ap()
    P.dbg[name] = o
    hd = H()
    op = sc.op("sp", lambda e: e.dma_start(out=o, in_=ap), reads=list(hs), writes=[hd], dma=True)
    sc.op("sp", lambda e: e.nop(), reads=[hd])


def make_in_maps(inputs, ncores=8):
    cs = _consts()
    shared = {}
    for name, r, c in W_SPECS:
        shared[name] = np.ascontiguousarray(np.asarray(inputs[name], np.float32).reshape(r, c))
    for name in V_SPECS:
        shared[name] = np.ascontiguousarray(np.asarray(inputs[name], np.float32).reshape(1, D))
    shared["b_gate"] = np.ascontiguousarray(
        np.asarray(inputs["b_gate"], np.float32).reshape(16, 128).T)
    for k, v in cs.items():
        shared["c_" + k] = v
    x = np.asarray(inputs["x"], np.float32)
    mem = np.asarray(inputs["mem"], np.float32)
    maps = []
    for b in range(ncores):
        m = dict(shared)
        m["x"] = np.ascontiguousarray(x[b])
        m["mem"] = np.ascontiguousarray(mem[b])
        maps.append(m)
    return maps


_CACHE = {}


def kernel(**inputs):
    if "nc" not in _CACHE:
        _CACHE["nc"] = build("full")
    nc, P = _CACHE["nc"]
    maps = make_in_maps(inputs, 8)
    res = run_bass_kernel_spmd(nc, maps, core_ids=list(range(8)))
    out = np.stack([np.asarray(r["out"], np.float32) for r in res.results], axis=0)
    return out
```

```python
import numpy as np
import ml_dtypes
import concourse.bass as bass
import concourse.mybir as mybir
from concourse.bass_utils import run_bass_kernel_spmd

F32 = mybir.dt.float32
BF16 = mybir.dt.bfloat16
AF = mybir.ActivationFunctionType
ALU = mybir.AluOpType

S = 2048
D = 1024
NT = S // 128
KC = D // 128
NMEM = 256
DFF = 4096
D_IN = 2888
EPS = 1e-6
NEG = -1.0e30
NBIS = 14
AX = mybir.AxisListType
import os as _os
TK_ACT = 0


class H:
    __slots__ = ("writers", "readers", "name")

    def __init__(self, name=""):
        self.writers = []
        self.readers = []
        self.name = name


class Op:
    __slots__ = ("eng", "fn", "waits", "count", "needed", "is_dma", "dslot", "dval")


class Sched:
    ENG = ["pe", "act", "dve", "pool", "sp"]
    R = 8

    def __init__(self, nc):
        self.nc = nc
        self.ops = {e: [] for e in self.ENG}
        self.ndma = {e: 0 for e in self.ENG}
        self.dma_ops = {e: [] for e in self.ENG}
        self.cnt = {e: 0 for e in self.ENG}
        self.esem = {e: nc.alloc_semaphore("es_" + e) for e in self.ENG}
        self.dsem = {
            e: [nc.alloc_semaphore("ds_%s_%d" % (e, i)) for i in range(self.R)]
            for e in ("sp", "pool", "act")
        }
        self.waited = {e: {} for e in self.ENG}
        self.nblk = 0

    def op(self, eng, fn, reads=(), writes=(), wadd=(), dma=False):
        o = Op()
        o.eng = eng
        o.fn = fn
        o.is_dma = dma
        o.needed = False
        o.count = 0
        deps = []
        for h in reads:
            deps += h.writers
        for h in writes:
            deps += h.writers
            deps += h.readers
        for h in wadd:
            deps += h.readers
        seen = set()
        w = []
        for d in deps:
            if id(d) in seen:
                continue
            seen.add(id(d))
            if d.eng == eng and eng == "pe" and (not d.is_dma) and (not dma):
                continue
            w.append(d)
            d.needed = True
        if dma:
            q = self.ndma[eng]
            self.ndma[eng] += 1
            o.dslot = q % self.R
            o.dval = 16 * (q // self.R + 1)
            o.needed = True
            if q >= self.R:
                w.append(self.dma_ops[eng][q - self.R])
            self.dma_ops[eng].append(o)
        o.waits = w
        self.ops[eng].append(o)
        for h in reads:
            h.readers.append(o)
        for h in writes:
            h.writers = [o]
            h.readers = []
        for h in wadd:
            h.writers.append(o)
        return o

    def flush(self):
        nc = self.nc
        for e in self.ENG:
            c = self.cnt[e]
            for o in self.ops[e]:
                if o.needed and not o.is_dma:
                    c += 1
                    o.count = c
            self.cnt[e] = c
            assert c < 60000, (e, c)
        names = {"pe": "tensor", "act": "scalar", "dve": "vector", "pool": "gpsimd", "sp": "sync"}
        pending = {e: self.ops[e] for e in self.ENG}
        self.ops = {e: [] for e in self.ENG}
        with nc.Block() as blk:
            for e in self.ENG:
                if not pending[e]:
                    continue

                def body(eng, e=e):
                    waited = self.waited[e]
                    for o in pending[e]:
                        for d in o.waits:
                            if d.is_dma:
                                sem = self.dsem[d.eng][d.dslot]
                                key = (d.eng, d.dslot)
                                val = d.dval
                            else:
                                sem = self.esem[d.eng]
                                key = d.eng
                                val = d.count
                            if waited.get(key, 0) >= val:
                                continue
                            eng.wait_ge(sem, val)
                            waited[key] = val
                        ins = o.fn(eng)
                        if o.is_dma:
                            ins.then_inc(self.dsem[e][o.dslot], 16)
                        elif o.needed:
                            ins.then_inc(self.esem[e], 1)

                getattr(blk, names[e])(body)
        self.nblk += 1


class Prog:
    pass


def _consts():
    inv = 10000.0 ** (-np.arange(0, 64, 2, dtype=np.float64) / 64.0)
    pos = np.arange(S, dtype=np.float64)
    ang = inv[:, None] * pos[None, :]
    cos = np.cos(ang)
    sin = np.sin(ang)
    cosT = np.zeros((128, S), np.float32)
    sinT = np.zeros((128, S), np.float32)
    for p in range(128):
        d = p % 64
        f = d % 32
        cosT[p] = cos[f]
        sinT[p] = -sin[f] if d < 32 else sin[f]
    ident = np.eye(128, dtype=np.float32).astype(ml_dtypes.bfloat16)
    t = np.arange(128)[:, None]
    s = np.arange(128)[None, :]
    m_strict = (s < t).astype(np.float32)
    m_strict_inv = (s >= t).astype(np.float32)
    m_caus_add = np.where(s <= t, 0.0, NEG).astype(np.float32)
    m_neg = np.where(s >= t, -2048.0, 0.0).astype(np.float32).astype(ml_dtypes.bfloat16)
    nident = (np.eye(128, dtype=np.float32) * -30000.0).astype(ml_dtypes.bfloat16)
    pw = np.tile((2.0 ** -(np.arange(NBIS + 2, dtype=np.float64) + 1)).astype(np.float32)[None, :], (128, 1))
    iota8 = np.tile(np.arange(8, dtype=np.float32)[None, :], (128, 1))
    pm = np.zeros((128, 128), np.float32)
    for pp in range(128):
        pm[(pp % 64 + 32) % 64 + 64 * (pp // 64), pp] = 1.0
    pm = pm.astype(ml_dtypes.bfloat16)
    return dict(cosT=cosT, sinT=sinT, ident=ident, nident=nident, pw=pw, iota8=iota8, pm=pm,
                m_strict=m_strict.astype(ml_dtypes.bfloat16),
                m_strict_inv=m_strict_inv, m_caus_add=m_caus_add, m_neg=m_neg)


W_SPECS = [
    ("w_in", D, D_IN), ("w_branch_sb", 512, D), ("w_branch_dsa", 512, D),
    ("w_gate", D, 2 * D), ("w_out", D, D), ("w_cq", D, 512), ("w_ckv", D, D),
    ("w_co", 512, D), ("w_up", D, DFF), ("w_down", DFF, D),
]
V_SPECS = ["norm_mix", "norm_cross", "norm_mem", "norm_mlp", "norm_final"]


def build(stage="full"):
    nc = bass.Bass("TRN2", target_bir_lowering=False)
    P = Prog()
    P.nc = nc
    P.stage = stage
    sc = Sched(nc)
    P.sc = sc
    P.dbg = {}

    P.x = nc.dram_tensor("x", [S, D], F32, kind="ExternalInput").ap()
    P.mem = nc.dram_tensor("mem", [NMEM, D], F32, kind="ExternalInput").ap()
    P.w32 = {}
    P.wbf = {}
    P.wh = {}
    for name, r, c in W_SPECS:
        P.w32[name] = nc.dram_tensor(name, [r, c], F32, kind="ExternalInput").ap()
        P.wbf[name] = nc.dram_tensor(name + "_bf", [r, c], BF16, kind="Internal").ap()
    P.vec = {}
    for name in V_SPECS:
        P.vec[name] = nc.dram_tensor(name, [1, D], F32, kind="ExternalInput").ap()
    P.b_gate = nc.dram_tensor("b_gate", [128, 16], F32, kind="ExternalInput").ap()
    cs = _consts()
    P.cdram = {}
    for k, v in cs.items():
        dt = BF16 if v.dtype == ml_dtypes.bfloat16 else F32
        P.cdram[k] = nc.dram_tensor("c_" + k, list(v.shape), dt, kind="ExternalInput").ap()
    P.out = nc.dram_tensor("out", [S, D], F32, kind="ExternalOutput").ap()

    P.bank = [nc.alloc_psum_tensor("bank%d" % i, [128, 512], F32) for i in range(8)]
    P.bankh = [H("bank%d" % i) for i in range(8)]

    from contextlib import ExitStack
    with ExitStack() as g:
        def sb(name, shape, dt):
            return g.enter_context(nc.sbuf_tensor(name, shape, dt))
        P.sb_global = sb
        P.ident = sb("ident", [128, 128], BF16)
        P.h_const = H("const")
        sc.op("sp", lambda e: e.dma_start(out=P.ident[:], in_=P.cdram["ident"]),
              writes=[P.h_const], dma=True)
        P.nident = sb("nident", [128, 128], BF16)
        sc.op("sp", lambda e: e.dma_start(out=P.nident[:], in_=P.cdram["nident"]),
              wadd=[P.h_const], dma=True)
        phases(P, g)
        sc.flush()
    return nc, P


def cast_dma(P, fn, wh, first):
    sc = P.sc
    hist = P.__dict__.setdefault("cast_hist", [])
    hc = H()
    rd = [hist[-3]] if len(hist) >= 3 else []
    sc.op("pool", fn, reads=rd, writes=[hc] + ([wh] if first else []), wadd=() if first else [wh], dma=True)
    hist.append(hc)


def cast_weights(P, names, cols=None, hname=None):
    for name in names:
        w = P.w32[name]
        o = P.wbf[name]
        r, c = w.shape
        ca, cb = cols if cols is not None else (0, c)
        hn = hname or name
        P.wh[hn] = H("w_" + hn)
        ncs = (cb - ca + 2047) // 2048
        cw = (cb - ca + ncs - 1) // ncs
        first = True
        for r0 in range(0, r, 1024):
            r1 = min(r, r0 + 1024)
            for c0 in range(ca, cb, cw):
                c1 = min(cb, c0 + cw)
                cast_dma(P, lambda e, w=w, o=o, r0=r0, r1=r1, c0=c0, c1=c1: e.dma_start(
                    out=o[r0:r1, c0:c1], in_=w[r0:r1, c0:c1]), P.wh[hn], first)
                first = False


def load_w(P, eng, dst, name, c0, ncols, hdst, r0=0, nk=None, hname=None):
    w = P.wbf[name]
    rows = w.shape[0]
    if nk is None:
        nk = rows // 128
    src = w.rearrange("(kc p) n -> p kc n", p=128)[:, r0 // 128:r0 // 128 + nk, c0:c0 + ncols]
    return P.sc.op(eng, lambda e: e.dma_start(out=dst, in_=src),
                   reads=[P.wh[hname or name]], writes=[hdst], dma=True)


class NormTmp:
    def __init__(self, P, sb):
        self.gbc = sb("nt_gbc", [128, D], F32)
        self.xt = [sb("nt_xt%d" % i, [128, D], F32) for i in range(2)]
        self.junk = sb("nt_junk", [128, D], BF16)
        self.ub = [sb("nt_ub%d" % i, [128, D], BF16) for i in range(2)]
        self.st = sb("nt_st", [128, 4 * NT], F32)
        self.h_g = H()
        self.h_xt = [H(), H()]
        self.h_junk = H()
        self.h_ub = [H(), H()]
        self.h_st = [H() for _ in range(NT)]


def rmsnorm_T(P, g, src_kind, gname, uT, uTh, ntiles=NT, src=None, srch=None, tag="n"):
    nc, sc = P.nc, P.sc
    T = P.ntmp
    gbc, xt, junk, ub, st = T.gbc, T.xt, T.junk, T.ub, T.st
    h_g, h_xt, h_junk, h_ub, h_st = T.h_g, T.h_xt, T.h_junk, T.h_ub, T.h_st
    sc.op("sp", lambda e: e.dma_start(out=gbc[:], in_=P.vec[gname].to_broadcast([128, D])),
          writes=[h_g], dma=True)
    pbank = [6, 7]
    for i in range(ntiles):
        j = i % 2
        if src_kind == "dram":
            xin = xt[j][:]
            hx = h_xt[j]
            sc.op("sp", lambda e, i=i, j=j: e.dma_start(out=xt[j][:], in_=src[i * 128:(i + 1) * 128, :]),
                  writes=[hx], dma=True)
        else:
            xin = src[:, i, :]
            hx = srch[i]
        ss = st[:, 4 * i:4 * i + 1]
        ms = st[:, 4 * i + 1:4 * i + 2]
        sd = st[:, 4 * i + 2:4 * i + 3]
        rs = st[:, 4 * i + 3:4 * i + 4]
        sc.op("act", lambda e, xin=xin, ss=ss: e.activation(out=junk[:], in_=xin, func=AF.Square, accum_out=ss),
              reads=[hx], writes=[h_junk, h_st[i]])
        sc.op("dve", lambda e, ss=ss, ms=ms: e.tensor_scalar(out=ms, in0=ss, scalar1=1.0 / D, scalar2=EPS,
                                                               op0=ALU.mult, op1=ALU.add),
              reads=[h_st[i]], writes=[h_st[i]])
        sc.op("act", lambda e, sd=sd, ms=ms: e.activation(out=sd, in_=ms, func=AF.Sqrt),
              reads=[h_st[i]], writes=[h_st[i]])
        sc.op("dve", lambda e, sd=sd, rs=rs: e.reciprocal(out=rs, in_=sd),
              reads=[h_st[i]], writes=[h_st[i]])
        sc.op("dve", lambda e, xin=xin, rs=rs, j=j: e.scalar_tensor_tensor(
            out=ub[j][:], in0=xin, scalar=rs, in1=gbc[:], op0=ALU.mult, op1=ALU.mult),
            reads=[hx, h_st[i], h_g], writes=[h_ub[j]])
        pb = pbank[j]
        pv = P.bank[pb][:].bitcast(BF16)
        for c in range(KC):
            sc.op("pe", lambda e, c=c, j=j, pv=pv: e.transpose(
                out=pv[:, c * 128:(c + 1) * 128], in_=ub[j][:, c * 128:(c + 1) * 128], identity=P.ident[:]),
                reads=[h_ub[j], P.h_const], writes=[P.bankh[pb]] if c == 0 else (),
                wadd=() if c == 0 else [P.bankh[pb]])
        sc.op("act", lambda e, i=i, pv=pv: e.copy(
            out=uT[:, :, i * 128:(i + 1) * 128], in_=pv.rearrange("p (c t) -> p c t", c=KC)),
            reads=[P.bankh[pb]], writes=[uTh[i]])


class Banks:
    def __init__(self, P, ids):
        self.P = P
        self.ids = list(ids)
        self.i = 0

    def next(self):
        b = self.ids[self.i % len(self.ids)]
        self.i += 1
        return b


def evac_copy(P, k, out, in_, reads, writes, wadd=()):
    if k % 2 == 0:
        return P.sc.op("act", lambda e: e.copy(out=out, in_=in_), reads=reads, writes=writes, wadd=wadd)
    return P.sc.op("dve", lambda e: e.tensor_copy(out=out, in_=in_), reads=reads, writes=writes, wadd=wadd)


def proj_fm(P, wt, wth, ncols, uT, uTh, banks, evac, nk=KC, ntok=S):
    sc = P.sc
    tgw = min(512, ntok)
    for j in range(ncols // 128):
        for tg in range(ntok // tgw):
            b = banks.next()
            rd = [wth] + [uTh[i] for i in range(tg * tgw // 128, (tg + 1) * tgw // 128)]
            for kc in range(nk):
                sc.op("pe", lambda e, j=j, tg=tg, kc=kc, b=b: e.matmul(
                    P.bank[b][:, 0:tgw], lhsT=wt[:, kc, j * 128:(j + 1) * 128],
                    rhs=uT[:, kc, tg * tgw:(tg + 1) * tgw], start=(kc == 0), stop=(kc == nk - 1)),
                    reads=rd, writes=[P.bankh[b]] if kc == 0 else (), wadd=() if kc == 0 else [P.bankh[b]])
            evac(j, tg, P.bank[b][:, 0:tgw], P.bankh[b])


def proj_tm(P, wt, wth, ncols, uT, uTh, banks, evac, nk=KC, ntiles=NT):
    sc = P.sc
    for i in range(ntiles):
        for c0 in range(0, ncols, 512):
            cw = min(512, ncols - c0)
            b = banks.next()
            for kc in range(nk):
                sc.op("pe", lambda e, i=i, kc=kc, b=b, c0=c0, cw=cw: e.matmul(
                    P.bank[b][:, 0:cw], lhsT=uT[:, kc, i * 128:(i + 1) * 128],
                    rhs=wt[:, kc, c0:c0 + cw], start=(kc == 0), stop=(kc == nk - 1)),
                    reads=[wth, uTh[i]], writes=[P.bankh[b]] if kc == 0 else (),
                    wadd=() if kc == 0 else [P.bankh[b]])
            evac(i, c0, cw, P.bank[b][:, 0:cw], P.bankh[b])


def sb_attention(P, g, qT, kT, v, hq, hk, hv, osT, osTh):
    nc, sc = P.nc, P.sc
    sb = lambda n, s, d: g.enter_context(nc.sbuf_tensor("sb_" + n, s, d))
    scale = 0.125
    ones = sb("ones", [128, S], BF16)
    h_ones = H()
    sc.op("pool", lambda e: e.memset(ones[:], 1.0), writes=[h_ones])
    negm = sb("negm", [128, 128], BF16)
    h_m = H()
    sc.op("sp", lambda e: e.dma_start(out=negm[:], in_=P.cdram["m_neg"]), writes=[h_m], dma=True)
    NB = 2
    beta = [sb("beta%d" % i, [128, S], BF16) for i in range(NB)]
    omb = [sb("omb%d" % i, [128, S], F32) for i in range(NB)]
    Q = [sb("Q%d" % i, [128, S], BF16) for i in range(NB)]
    a = [sb("a%d" % i, [128, S], BF16) for i in range(NB)]
    aT = [sb("aT%d" % i, [128, NT, 128], BF16) for i in range(NB)]
    otm = [sb("otm%d" % i, [128, 512], BF16) for i in range(NB)]
    h_beta = [H() for _ in range(NB)]
    h_omb = [H() for _ in range(NB)]
    h_Q = [H() for _ in range(NB)]
    h_a = [H() for _ in range(NB)]
    h_aT = [H() for _ in range(NB)]
    h_otm = [H() for _ in range(NB)]
    for r in range(NB):
        sc.op("pool", lambda e, r=r: e.memset(Q[r][:], 1.0), writes=[h_Q[r]])
    its = [(tb, h) for tb in range(NT) for h in range(8)]

    def zbank(n, npc, p):
        return p + (2 * (n % 2) if npc <= 2 else 0)

    def stage1(n):
        tb, h = its[n]
        r = n % NB
        L = (tb + 1) * 128
        npc = (L + 511) // 512
        ch, p0 = h // 2, (h % 2) * 64
        for p in range(npc):
            n_ = min(512, L - p * 512)
            b = zbank(n, npc, p)
            last = (p == npc - 1)
            sc.op("pe", lambda e, p=p, n_=n_, b=b, last=last: e.matmul(
                P.bank[b][:, 0:n_], lhsT=qT[p0:p0 + 64, ch, tb * 128:(tb + 1) * 128],
                rhs=kT[p0:p0 + 64, ch, p * 512:p * 512 + n_], start=True, stop=not last),
                reads=[hq[tb]] + [hk[i] for i in range(p * 4, p * 4 + n_ // 128)], writes=[P.bankh[b]])
            if last:
                sc.op("pe", lambda e, n_=n_, b=b: e.matmul(
                    P.bank[b][:, n_ - 128:n_], lhsT=P.ident[:], rhs=negm[:], start=False, stop=True),
                    reads=[h_m, P.h_const], wadd=[P.bankh[b]])
        for p in range(npc):
            n_ = min(512, L - p * 512)
            b = zbank(n, npc, p)
            sc.op("act", lambda e, p=p, n_=n_, b=b: e.activation(
                out=beta[r][:, p * 512:p * 512 + n_], in_=P.bank[b][:, 0:n_], func=AF.Sigmoid, scale=scale),
                reads=[P.bankh[b]], writes=[h_beta[r]] if p == 0 else (), wadd=() if p == 0 else [h_beta[r]])
            sc.op("act", lambda e, p=p, n_=n_, b=b: e.activation(
                out=omb[r][:, p * 512:p * 512 + n_], in_=P.bank[b][:, 0:n_], func=AF.Sigmoid, scale=-scale),
                reads=[P.bankh[b]], writes=[h_omb[r]] if p == 0 else (), wadd=() if p == 0 else [h_omb[r]])
    def stage1b(n):
        tb, h = its[n]
        r = n % NB
        L = (tb + 1) * 128
        sc.op("dve", lambda e: e.tensor_tensor_scan(
            out=Q[r][:, L - 2::-1], data0=omb[r][:, L - 1:0:-1], data1=ones[:, 0:L - 1],
            initial=1.0, op0=ALU.mult, op1=ALU.mult),
            reads=[h_omb[r], h_ones], writes=[h_Q[r]])
        sc.op("dve", lambda e: e.tensor_tensor(
            out=a[r][:, 0:L], in0=beta[r][:, 0:L], in1=Q[r][:, 0:L], op=ALU.mult),
            reads=[h_beta[r], h_Q[r]], writes=[h_a[r]])

    def stage2(n):
        tb, h = its[n]
        r = n % NB
        for kb0 in range(0, tb + 1, 8):
            nb = min(8, tb + 1 - kb0)
            pb = 4 + (kb0 // 8)
            pv = P.bank[pb][:].bitcast(BF16)
            for kb in range(kb0, kb0 + nb):
                sc.op("pe", lambda e, kb=kb, kb0=kb0, pv=pv: e.transpose(
                    out=pv[:, (kb - kb0) * 128:(kb - kb0 + 1) * 128], in_=a[r][:, kb * 128:(kb + 1) * 128],
                    identity=P.ident[:]),
                    reads=[h_a[r], P.h_const], writes=[P.bankh[pb]] if kb == kb0 else (),
                    wadd=() if kb == kb0 else [P.bankh[pb]])
            sc.op("act", lambda e, kb0=kb0, nb=nb, pv=pv: e.copy(
                out=aT[r][:, kb0:kb0 + nb, :], in_=pv[:, 0:nb * 128].rearrange("p (c t) -> p c t", c=nb)),
                reads=[P.bankh[pb]], writes=[h_aT[r]] if kb0 == 0 else (), wadd=() if kb0 == 0 else [h_aT[r]])
    def stage2b(n):
        tb, h = its[n]
        r = n % NB
        for kb in range(tb + 1):
            sc.op("pe", lambda e, kb=kb: e.matmul(
                P.bank[6][:, h * 64:(h + 1) * 64], lhsT=aT[r][:, kb, :], rhs=v[:, kb, h * 64:(h + 1) * 64],
                start=(kb == 0), stop=(kb == tb)),
                reads=[h_aT[r], hv[kb]], writes=[P.bankh[6]] if (kb == 0 and h == 0) else (),
                wadd=() if (kb == 0 and h == 0) else [P.bankh[6]])
        if h == 7:
            ro = tb % NB
            sc.op("dve", lambda e: e.tensor_copy(out=otm[ro][:], in_=P.bank[6][:, :]),
                  reads=[P.bankh[6]], writes=[h_otm[ro]])
            pv = P.bank[7][:].bitcast(BF16)
            for c in range(4):
                sc.op("pe", lambda e, c=c, pv=pv: e.transpose(
                    out=pv[:, c * 128:(c + 1) * 128], in_=otm[ro][:, c * 128:(c + 1) * 128], identity=P.ident[:]),
                    reads=[h_otm[ro], P.h_const], writes=[P.bankh[7]] if c == 0 else (),
                    wadd=() if c == 0 else [P.bankh[7]])
            sc.op("dve", lambda e, pv=pv: e.tensor_copy(
                out=osT[:, :, tb * 128:(tb + 1) * 128], in_=pv[:, 0:512].rearrange("p (c t) -> p c t", c=4)),
                reads=[P.bankh[7]], writes=[osTh[tb]])

    N = len(its)
    for n in range(N + 3):
        if n < N:
            stage1(n)
        if 1 <= n <= N:
            stage1b(n - 1)
        if 2 <= n <= N + 1:
            stage2(n - 2)
        if n >= 3:
            stage2b(n - 3)


DSA_COLS = 1408


def build_dsa_weights(P):
    nc, sc = P.nc, P.sc
    w = P.w32["w_in"]
    P.dsaA = nc.dram_tensor("dsaA_bf", [D, DSA_COLS], BF16, kind="Internal").ap()
    P.wh["dsaA"] = H()
    blocks = [(0, 1536, 8), (512, 2048, 1), (576, 2048, 1), (640, 2112, 1), (704, 2112, 1),
              (768, 2304, 8), (1280, 2816, 1), (1344, 2816, 1)]
    for d0, s0, nh in blocks:
        cast_dma(P, lambda e, d0=d0, s0=s0, nh=nh: e.dma_start(
            out=P.dsaA[:, d0:d0 + nh * 64], in_=w[:, s0:s0 + nh * 64]), P.wh["dsaA"], False)


def dsa_project(P, g, uT, uTh, dqT, kT2, iqT, ikT2, vp, iwS, hdq, hk2, hiq, hik2, hvp, hiw):
    nc, sc = P.nc, P.sc
    sb = lambda n, s, d: g.enter_context(nc.sbuf_tensor("dp_" + n, s, d))
    cosT = sb("cos", [128, S], F32)
    sinT = sb("sin", [128, S], F32)
    h_cs = H()
    sc.op("sp", lambda e: e.dma_start(out=cosT[:], in_=P.cdram["cosT"]), wadd=[h_cs], dma=True)
    sc.op("sp", lambda e: e.dma_start(out=sinT[:], in_=P.cdram["sinT"]), wadd=[h_cs], dma=True)
    wv = sb("wv", [128, KC, 136], BF16)
    h_wv = H()
    load_w(P, "sp", wv[:, :, 0:128], "w_in", 2176, 128, h_wv)
    wsrc = P.wbf["w_in"].rearrange("(kc p) n -> p kc n", p=128)[:, :, 2880:2888]
    sc.op("sp", lambda e: e.dma_start(out=wv[:, :, 128:136], in_=wsrc), reads=[P.wh["w_in"]], wadd=[h_wv], dma=True)
    h_one = H()
    sc.op("pool", lambda e: e.memset(vp[:], 1.0), writes=hvp)
    banks = Banks(P, [4, 5])
    wsc = 0.125 * (8.0 ** -0.5)

    def evv(i, c0, cw, bap, bh):
        sc.op("act", lambda e, i=i, bap=bap: e.copy(
            out=vp[:, i, :].rearrange("p (g c) -> p g c", c=128)[:, :, 0:64],
            in_=bap[:, 0:128].rearrange("p (g c) -> p g c", c=64)),
            reads=[bh], writes=[hvp[i]])
        sc.op("act", lambda e, i=i, bap=bap: e.mul(out=iwS[:, i, :], in_=bap[:, 128:136], mul=wsc),
              reads=[bh], writes=[hiw[i]])
    proj_tm(P, wv, h_wv, 136, uT, uTh, banks, evv)
    wA = sb("wA", [128, KC, 512], BF16)
    h_wA = H()
    pmt = sb("pm", [128, 128], BF16)
    h_pm = H()
    sc.op("sp", lambda e: e.dma_start(out=pmt[:], in_=P.cdram["pm"]), writes=[h_pm], dma=True)
    xb = [sb("xb%d" % i, [128, 512], BF16) for i in range(2)]
    h_xb = [H(), H()]
    t1 = [sb("t1_%d" % i, [128, 512], F32) for i in range(2)]
    t2 = [sb("t2_%d" % i, [128, 512], F32) for i in range(2)]
    h_t1 = [H(), H()]
    h_t2 = [H(), H()]
    dests = ([(dqT, c, hdq) for c in range(4)] + [(kT2, 0, hk2), (kT2, 1, hk2)] +
             [(iqT, c, hiq) for c in range(4)] + [(ikT2, 0, hik2)])
    it = [0]
    for grp, (c0, ncols) in enumerate([(0, 512), (512, 512), (1024, 384)]):
        sc.op("sp", lambda e, c0=c0, ncols=ncols: e.dma_start(
            out=wA[:, :, 0:ncols], in_=P.dsaA.rearrange("(kc p) n -> p kc n", p=128)[:, :, c0:c0 + ncols]),
            reads=[P.wh["dsaA"]], writes=[h_wA], dma=True)
        for jj in range(ncols // 128):
            dst, dc, hd = dests[c0 // 128 + jj]
            for tg in range(4):
                r = it[0] % 2
                it[0] += 1
                bA, bB = (0, 1) if r == 0 else (2, 3)
                rd = [uTh[i] for i in range(tg * 4, tg * 4 + 4)]
                for kc in range(KC):
                    sc.op("pe", lambda e, bA=bA, kc=kc, jj=jj, tg=tg: e.matmul(
                        P.bank[bA][:, :], lhsT=wA[:, kc, jj * 128:(jj + 1) * 128],
                        rhs=uT[:, kc, tg * 512:(tg + 1) * 512], start=(kc == 0), stop=(kc == KC - 1)),
                        reads=rd + [h_wA], writes=[P.bankh[bA]] if kc == 0 else (),
                        wadd=() if kc == 0 else [P.bankh[bA]])
                sc.op("act", lambda e, bA=bA, r=r: e.copy(out=xb[r][:], in_=P.bank[bA][:, :]),
                      reads=[P.bankh[bA]], writes=[h_xb[r]])
                sc.op("pe", lambda e, bB=bB, r=r: e.matmul(
                    P.bank[bB][:, :], lhsT=pmt[:], rhs=xb[r][:], start=True, stop=True),
                    reads=[h_xb[r], h_pm], writes=[P.bankh[bB]])
                sc.op("dve", lambda e, bA=bA, r=r, tg=tg: e.tensor_tensor(
                    out=t1[r][:], in0=P.bank[bA][:, :], in1=cosT[:, tg * 512:(tg + 1) * 512], op=ALU.mult),
                    reads=[P.bankh[bA], h_cs, h_xb[r]], writes=[h_t1[r]])
                sc.op("dve", lambda e, bB=bB, r=r, tg=tg: e.tensor_tensor(
                    out=t2[r][:], in0=P.bank[bB][:, :], in1=sinT[:, tg * 512:(tg + 1) * 512], op=ALU.mult),
                    reads=[P.bankh[bB], h_cs], writes=[h_t2[r]])
                sc.op("pool", lambda e, r=r, dst=dst, dc=dc, tg=tg: e.tensor_tensor(
                    out=dst[:, dc, tg * 512:(tg + 1) * 512], in0=t1[r][:], in1=t2[r][:], op=ALU.add),
                    reads=[h_t1[r], h_t2[r]], writes=[hd[i] for i in range(tg * 4, tg * 4 + 4)])


def dsa_attention(P, g, dqT, kT2, iqT, ikT2, vp, iwS, hdq, hk2, hiq, hik2, hvp, hiw, odT, odTh):
    nc, sc = P.nc, P.sc
    sb = lambda n, s, d: g.enter_context(nc.sbuf_tensor("da_" + n, s, d))
    scale = 0.125
    KTOP = 256
    U32 = mybir.dt.uint32
    mca = sb("mca", [128, 128], F32)
    h_m = H()
    sc.op("sp", lambda e: e.dma_start(out=mca[:], in_=P.cdram["m_caus_add"]), writes=[h_m], dma=True)
    thrc = sb("thrc", [128, 1], F32)
    h_thrc = H()
    sc.op("pool", lambda e: e.memset(thrc[:], -1.0e29), writes=[h_thrc])
    io8 = sb("io8", [128, 8], F32)
    h_io8 = H()
    sc.op("sp", lambda e: e.dma_start(out=io8[:], in_=P.cdram["iota8"]), writes=[h_io8], dma=True)
    pw = sb("pw", [128, NBIS + 2], F32)
    h_pw = H()
    sc.op("sp", lambda e: e.dma_start(out=pw[:], in_=P.cdram["pw"]), writes=[h_pw], dma=True)
    NA = 3
    acc = [sb("acc%d" % i, [128, S], F32) for i in range(NA)]
    h_acc = [H() for _ in range(NA)]
    rmin = [sb("rmin%d" % i, [128, 1], F32) for i in range(NA)]
    h_rmin = [H() for _ in range(NA)]
    junk = sb("junk", [128, S], F32)
    wsel = sb("wsel", [128, S], F32)
    h_junk, h_wsel = H(), H()
    NTH = 4
    th = [sb("th%d" % i, [128, 512], F32) for i in range(2)] + [P.ntmp.gbc[:, 0:512], P.ntmp.gbc[:, 512:1024]]
    h_th = [H() for _ in range(NTH)]
    mask = sb("mask", [128, S], BF16)
    h_mask = H()
    maskT = [sb("maskT%d" % i, [128, NT, 128], BF16) for i in range(2)]
    h_maskT = [H(), H()]
    NE = 3
    pbuf = [sb("p%d" % i, [128, 512], BF16) for i in range(NE)]
    h_p = [H() for _ in range(NE)]
    rec = [sb("rec%d" % i, [128, 512], F32) for i in range(2)]
    h_rec = [H(), H()]

    class Chain:
        pass
    chains = []
    for ci in range(2):
        c = Chain()
        c.j16 = P.ntmp.xt[ci][:].bitcast(BF16)
        c.dl = sb("dl%d" % ci, [128, NBIS + 2], F32)
        c.bs = sb("bs%d" % ci, [128, 8], F32)
        c.Mh = sb("Mh%d" % ci, [128, NBIS + 1], F32)
        c.Uh = sb("Uh%d" % ci, [128, NBIS], F32)
        c.cand = sb("cand%d" % ci, [128, NBIS], F32)
        c.candh = sb("candh%d" % ci, [128, NBIS], F32)
        c.Vh = sb("Vh%d" % ci, [128, NBIS], F32)
        c.fs = sb("fs%d" % ci, [128, 8], F32)
        c.m8 = sb("m8_%d" % ci, [128, 8], F32)
        c.m8b = sb("m8b%d" % ci, [128, 8], F32)
        c.t8 = sb("t8_%d" % ci, [128, 8], F32)
        for nm in ("j16", "dl", "rng", "Mh", "Uh", "cand", "candh", "Vh", "fs", "m8", "m8b", "t8", "cnt", "lo", "tmp"):
            setattr(c, "h_" + nm, H())
        chains.append(c)
    cnt = {"th": 0, "ep": 0, "lb": 0}

    def part_I(tb):
        L = (tb + 1) * 128
        npc = (L + 511) // 512
        ac, h_ac = acc[tb % NA], h_acc[tb % NA]
        thunks = []
        for h in range(8):
            def f(h=h):
                ch, p0 = h // 2, (h % 2) * 64
                for p in range(npc):
                    n = min(512, L - p * 512)
                    b = p % 2
                    r = cnt["th"] % NTH
                    cnt["th"] += 1
                    sc.op("pe", lambda e, p=p, n=n, b=b: e.matmul(
                        P.bank[b][:, 0:n], lhsT=iqT[p0:p0 + 64, ch, tb * 128:(tb + 1) * 128],
                        rhs=ikT2[p0:p0 + 64, 0, p * 512:p * 512 + n], start=True, stop=True),
                        reads=[hiq[tb]] + [hik2[i] for i in range(p * 4, p * 4 + n // 128)], writes=[P.bankh[b]])
                    if h == 0:
                        sc.op("act", lambda e, n=n, b=b, r=r: e.activation(
                            out=th[r][:, 0:n], in_=P.bank[b][:, 0:n], func=AF.Relu),
                            reads=[P.bankh[b]], writes=[h_th[r]])
                        sc.op("pool", lambda e, p=p, n=n, r=r: e.tensor_scalar(
                            out=ac[:, p * 512:p * 512 + n], in0=th[r][:, 0:n], scalar1=iwS[:, tb, h:h + 1],
                            scalar2=0.0, op0=ALU.mult, op1=ALU.add),
                            reads=[h_th[r], hiw[tb]], writes=[h_ac] if p == 0 else (), wadd=() if p == 0 else [h_ac])
                    else:
                        sc.op("act", lambda e, n=n, b=b, r=r: e.activation(
                            out=th[r][:, 0:n], in_=P.bank[b][:, 0:n], func=AF.Relu),
                            reads=[P.bankh[b]], writes=[h_th[r]])
                        sc.op("pool", lambda e, n=n, r=r: e.tensor_scalar(
                            out=th[r][:, 0:n], in0=th[r][:, 0:n], scalar1=iwS[:, tb, h:h + 1], scalar2=0.0,
                            op0=ALU.mult, op1=ALU.add),
                            reads=[hiw[tb]], writes=[h_th[r]])
                        sc.op("pool", lambda e, p=p, n=n, r=r: e.tensor_tensor(
                            out=ac[:, p * 512:p * 512 + n], in0=ac[:, p * 512:p * 512 + n], in1=th[r][:, 0:n],
                            op=ALU.add),
                            reads=[h_th[r]], writes=[h_ac])
                if h == 7:
                    if tb >= 2:
                        sc.op("dve", lambda e: e.tensor_reduce(
                            out=rmin[tb % NA][:], in_=ac[:, 0:L], axis=AX.X, op=ALU.min),
                            reads=[h_ac], writes=[h_rmin[tb % NA]])
                    sc.op("pool", lambda e: e.tensor_tensor(
                        out=ac[:, L - 128:L], in0=ac[:, L - 128:L], in1=mca[:], op=ALU.add),
                        reads=[h_m], writes=[h_ac])
            thunks.append(f)
        return thunks

    def part_T(tb):
        L = (tb + 1) * 128
        mT, h_mT = maskT[tb % 2], h_maskT[tb % 2]
        ac, h_ac = acc[tb % NA], h_acc[tb % NA]
        c = chains[tb % 2]
        ops = []
        A = ops.append
        KB = NBIS if L > 1024 else (NBIS - 1 if L > 512 else NBIS - 2)
        if tb >= 2:
            rm, h_rm = rmin[tb % NA], h_rmin[tb % NA]
            lo, cn = c.bs[:, 0:1], c.bs[:, 4:5]
            A(lambda: sc.op("dve", lambda e: e.max(out=c.m8[:], in_=ac[:, 0:L]), reads=[h_ac], writes=[c.h_m8]))
            A(lambda: sc.op("dve", lambda e: e.tensor_tensor(out=c.bs[:, 1:2], in0=c.m8[:, 0:1], in1=rm[:],
                                                             op=ALU.subtract),
                            reads=[c.h_m8, h_rm], writes=[c.h_rng]))
            A(lambda: sc.op("dve", lambda e: e.tensor_scalar(out=c.dl[:], in0=pw[:], scalar1=c.bs[:, 1:2],
                                                             scalar2=None, op0=ALU.mult),
                            reads=[h_pw, c.h_rng], writes=[c.h_dl]))
            A(lambda: sc.op("dve", lambda e: e.tensor_tensor(out=c.Mh[:, 0:1], in0=rm[:], in1=c.dl[:, 0:1],
                                                             op=ALU.add),
                            reads=[h_rm, c.h_dl], writes=[c.h_Mh]))
            A(lambda: sc.op("pool", lambda e: e.memset(c.cand[:], NEG), writes=[c.h_cand]))
            A(lambda: sc.op("pool", lambda e: e.memset(c.candh[:], -NEG), writes=[c.h_candh]))
            for k in range(KB):
                mid = c.Mh[:, k:k + 1]
                A(lambda mid=mid: sc.op("dve", lambda e: e.tensor_scalar(
                    out=c.j16[:, 0:L], in0=ac[:, 0:L], scalar1=mid, scalar2=None, op0=ALU.is_ge, op1=ALU.add,
                    accum_out=cn),
                    reads=[h_ac, c.h_Mh], writes=[c.h_j16, c.h_cnt]))
                A(lambda k=k: sc.op("dve", lambda e: e.scalar_tensor_tensor(
                    out=c.Uh[:, k:k + 1], in0=cn, scalar=float(KTOP), in1=c.dl[:, k:k + 1],
                    op0=ALU.is_ge, op1=ALU.mult),
                    reads=[c.h_cnt, c.h_dl], writes=[c.h_Uh]))
                A(lambda k=k, mid=mid: sc.op("dve", lambda e: e.scalar_tensor_tensor(
                    out=c.Mh[:, k + 1:k + 2], in0=c.Uh[:, k:k + 1], scalar=c.dl[:, k + 1:k + 2], in1=mid,
                    op0=ALU.subtract, op1=ALU.add),
                    reads=[c.h_Uh, c.h_dl], writes=[c.h_Mh]))
            nsplit = 6 + 3 * (KB // 2)
            A(lambda: sc.op("dve", lambda e: e.copy_predicated(out=c.cand[:, 0:KB], mask=c.Uh[:, 0:KB].bitcast(U32),
                                                               data=c.Mh[:, 0:KB]),
                            reads=[c.h_Uh, c.h_Mh], writes=[c.h_cand]))
            A(lambda: sc.op("dve", lambda e: e.tensor_reduce(out=c.bs[:, 6:7], in_=c.cand[:], axis=AX.X, op=ALU.max),
                            reads=[c.h_cand], writes=[c.h_tmp]))
            A(lambda: sc.op("dve", lambda e: e.tensor_tensor(out=lo, in0=c.bs[:, 6:7], in1=rm[:], op=ALU.max),
                            reads=[c.h_tmp, h_rm], writes=[c.h_lo]))
            A(lambda: sc.op("dve", lambda e: e.tensor_scalar(out=c.Vh[:, 0:KB], in0=c.Uh[:, 0:KB], scalar1=0.0,
                                                             scalar2=None, op0=ALU.is_equal),
                            reads=[c.h_Uh], writes=[c.h_Vh]))
            A(lambda: sc.op("dve", lambda e: e.copy_predicated(out=c.candh[:, 0:KB], mask=c.Vh[:, 0:KB].bitcast(U32),
                                                               data=c.Mh[:, 0:KB]),
                            reads=[c.h_Vh, c.h_Mh], writes=[c.h_candh]))
            A(lambda: sc.op("dve", lambda e: e.tensor_reduce(out=c.fs[:, 0:1], in_=c.candh[:], axis=AX.X, op=ALU.min),
                            reads=[c.h_candh], writes=[c.h_fs]))
            A(lambda: sc.op("dve", lambda e: e.tensor_tensor(out=c.fs[:, 1:2], in0=c.fs[:, 0:1], in1=c.m8[:, 0:1],
                                                             op=ALU.min),
                            reads=[c.h_fs, c.h_m8], writes=[c.h_fs]))
            A(lambda: sc.op("pool", lambda e: e.memset(wsel[:, 0:L], NEG), writes=[h_wsel]))
            A(lambda: sc.op("dve", lambda e: e.tensor_scalar(
                out=junk[:, 0:L], in0=ac[:, 0:L], scalar1=c.fs[:, 1:2], scalar2=None, op0=ALU.is_lt, op1=ALU.add,
                accum_out=c.fs[:, 2:3]),
                reads=[h_ac, c.h_fs], writes=[h_junk, c.h_fs]))
            A(lambda: sc.op("dve", lambda e: e.copy_predicated(
                out=wsel[:, 0:L], mask=junk[:, 0:L].bitcast(U32), data=ac[:, 0:L]),
                reads=[h_junk, h_ac], writes=[h_wsel]))
            A(lambda: sc.op("dve", lambda e: e.max(out=c.m8b[:], in_=wsel[:, 0:L]), reads=[h_wsel], writes=[c.h_m8b]))
            A(lambda: sc.op("dve", lambda e: e.tensor_scalar(out=c.fs[:, 3:4], in0=c.fs[:, 2:3],
                                                             scalar1=float(KTOP - 1 - L), scalar2=None, op0=ALU.add),
                            reads=[c.h_fs], writes=[c.h_fs]))
            A(lambda: sc.op("dve", lambda e: e.scalar_tensor_tensor(
                out=c.t8[:], in0=io8[:], scalar=c.fs[:, 3:4], in1=c.m8b[:], op0=ALU.is_equal, op1=ALU.mult,
                accum_out=c.fs[:, 4:5]),
                reads=[h_io8, c.h_fs, c.h_m8b], writes=[c.h_t8, c.h_fs]))
            A(lambda: sc.op("dve", lambda e: e.tensor_scalar(out=c.fs[:, 5:6], in0=c.fs[:, 3:4], scalar1=7.5,
                                                             scalar2=None, op0=ALU.is_gt),
                            reads=[c.h_fs], writes=[c.h_fs]))
            A(lambda: sc.op("dve", lambda e: e.copy_predicated(out=c.fs[:, 4:5], mask=c.fs[:, 5:6].bitcast(U32),
                                                               data=lo),
                            reads=[c.h_fs, c.h_lo], writes=[c.h_fs]))
            thr, hthr = c.fs[:, 4:5], c.h_fs
        else:
            nsplit = 0
            thr, hthr = thrc[:, 0:1], h_thrc
        A(lambda: sc.op("dve", lambda e: e.tensor_scalar(
            out=mask[:, 0:L], in0=ac[:, 0:L], scalar1=thr, scalar2=None, op0=ALU.is_lt),
            reads=[h_ac, hthr], writes=[h_mask]))
        for kb0 in range(0, tb + 1, 8):
            nb = min(8, tb + 1 - kb0)
            pb = kb0 // 8
            pv = P.bank[pb][:].bitcast(BF16)
            def grp(kb0=kb0, nb=nb, pv=pv, pb=pb):
                for kb in range(kb0, kb0 + nb):
                    sc.op("pe", lambda e, kb=kb: e.transpose(
                        out=pv[:, (kb - kb0) * 128:(kb - kb0 + 1) * 128], in_=mask[:, kb * 128:(kb + 1) * 128],
                        identity=P.ident[:]),
                        reads=[h_mask, P.h_const], writes=[P.bankh[pb]] if kb == kb0 else (),
                        wadd=() if kb == kb0 else [P.bankh[pb]])
                sc.op("act", lambda e: e.copy(
                    out=mT[:, kb0:kb0 + nb, :], in_=pv[:, 0:nb * 128].rearrange("p (c t) -> p c t", c=nb)),
                    reads=[P.bankh[pb]], writes=[h_mT] if kb0 == 0 else (), wadd=() if kb0 == 0 else [h_mT])
            A(grp)
        return ops[:nsplit], ops[nsplit:]

    def part_A(tb):
        mT, h_mT = maskT[tb % 2], h_maskT[tb % 2]
        thunks = []
        for gk in range(2):
            ob = 6 + gk
            for kb in range(tb + 1):
                def f(gk=gk, ob=ob, kb=kb):
                    sset = cnt["lb"] % 2
                    cnt["lb"] += 1
                    bxy = (2, 3) if sset == 0 else (4, 5)
                    r = cnt["ep"] % NE
                    cnt["ep"] += 1
                    for half in range(2):
                        p0 = half * 64
                        b = bxy[half]
                        sc.op("pe", lambda e, p0=p0, b=b: e.matmul(
                            P.bank[b][:, 0:256],
                            lhsT=kT2[p0:p0 + 64, gk, kb * 128:(kb + 1) * 128],
                            rhs=dqT[p0:p0 + 64, 2 * gk:2 * gk + 2, tb * 128:(tb + 1) * 128],
                            start=True, stop=False),
                            reads=[hk2[kb], hdq[tb]], writes=[P.bankh[b]])
                    for half in range(2):
                        b = bxy[half]
                        sc.op("pe", lambda e, b=b: e.matmul(
                            P.bank[b][:, 0:256], lhsT=P.nident[:],
                            rhs=mT[:, kb, :].unsqueeze(1).to_broadcast([128, 2, 128]), start=False, stop=True),
                            reads=[h_mT, P.h_const], wadd=[P.bankh[b]])
                    for half in range(2):
                        b = bxy[half]
                        sc.op("act", lambda e, b=b, half=half: e.activation(
                            out=pbuf[r][:, half * 256:(half + 1) * 256], in_=P.bank[b][:, 0:256], func=AF.Exp,
                            scale=scale),
                            reads=[P.bankh[b]], writes=[h_p[r]] if half == 0 else (), wadd=() if half == 0 else [h_p[r]])
                    sc.op("pe", lambda e: e.matmul(
                        P.bank[ob][:, :], lhsT=vp[:, kb, gk * 128:(gk + 1) * 128], rhs=pbuf[r][:, :],
                        start=(kb == 0), stop=(kb == tb)),
                        reads=[h_p[r], hvp[kb]], writes=[P.bankh[ob]] if kb == 0 else (),
                        wadd=() if kb == 0 else [P.bankh[ob]])
                thunks.append(f)

        def fin():
            first = True
            for gk in range(2):
                ob = 6 + gk
                sc.op("act", lambda e, ob=ob, gk=gk: e.activation(
                    out=rec[gk][64:128, :], in_=P.bank[ob][64:128, :], func=AF.Ln),
                    reads=[P.bankh[ob]], writes=[h_rec[gk]])
                sc.op("act", lambda e, gk=gk: e.activation(
                    out=rec[gk][64:128, :], in_=rec[gk][64:128, :], func=AF.Exp, scale=-1.0),
                    writes=[h_rec[gk]])
                for blk in range(4):
                    half, cidx = blk // 2, blk % 2
                    sc.op("dve", lambda e, ob=ob, gk=gk, blk=blk, half=half, cidx=cidx: e.tensor_tensor(
                        out=odT[half * 64:(half + 1) * 64, 2 * gk + cidx, tb * 128:(tb + 1) * 128],
                        in0=P.bank[ob][0:64, blk * 128:(blk + 1) * 128],
                        in1=rec[gk][64:128, blk * 128:(blk + 1) * 128], op=ALU.mult),
                        reads=[P.bankh[ob], h_rec[gk]], writes=[odTh[tb]] if first else (),
                        wadd=() if first else [odTh[tb]])
                    first = False
        thunks.append(fin)
        return thunks

    def run(l):
        for f in l:
            f()

    def merge(lists):
        pos = [0] * len(lists)
        while True:
            best, bi = None, -1
            for i, l in enumerate(lists):
                if pos[i] < len(l):
                    key = (pos[i] + 0.5) / len(l)
                    if best is None or key < best:
                        best, bi = key, i
            if bi < 0:
                break
            lists[bi][pos[bi]]()
            pos[bi] += 1

    def zip2(x, y):
        out = []
        nx, ny = len(x), len(y)
        ix = iy = 0
        while ix < nx or iy < ny:
            if ix < nx and (iy >= ny or ix * ny <= iy * nx):
                out.append(x[ix])
                ix += 1
            else:
                out.append(y[iy])
                iy += 1
        return out

    halves = {}

    def T1(tb):
        halves[tb] = part_T(tb)
        return halves[tb][0]

    def T2(tb):
        if tb not in halves:
            halves[tb] = part_T(tb)
        return halves[tb][1]

    for tb0 in (0, 1, 2):
        run(part_I(tb0))
    run(T1(0))
    run(T2(0))
    for tb in range(NT):
        lists = [part_A(tb)]
        x = T2(tb + 1) if tb + 1 < NT else []
        if tb + 1 < NT and tb + 1 < 2:
            x = T1(tb + 1) + x
        y = T1(tb + 2) if tb + 2 < NT else []
        tl = (x + y) if _os.environ.get("NOZIP") else zip2(x, y)
        if tl:
            lists.append(tl)
        if tb + 3 < NT:
            lists.append(part_I(tb + 3))
        merge(lists)


def merge_phase(P, g, uT, uTh, osT, osTh, odT, odTh):
    nc, sc = P.nc, P.sc
    sb = lambda n, s, d: g.enter_context(nc.sbuf_tensor("mg_" + n, s, d))
    wg = sb("wg", [128, KC, 2 * D], BF16)
    wbs = sb("wbs", [128, 4, D], BF16)
    wbd = sb("wbd", [128, 4, D], BF16)
    bg = sb("bg", [128, 16], F32)
    h_wg = [H() for _ in range(4)]
    h_wbs, h_wbd, h_bg = H(), H(), H()
    sc.op("sp", lambda e: e.dma_start(out=bg[:], in_=P.b_gate), writes=[h_bg], dma=True)
    for q in (0, 2, 1, 3):
        load_w(P, "sp", wg[:, :, q * 512:(q + 1) * 512], "w_gate", q * 512, 512, h_wg[q])
        if q == 2:
            load_w(P, "sp", wbs[:], "w_branch_sb", 0, D, h_wbs)
            load_w(P, "sp", wbd[:], "w_branch_dsa", 0, D, h_wbd)
    g1 = [sb("g1_%d" % i, [128, 512], F32) for i in range(2)]
    g2 = [sb("g2_%d" % i, [128, 512], F32) for i in range(2)]
    m1 = [sb("m1_%d" % i, [128, 512], F32) for i in range(2)]
    m2 = [sb("m2_%d" % i, [128, 512], F32) for i in range(2)]
    h_g1, h_g2, h_m1, h_m2 = [H(), H()], [H(), H()], [H(), H()], [H(), H()]
    tmp = sb("tmp", [128, KC, 512], BF16)
    h_tmp = H()
    it = 0
    for tg in range(4):
        tiles = list(range(tg * 4, tg * 4 + 4))
        for c in range(KC):
            r = it % 2
            it += 1
            bs = [0, 1, 2, 3] if r == 0 else [4, 5, 6, 7]
            specs = [(bs[0], wg, c * 128, uT, KC, [h_wg[c // 4]] + [uTh[i] for i in tiles]),
                     (bs[1], wg, D + c * 128, uT, KC, [h_wg[2 + c // 4]] + [uTh[i] for i in tiles]),
                     (bs[2], wbs, c * 128, osT, 4, [h_wbs] + [osTh[i] for i in tiles]),
                     (bs[3], wbd, c * 128, odT, 4, [h_wbd] + [odTh[i] for i in tiles])]
            for (bk, wt, c0, act, nk, rd) in specs:
                for kc in range(nk):
                    sc.op("pe", lambda e, bk=bk, wt=wt, c0=c0, act=act, kc=kc, nk=nk, tg=tg: e.matmul(
                        P.bank[bk][:, :], lhsT=wt[:, kc, c0:c0 + 128], rhs=act[:, kc, tg * 512:(tg + 1) * 512],
                        start=(kc == 0), stop=(kc == nk - 1)),
                        reads=rd, writes=[P.bankh[bk]] if kc == 0 else (), wadd=() if kc == 0 else [P.bankh[bk]])
            sc.op("act", lambda e, r=r, bk=bs[0], c=c: e.activation(
                out=g1[r][:], in_=P.bank[bk][:, :], func=AF.Sigmoid, bias=bg[:, c:c + 1]),
                reads=[P.bankh[bs[0]], h_bg], writes=[h_g1[r]])
            sc.op("act", lambda e, r=r, bk=bs[1], c=c: e.activation(
                out=g2[r][:], in_=P.bank[bk][:, :], func=AF.Sigmoid, bias=bg[:, 8 + c:9 + c]),
                reads=[P.bankh[bs[1]], h_bg], writes=[h_g2[r]])
            sc.op("dve", lambda e, r=r, bk=bs[2]: e.tensor_tensor(
                out=m1[r][:], in0=P.bank[bk][:, :], in1=g1[r][:], op=ALU.mult),
                reads=[P.bankh[bs[2]], h_g1[r]], writes=[h_m1[r]])
            sc.op("dve", lambda e, r=r, bk=bs[3]: e.tensor_tensor(
                out=m2[r][:], in0=P.bank[bk][:, :], in1=g2[r][:], op=ALU.mult),
                reads=[P.bankh[bs[3]], h_g2[r]], writes=[h_m2[r]])
            sc.op("pool", lambda e, r=r, c=c: e.tensor_tensor(
                out=tmp[:, c, :], in0=m1[r][:], in1=m2[r][:], op=ALU.add),
                reads=[h_m1[r], h_m2[r]], writes=[h_tmp] if c == 0 else (), wadd=() if c == 0 else [h_tmp])
        sc.op("dve", lambda e, tg=tg: e.tensor_copy(out=uT[:, :, tg * 512:(tg + 1) * 512], in_=tmp[:]),
              reads=[h_tmp], writes=[uTh[i] for i in tiles])


def wout_phase(P, g, mT, mTh, hres, hh):
    nc, sc = P.nc, P.sc
    sb = lambda n, s, d: g.enter_context(nc.sbuf_tensor("wo_" + n, s, d))
    wo = sb("wo", [128, KC, D], BF16)
    h_wo = H()
    load_w(P, "sp", wo[:], "w_out", 0, D, h_wo)
    xt = P.ntmp.xt
    h_xt = P.ntmp.h_xt
    banks = Banks(P, [0, 1, 2, 3])
    for i in range(NT):
        j = i % 2
        sc.op("sp", lambda e, i=i, j=j: e.dma_start(out=xt[j][:], in_=P.x[i * 128:(i + 1) * 128, :]),
              writes=[h_xt[j]], dma=True)
        for c0 in (0, 512):
            b = banks.next()
            for kc in range(KC):
                sc.op("pe", lambda e, b=b, i=i, kc=kc, c0=c0: e.matmul(
                    P.bank[b][:, :], lhsT=mT[:, kc, i * 128:(i + 1) * 128], rhs=wo[:, kc, c0:c0 + 512],
                    start=(kc == 0), stop=(kc == KC - 1)),
                    reads=[mTh[i], h_wo], writes=[P.bankh[b]] if kc == 0 else (), wadd=() if kc == 0 else [P.bankh[b]])
            sc.op("dve", lambda e, b=b, i=i, j=j, c0=c0: e.tensor_tensor(
                out=hres[:, i, c0:c0 + 512], in0=P.bank[b][:, :], in1=xt[j][:, c0:c0 + 512], op=ALU.add),
                reads=[P.bankh[b], h_xt[j]], writes=[hh[i]] if c0 == 0 else (), wadd=() if c0 == 0 else [hh[i]])


def cross_phase(P, g, uT, uTh, hres, hh, kcT, hkc, vc, hvc):
    nc, sc = P.nc, P.sc
    sb = lambda n, s, d: g.enter_context(nc.sbuf_tensor("cx_" + n, s, d))
    scale = 128.0 ** -0.5
    wq = sb("wq", [128, KC, 512], BF16)
    wco = sb("wco", [128, 4, D], BF16)
    h_wq, h_wco = H(), H()
    load_w(P, "sp", wq[:], "w_cq", 0, 512, h_wq)
    load_w(P, "sp", wco[:], "w_co", 0, D, h_wco)
    rmsnorm_T(P, None, "sbuf", "norm_cross", uT, uTh, src=hres, srch=hh, tag="n2")
    qcT = sb("qcT", [128, 4, S], BF16)
    hqc = [H() for _ in range(NT)]
    ocT = sb("ocT", [128, 4, S], BF16)
    hoc = [H() for _ in range(NT)]
    banks = Banks(P, [0, 1, 2, 3])
    kk = [0]

    def evq(j, tg, bap, bh):
        kk[0] += 1
        evac_copy(P, kk[0], qcT[:, j, tg * 512:(tg + 1) * 512], bap, [bh], [hqc[i] for i in range(tg * 4, tg * 4 + 4)])
    proj_fm(P, wq, h_wq, 512, uT, uTh, banks, evq)
    pT = [[sb("pT%d_%d" % (h, mb), [128, 512], BF16) for mb in range(2)] for h in range(4)]
    h_pT = [[H() for mb in range(2)] for h in range(4)]
    rden = sb("rden", [128, 4], F32)
    h_rden = H()
    otm = [sb("otm%d" % i, [128, 512], BF16) for i in range(2)]
    h_otm = [H(), H()]
    lbanks = Banks(P, [0, 1, 2, 3])
    for tg in range(4):
        for h in range(4):
            for mb in range(2):
                b = lbanks.next()
                sc.op("pe", lambda e, b=b, h=h, mb=mb, tg=tg: e.matmul(
                    P.bank[b][:, :], lhsT=kcT[:, h, mb * 128:(mb + 1) * 128], rhs=qcT[:, h, tg * 512:(tg + 1) * 512],
                    start=True, stop=True),
                    reads=[hkc[0]] + [hqc[i] for i in range(tg * 4, tg * 4 + 4)], writes=[P.bankh[b]])
                sc.op("act", lambda e, b=b, h=h, mb=mb: e.activation(
                    out=pT[h][mb][:], in_=P.bank[b][:, :], func=AF.Exp, scale=scale),
                    reads=[P.bankh[b]], writes=[h_pT[h][mb]])
        import os
        CXL = int(os.environ.get("CXL", "9"))
        if CXL < 1:
            continue
        for tt in range(4):
            i = tg * 4 + tt
            ro = i % 2
            for h in range(4):
                ob = 6 + h // 2
                oc = (h % 2) * 129
                for mb in range(2):
                    first = (h % 2 == 0 and mb == 0)
                    sc.op("pe", lambda e, ob=ob, oc=oc, h=h, mb=mb, tt=tt: e.matmul(
                        P.bank[ob][:, oc:oc + 129], lhsT=pT[h][mb][:, tt * 128:(tt + 1) * 128],
                        rhs=vc[:, mb, h * 129:(h + 1) * 129], start=(mb == 0), stop=(mb == 1)),
                        reads=[h_pT[h][mb], hvc[mb]], writes=[P.bankh[ob]] if first else (),
                        wadd=() if first else [P.bankh[ob]])
            if CXL < 2:
                continue
            for h in range(4):
                ob = 6 + h // 2
                oc = (h % 2) * 129
                sc.op("dve", lambda e, ob=ob, oc=oc, h=h: e.reciprocal(
                    out=rden[:, h:h + 1], in_=P.bank[ob][:, oc + 128:oc + 129]),
                    reads=[P.bankh[ob]], writes=[h_rden] if h == 0 else (), wadd=() if h == 0 else [h_rden])
            CXR = int(os.environ.get("CXR", "9"))
            for h in range(4):
                ob = 6 + h // 2
                oc = (h % 2) * 129
                if True:
                    sc.op("act", lambda e, ob=ob, oc=oc, h=h, ro=ro: e.activation(
                        out=otm[ro][:, h * 128:(h + 1) * 128], in_=P.bank[ob][:, oc:oc + 128], func=AF.Copy,
                        scale=rden[:, h:h + 1]),
                        reads=[P.bankh[ob], h_rden], writes=[h_otm[ro]] if h == 0 else (),
                        wadd=() if h == 0 else [h_otm[ro]])
                else:
                    sc.op("dve", lambda e, ob=ob, oc=oc, h=h, ro=ro: e.tensor_scalar(
                        out=otm[ro][:, h * 128:(h + 1) * 128], in0=P.bank[ob][:, oc:oc + 128],
                        scalar1=rden[:, h:h + 1], scalar2=None, op0=ALU.mult),
                        reads=[P.bankh[ob], h_rden], writes=[h_otm[ro]] if h == 0 else (),
                        wadd=() if h == 0 else [h_otm[ro]])
            if CXL < 3:
                continue
            b = 4 + (i % 2)
            pv = P.bank[b][:].bitcast(BF16)
            for c in range(4):
                sc.op("pe", lambda e, c=c, ro=ro, pv=pv: e.transpose(
                    out=pv[:, c * 128:(c + 1) * 128], in_=otm[ro][:, c * 128:(c + 1) * 128], identity=P.ident[:]),
                    reads=[h_otm[ro], P.h_const], writes=[P.bankh[b]] if c == 0 else (),
                    wadd=() if c == 0 else [P.bankh[b]])
            sc.op("act", lambda e, i=i, pv=pv: e.copy(
                out=ocT[:, :, i * 128:(i + 1) * 128], in_=pv[:, 0:512].rearrange("p (c t) -> p c t", c=4)),
                reads=[P.bankh[b]], writes=[hoc[i]])
    if P.stage == "H3":
        return
    for i in range(NT):
        for c0 in (0, 512):
            b = lbanks.next()
            for kc in range(4):
                sc.op("pe", lambda e, b=b, i=i, kc=kc, c0=c0: e.matmul(
                    P.bank[b][:, :], lhsT=ocT[:, kc, i * 128:(i + 1) * 128], rhs=wco[:, kc, c0:c0 + 512],
                    start=(kc == 0), stop=(kc == 3)),
                    reads=[hoc[i], h_wco], writes=[P.bankh[b]] if kc == 0 else (), wadd=() if kc == 0 else [P.bankh[b]])
            sc.op("dve", lambda e, b=b, i=i, c0=c0: e.tensor_tensor(
                out=hres[:, i, c0:c0 + 512], in0=P.bank[b][:, :], in1=hres[:, i, c0:c0 + 512], op=ALU.add),
                reads=[P.bankh[b]], writes=[hh[i]])


def mlp_phase(P, g, uT, uTh, hres, hh):
    nc, sc = P.nc, P.sc
    sb = lambda n, s, d: g.enter_context(nc.sbuf_tensor("ml_" + n, s, d))
    from contextlib import ExitStack
    rmsnorm_T(P, None, "sbuf", "norm_mlp", uT, uTh, src=hres, srch=hh, tag="n3")
    hid = sb("hid", [128, 32, 512], BF16)
    h_hid = [H() for _ in range(32)]
    wu = [sb("wu%d" % i, [128, KC, 512], BF16) for i in range(2)]
    h_wu = [H(), H()]
    wd = [sb("wd%d" % i, [128, 4, 512], BF16) for i in range(3)]
    h_wd = [H(), H(), H()]
    rl = [sb("rl%d" % i, [128, 512], F32) for i in range(2)]
    h_rl = [H(), H()]
    iu = 0
    idn = 0
    irl = 0
    ub = Banks(P, [4, 5, 6, 7])
    for tg in range(4):
        tiles = list(range(tg * 4, tg * 4 + 4))
        for cblk in range(8):
            r = iu % 2
            iu += 1
            load_w(P, "sp", wu[r][:], "w_up", cblk * 512, 512, h_wu[r])
            for j in range(4):
                b = ub.next()
                for kc in range(KC):
                    sc.op("pe", lambda e, b=b, r=r, kc=kc, j=j, tg=tg: e.matmul(
                        P.bank[b][:, :], lhsT=wu[r][:, kc, j * 128:(j + 1) * 128],
                        rhs=uT[:, kc, tg * 512:(tg + 1) * 512], start=(kc == 0), stop=(kc == KC - 1)),
                        reads=[h_wu[r]] + [uTh[i] for i in tiles], writes=[P.bankh[b]] if kc == 0 else (),
                        wadd=() if kc == 0 else [P.bankh[b]])
                q = irl % 2
                irl += 1
                sc.op("act", lambda e, b=b, q=q: e.activation(out=rl[q][:], in_=P.bank[b][:, :], func=AF.Relu),
                      reads=[P.bankh[b]], writes=[h_rl[q]])
                sc.op("pool", lambda e, q=q, cblk=cblk, j=j: e.tensor_tensor(
                    out=hid[:, cblk * 4 + j, :], in0=rl[q][:], in1=rl[q][:], op=ALU.mult),
                    reads=[h_rl[q]], writes=[h_hid[cblk * 4 + j]])
        for c0 in (0, 512):
            for rblk in range(8):
                r = idn % 3
                idn += 1
                load_w(P, "sp", wd[r][:], "w_down", c0, 512, h_wd[r], r0=rblk * 512, nk=4)
                for tt in range(4):
                    b = tt
                    for k4 in range(4):
                        kc = rblk * 4 + k4
                        first = (rblk == 0 and k4 == 0)
                        sc.op("pe", lambda e, b=b, r=r, k4=k4, kc=kc, tt=tt, first=first: e.matmul(
                            P.bank[b][:, :], lhsT=hid[:, kc, tt * 128:(tt + 1) * 128], rhs=wd[r][:, k4, :],
                            start=first, stop=(kc == 31)),
                            reads=[h_wd[r], h_hid[kc]], writes=[P.bankh[b]] if first else (),
                            wadd=() if first else [P.bankh[b]])
            for tt in range(4):
                i = tg * 4 + tt
                sc.op("dve", lambda e, tt=tt, i=i, c0=c0: e.tensor_tensor(
                    out=hres[:, i, c0:c0 + 512], in0=P.bank[tt][:, :], in1=hres[:, i, c0:c0 + 512], op=ALU.add),
                    reads=[P.bankh[tt]], writes=[hh[i]])


def final_phase(P, g, hres, hh):
    nc, sc = P.nc, P.sc
    T = P.ntmp
    gbc, junk, ot, st = T.gbc, T.junk, T.xt, T.st
    h_g, h_junk, h_ot, h_st = T.h_g, T.h_junk, T.h_xt, T.h_st
    sc.op("sp", lambda e: e.dma_start(out=gbc[:], in_=P.vec["norm_final"].to_broadcast([128, D])),
          writes=[h_g], dma=True)
    h_out = H()
    for i in range(NT):
        j = i % 2
        xin = hres[:, i, :]
        ss = st[:, 4 * i:4 * i + 1]
        ms = st[:, 4 * i + 1:4 * i + 2]
        sd = st[:, 4 * i + 2:4 * i + 3]
        rs = st[:, 4 * i + 3:4 * i + 4]
        sc.op("act", lambda e, xin=xin, ss=ss: e.activation(out=junk[:], in_=xin, func=AF.Square, accum_out=ss),
              reads=[hh[i]], writes=[h_junk, h_st[i]])
        sc.op("dve", lambda e, ss=ss, ms=ms: e.tensor_scalar(out=ms, in0=ss, scalar1=1.0 / D, scalar2=EPS,
                                                               op0=ALU.mult, op1=ALU.add),
              reads=[h_st[i]], writes=[h_st[i]])
        sc.op("act", lambda e, sd=sd, ms=ms: e.activation(out=sd, in_=ms, func=AF.Sqrt),
              reads=[h_st[i]], writes=[h_st[i]])
        sc.op("dve", lambda e, sd=sd, rs=rs: e.reciprocal(out=rs, in_=sd),
              reads=[h_st[i]], writes=[h_st[i]])
        sc.op("dve", lambda e, xin=xin, rs=rs, j=j: e.scalar_tensor_tensor(
            out=ot[j][:], in0=xin, scalar=rs, in1=gbc[:], op0=ALU.mult, op1=ALU.mult),
            reads=[hh[i], h_st[i], h_g], writes=[h_ot[j]])
        sc.op("sp", lambda e, i=i, j=j: e.dma_start(out=P.out[i * 128:(i + 1) * 128, :], in_=ot[j][:]),
              reads=[h_ot[j]], wadd=[h_out], dma=True)
    sc.op("sp", lambda e: e.nop(), reads=[h_out])


def phases(P, g):
    nc, sc = P.nc, P.sc
    sb = P.sb_global
    from contextlib import ExitStack
    P.first_cast_done = False

    def cast_first():
        if not P.first_cast_done:
            cast_weights(P, ["w_in"], cols=(0, 1536), hname="w_in_sb")
            cast_weights(P, ["w_ckv"])
            cast_weights(P, ["w_in"], cols=(1536, D_IN))
            build_dsa_weights(P)
            P.first_cast_done = True
    if P.stage in ("D", "E"):
        cast_first()
    P.rest_cast_done = False

    def cast_rest():
        if not P.rest_cast_done:
            cast_weights(P, [n for n, _, _ in W_SPECS if n not in ("w_in", "w_ckv")])
            P.rest_cast_done = True
    uT = sb("uT", [128, KC, S], BF16)
    uTh = [H("uT%d" % i) for i in range(NT)]
    P.ntmp = NormTmp(P, sb)
    kcT = sb("kcT", [128, 4, NMEM], BF16)
    hkc = [H()]
    vc = sb("vc", [128, 2, 4 * 129], BF16)
    hvc = [H(), H()]
    rmsnorm_T(P, None, "dram", "norm_mix", uT, uTh, src=P.x, tag="n1")
    if P.stage == "A":
        dbg_out(P, "uT", uT[:], [128, KC, S], BF16, uTh)
        return
    gmid = g.enter_context(ExitStack())
    sbm = lambda n, s, d: gmid.enter_context(nc.sbuf_tensor(n, s, d))
    osT = sbm("osT", [128, 4, S], BF16)
    osTh = [H() for _ in range(NT)]
    odT = sbm("odT", [128, 4, S], BF16)
    odTh = [H() for _ in range(NT)]
    if P.stage not in ("D", "E"):
      with ExitStack() as g2:
        sb2 = lambda n, s, d: g2.enter_context(nc.sbuf_tensor(n, s, d))
        qT = sb2("sbq", [128, 4, S], BF16)
        kT = sb2("sbk", [128, 4, S], BF16)
        v = sb2("sbv", [128, NT, 512], BF16)
        hq = [H() for _ in range(NT)]
        hk = [H() for _ in range(NT)]
        hv = [H() for _ in range(NT)]
        with ExitStack() as g3:
            wb = [g3.enter_context(nc.sbuf_tensor("wb%d" % i, [128, KC, 512], BF16)) for i in range(2)]
            wbh = [H(), H()]
            banks = Banks(P, [0, 1, 2, 3])
            kk = [0]
            sc.op("pool", lambda e: e.dma_start(
                out=wb[0][:], in_=P.w32["w_in"].rearrange("(kc p) n -> p kc n", p=128)[:, :, 0:512]),
                writes=[wbh[0]], dma=True)
            cast_first()
            for wi, (dst, hd) in enumerate([(qT, hq), (kT, hk)]):
                if wi > 0:
                    load_w(P, "sp", wb[wi % 2][:], "w_in", wi * 512, 512, wbh[wi % 2], hname="w_in_sb")

                def ev(j, tg, bap, bh, dst=dst, hd=hd):
                    kk[0] += 1
                    evac_copy(P, kk[0], dst[:, j, tg * 512:(tg + 1) * 512], bap, [bh],
                              [hd[i] for i in range(tg * 4, tg * 4 + 4)])
                proj_fm(P, wb[wi % 2], wbh[wi % 2], 512, uT, uTh, banks, ev)
            load_w(P, "sp", wb[0][:], "w_in", 1024, 512, wbh[0], hname="w_in_sb")

            def evv(i, c0, cw, bap, bh):
                kk[0] += 1
                evac_copy(P, kk[0], v[:, i, c0:c0 + cw], bap, [bh], [hv[i]])
            proj_tm(P, wb[0], wbh[0], 512, uT, uTh, banks, evv)
            memT = g3.enter_context(nc.sbuf_tensor("memT", [128, KC, NMEM], BF16))
            memTh = [H(), H()]
            wkv = g3.enter_context(nc.sbuf_tensor("wkv", [128, KC, D], BF16))
            h_wkv = H()
            load_w(P, "sp", wkv[:], "w_ckv", 0, D, h_wkv)
            rmsnorm_T(P, None, "dram", "norm_mem", memT, memTh, ntiles=2, src=P.mem, tag="nm")

            def evk(j, tg, bap, bh):
                kk[0] += 1
                evac_copy(P, kk[0], kcT[:, j, :], bap, [bh], (), wadd=[hkc[0]])
            proj_fm(P, wkv, h_wkv, 512, memT, memTh, banks, evk, ntok=NMEM)
            sc.op("pool", lambda e: e.memset(vc[:], 1.0), writes=hvc)

            def evmv(i, c0, cw, bap, bh):
                sc.op("act", lambda e, i=i, bap=bap: e.copy(
                    out=vc[:, i, :].rearrange("p (h c) -> p h c", c=129)[:, :, 0:128],
                    in_=bap[:, 0:512].rearrange("p (h c) -> p h c", c=128)),
                    reads=[bh], writes=[hvc[i]])
            proj_tm(P, wkv[:, :, 512:1024], h_wkv, 512, memT, memTh, banks, evmv, ntiles=2)
            sc.flush()
        if P.stage == "B":
            dbg_out(P, "qT", qT[:], [128, 4, S], BF16, hq)
            dbg_out(P, "kT", kT[:], [128, 4, S], BF16, hk)
            dbg_out(P, "v", v[:], [128, NT, 512], BF16, hv)
            sc.flush()
            return
        with ExitStack() as g3:
            sb_attention(P, g3, qT, kT, v, hq, hk, hv, osT, osTh)
            cast_rest()
            sc.flush()
    if P.stage == "C":
        dbg_out(P, "osT", osT[:], [128, 4, S], BF16, osTh)
        return
    cast_rest()
    with ExitStack() as g2:
        sb2 = lambda n, s, d: g2.enter_context(nc.sbuf_tensor(n, s, d))
        dqT = sb2("dqT", [128, 4, S], BF16)
        kT2 = sb2("kT2", [128, 2, S], BF16)
        iqT = sb2("iqT", [128, 4, S], BF16)
        ikT2 = sb2("ikT2", [128, 1, S], BF16)
        vp = sb2("vp", [128, NT, 256], BF16)
        iwS = sb2("iwS", [128, NT, 8], F32)
        hdq = [H() for _ in range(NT)]
        hk2 = [H() for _ in range(NT)]
        hiq = [H() for _ in range(NT)]
        hik2 = [H() for _ in range(NT)]
        hvp = [H() for _ in range(NT)]
        hiw = [H() for _ in range(NT)]
        with ExitStack() as g3:
            dsa_project(P, g3, uT, uTh, dqT, kT2, iqT, ikT2, vp, iwS, hdq, hk2, hiq, hik2, hvp, hiw)
            sc.flush()
        if P.stage == "D":
            dbg_out(P, "dqT", dqT[:], [128, 4, S], BF16, hdq)
            dbg_out(P, "kT2", kT2[:], [128, 2, S], BF16, hk2)
            dbg_out(P, "iqT", iqT[:], [128, 4, S], BF16, hiq)
            dbg_out(P, "ikT2", ikT2[:], [128, 1, S], BF16, hik2)
            dbg_out(P, "vp", vp[:], [128, NT, 256], BF16, hvp)
            dbg_out(P, "iwS", iwS[:], [128, NT, 8], F32, hiw)
            sc.flush()
            return
        with ExitStack() as g3:
            dsa_attention(P, g3, dqT, kT2, iqT, ikT2, vp, iwS, hdq, hk2, hiq, hik2, hvp, hiw, odT, odTh)
            sc.flush()
    if P.stage == "E":
        dbg_out(P, "odT", odT[:], [128, 4, S], BF16, odTh)
        sc.flush()
        gmid.close()
        return
    with ExitStack() as g2:
        merge_phase(P, g2, uT, uTh, osT, osTh, odT, odTh)
        sc.flush()
    gmid.close()
    if P.stage == "F":
        dbg_out(P, "mT", uT[:], [128, KC, S], BF16, uTh)
        return
    hres = sb("hres", [128, NT, D], F32)
    hh = [H() for _ in range(NT)]
    with ExitStack() as g2:
        wout_phase(P, g2, uT, uTh, hres, hh)
        if P.stage == "G":
            sc.flush()
            dbg_out(P, "h1", hres[:], [128, NT, D], F32, hh)
            sc.flush()
            return
        cross_phase(P, g2, uT, uTh, hres, hh, kcT, hkc, vc, hvc)
        sc.flush()
    if P.stage in ("H", "H1", "H2", "H3"):
        dbg_out(P, "h2", hres[:], [128, NT, D], F32, hh)
        return
    with ExitStack() as g2:
        mlp_phase(P, g2, uT, uTh, hres, hh)
        sc.flush()
    if P.stage == "I":
        dbg_out(P, "h3", hres[:], [128, NT, D], F32, hh)
        return
    with ExitStack() as g2:
        final_phase(P, g2, hres, hh)
        sc.flush()


def dbg_out(P, name, ap, shape, dt, hs):
    nc, sc = P.nc, P.sc
    o = nc.dram_tensor("dbg_" + name, list(shape), dt, kind="ExternalOutput").ap()
    P.dbg[name] = o
    hd = H()
    op = sc.op("sp", lambda e: e.dma_start(out=o, in_=ap), reads=list(hs), writes=[hd], dma=True)
    sc.op("sp", lambda e: e.nop(), reads=[hd])


def make_in_maps(inputs, ncores=8):
    cs = _consts()
    shared = {}
    for name, r, c in W_SPECS:
        shared[name] = np.ascontiguousarray(np.asarray(inputs[name], np.float32).reshape(r, c))
    for name in V_SPECS:
        shared[name] = np.ascontiguousarray(np.asarray(inputs[name], np.float32).reshape(1, D))
    shared["b_gate"] = np.ascontiguousarray(
        np.asarray(inputs["b_gate"], np.float32).reshape(16, 128).T)
    for k, v in cs.items():
        shared["c_" + k] = v
    x = np.asarray(inputs["x"], np.float32)
    mem = np.asarray(inputs["mem"], np.float32)
    maps = []
    for b in range(ncores):
        m = dict(shared)
        m["x"] = np.ascontiguousarray(x[b])
        m["mem"] = np.ascontiguousarray(mem[b])
        maps.append(m)
    return maps


_CACHE = {}


def kernel(**inputs):
    if "nc" not in _CACHE:
        _CACHE["nc"] = build("full")
    nc, P = _CACHE["nc"]
    maps = make_in_maps(inputs, 8)
    res = run_bass_kernel_spmd(nc, maps, core_ids=list(range(8)))
    out = np.stack([np.asarray(r["out"], np.float32) for r in res.results], axis=0)
    return out
```
